# Optimizing a Trainium2 kernel written in Bass

```python
import math
import jax, jax.numpy as jnp
from jax import lax
import numpy as np

D_MODEL = 2048
BATCH = 4
SEQ = 2048
DEPTH = 2

A_HEADS = 8
A_EXPAND = 128
A_HEAD_V = 128
A_WIDTH = A_HEADS * A_EXPAND
A_CHUNK = 16
B_GROUPS = ((128, 1), (512, 4), (2048, 16))
B_HEADS_PER_GROUP = 4
B_HEADS = B_HEADS_PER_GROUP * len(B_GROUPS)
B_HEAD_DIM = 64
B_WIDTH = B_HEADS * B_HEAD_DIM
B_OUT = B_HEADS_PER_GROUP * B_HEAD_DIM
C_WIDTH = 768
C_KERNEL = 31
REL_BUCKETS = 32
REL_MAX_DIST = 2048
D_FF = 4 * D_MODEL
N_BRANCH = 3
IN_SPLITS = (A_WIDTH, A_WIDTH, A_HEADS * A_HEAD_V, A_HEADS * A_HEAD_V,
             B_WIDTH, B_WIDTH, B_WIDTH, C_WIDTH, C_WIDTH)
IN_WIDTH = sum(IN_SPLITS)
EPS = 1e-6
MASK_VALUE = -1e30
TINY = 1e-30

kernel_name = "hybrid_hgrn2_dilated_conformer_block"


def rms_norm(x, w):
    xf = x.astype(jnp.float32)
    y = xf * lax.rsqrt(jnp.mean(xf * xf, axis=-1, keepdims=True) + EPS)
    return (y * w).astype(x.dtype)


def layer_norm(x, g, b):
    xf = x.astype(jnp.float32)
    mu = jnp.mean(xf, axis=-1, keepdims=True)
    var = jnp.mean(jnp.square(xf - mu), axis=-1, keepdims=True)
    return ((xf - mu) * lax.rsqrt(var + EPS) * g + b).astype(x.dtype)


def split_cols(a, sizes):
    idx = [int(v) for v in np.cumsum(sizes)[:-1]]
    return jnp.split(a, idx, axis=-1)


def hgrn2_mixer(q_pre, f_pre, inp, og, lb, norm_w):
    f32 = jnp.float32
    Bsz, S, _ = q_pre.shape
    H, K, V, C = A_HEADS, A_EXPAND, A_HEAD_V, A_CHUNK
    N = S // C

    def chunks(a, d):
        return a.astype(f32).reshape(Bsz, N, C, H, d).transpose(0, 3, 1, 2, 4)

    lb = jnp.maximum(lb.astype(f32), 0.0).reshape(H, K)[None, :, None, None, :]
    z = chunks(f_pre, K)
    log_f = jnp.logaddexp(jnp.log(lb + TINY), jnp.log1p(-lb) + jax.nn.log_sigmoid(z))
    k = (1.0 - lb) * jax.nn.sigmoid(-z)
    q = jax.nn.silu(chunks(q_pre, K))
    v = chunks(inp, V)
    b = jnp.cumsum(log_f, axis=3)
    causal = jnp.tril(jnp.ones((C, C), bool))
    decay = jnp.exp(jnp.where(causal[:, :, None],
                              b[..., :, None, :] - b[..., None, :, :], MASK_VALUE))
    attn = jnp.einsum('bhntk,bhntsk,bhnsk->bhnts', q, decay, k)
    o = jnp.einsum('bhnts,bhnsv->bhntv', attn, v)
    b_last = b[..., -1, :]
    delta = jnp.einsum('bhnck,bhncv->bhnkv', k * jnp.exp(b_last[..., None, :] - b), v)

    def step(state, xs):
        dec, dlt = xs
        return dec[..., None] * state + dlt, state

    _, s_prev = lax.scan(step, jnp.zeros((Bsz, H, K, V), f32),
                         (jnp.moveaxis(jnp.exp(b_last), 2, 0), jnp.moveaxis(delta, 2, 0)))
    s_prev = jnp.moveaxis(s_prev, 0, 2)
    o = o + jnp.einsum('bhnck,bhnkv->bhncv', q * jnp.exp(b), s_prev)
    o = o.transpose(0, 2, 3, 1, 4).reshape(Bsz, S, H, V)
    o = rms_norm(o, norm_w) * jax.nn.silu(og.astype(f32).reshape(Bsz, S, H, V))
    return o.reshape(Bsz, S, H * V).astype(q_pre.dtype)


def t5_bucket(dist):
    exact = REL_BUCKETS // 2
    d = jnp.maximum(dist, 1).astype(jnp.float32)
    large = exact + (jnp.log(d / exact) / math.log(REL_MAX_DIST / exact)
                     * (REL_BUCKETS - exact)).astype(jnp.int32)
    return jnp.where(dist < exact, dist, jnp.clip(large, exact, REL_BUCKETS - 1))


def dilated_window_group(q, k, v, bias_tab, window, dil):
    f32 = jnp.float32
    Bsz, S, H, Dh = q.shape
    J = window // dil
    Q = J
    L = S // dil
    nb = -(-L // Q)
    Lp = nb * Q

    def strided(a):
        a = a.reshape(Bsz, L, dil, H, Dh).transpose(0, 2, 1, 3, 4)
        return jnp.pad(a, ((0, 0), (0, 0), (0, Lp - L), (0, 0), (0, 0)))

    def band(a):
        a = jnp.pad(a, ((0, 0), (0, 0), (Q, 0), (0, 0), (0, 0))).reshape(Bsz, dil, nb + 1, Q, H, Dh)
        return jnp.concatenate([a[:, :, :-1], a[:, :, 1:]], axis=3)

    qb = strided(q).reshape(Bsz, dil, nb, Q, H, Dh)
    kb, vb = band(strided(k)), band(strided(v))
    steps = jnp.arange(Q)[:, None] + Q - jnp.arange(2 * Q)[None, :]
    key_pos = jnp.arange(nb)[:, None] * Q + jnp.arange(2 * Q)[None, :] - Q
    valid = ((steps >= 0) & (steps <= J))[None] & (key_pos >= 0)[:, None, :]
    bias = bias_tab[t5_bucket(jnp.clip(steps, 0, J) * dil)].transpose(2, 0, 1)
    logits = (jnp.einsum('brnqhd,brnkhd->brnhqk', qb, kb).astype(f32) * (Dh ** -0.5)
              + bias.astype(f32))
    logits = jnp.where(valid[None, None, :, None], logits, MASK_VALUE)
    m = jnp.max(logits, axis=-1, keepdims=True)
    p = jnp.exp(logits - m)
    z = jnp.sum(p, axis=-1, keepdims=True)
    o = jnp.einsum('brnhqk,brnkhd->brnqhd', p / z, vb.astype(f32))
    lse = (m + jnp.log(z))[..., 0]
    o = o.reshape(Bsz, dil, Lp, H, Dh)[:, :, :L].transpose(0, 2, 1, 3, 4).reshape(Bsz, S, H, Dh)
    lse = (lse.transpose(0, 1, 2, 4, 3).reshape(Bsz, dil, Lp, H)[:, :, :L]
           .transpose(0, 2, 1, 3).reshape(Bsz, S, H))
    return o, lse


def dilated_attention_mixer(q, k, v, rel_bias):
    Bsz, S, _ = q.shape
    shp = (Bsz, S, B_HEADS, B_HEAD_DIM)
    q, k, v = q.reshape(shp), k.reshape(shp), v.reshape(shp)
    outs, lses = [], []
    for g, (window, dil) in enumerate(B_GROUPS):
        hs = slice(g * B_HEADS_PER_GROUP, (g + 1) * B_HEADS_PER_GROUP)
        o, lse = dilated_window_group(q[:, :, hs], k[:, :, hs], v[:, :, hs],
                                      rel_bias[:, hs], window, dil)
        outs.append(o)
        lses.append(lse)
    alpha = jax.nn.softmax(jnp.stack(lses, axis=0), axis=0)
    out = jnp.sum(alpha[..., None] * jnp.stack(outs, axis=0), axis=0)
    return out.reshape(Bsz, S, B_OUT).astype(q.dtype)


def conv_module(a, gate, conv_w, conv_b, ln_g, ln_b):
    u = a * jax.nn.sigmoid(gate)
    y = lax.conv_general_dilated(u, conv_w[:, None, :], window_strides=(1,),
                                 padding=((C_KERNEL - 1, 0),),
                                 dimension_numbers=('NWC', 'WIO', 'NWC'),
                                 feature_group_count=C_WIDTH) + conv_b
    return jax.nn.silu(layer_norm(y, ln_g, ln_b))


def setup_inputs(seed: int = 0) -> dict:
    key = jax.random.key(seed)
    ks = jax.random.split(key, 26)
    f32 = jnp.float32

    def nrm(k, shape, scale):
        return jax.random.normal(k, shape, f32) * scale

    D = D_MODEL
    return {
        "x": nrm(ks[0], (BATCH, SEQ, D), 1.0),
        "c": nrm(ks[1], (BATCH, D), 1.0),
        "rel_bias": nrm(ks[2], (REL_BUCKETS, B_HEADS), 0.5),
        "hgrn_lb_logits": nrm(ks[3], (DEPTH, A_WIDTH), 1.0),
        "w_ada": nrm(ks[4], (DEPTH, D, 6 * D), 0.3 * D ** -0.5),
        "b_ada": nrm(ks[5], (DEPTH, 6 * D), 0.02),
        "mix_norm_pre": 1.0 + nrm(ks[6], (DEPTH, D), 0.05),
        "mix_norm_post": 1.0 + nrm(ks[7], (DEPTH, D), 0.05),
        "w_in": nrm(ks[8], (DEPTH, D, IN_WIDTH), D ** -0.5),
        "w_gate": nrm(ks[9], (DEPTH, D, N_BRANCH * D), D ** -0.5),
        "b_gate": nrm(ks[10], (DEPTH, N_BRANCH * D), 0.02),
        "hgrn_norm_w": 1.0 + nrm(ks[11], (DEPTH, A_HEAD_V), 0.05),
        "conv_w": nrm(ks[12], (DEPTH, C_KERNEL, C_WIDTH), C_KERNEL ** -0.5),
        "conv_b": nrm(ks[13], (DEPTH, C_WIDTH), 0.02),
        "conv_ln_g": 1.0 + nrm(ks[14], (DEPTH, C_WIDTH), 0.05),
        "conv_ln_b": nrm(ks[15], (DEPTH, C_WIDTH), 0.02),
        "w_a_out": nrm(ks[16], (DEPTH, A_HEADS * A_HEAD_V, D), (A_HEADS * A_HEAD_V) ** -0.5),
        "w_b_out": nrm(ks[17], (DEPTH, B_OUT, D), B_OUT ** -0.5),
        "w_c_out": nrm(ks[18], (DEPTH, C_WIDTH, D), C_WIDTH ** -0.5),
        "w_o": nrm(ks[19], (DEPTH, D, D), D ** -0.5),
        "mlp_norm_pre": 1.0 + nrm(ks[20], (DEPTH, D), 0.05),
        "mlp_norm_post": 1.0 + nrm(ks[21], (DEPTH, D), 0.05),
        "w_up": nrm(ks[22], (DEPTH, D, D_FF), D ** -0.5),
        "w_down": nrm(ks[23], (DEPTH, D_FF, D), D_FF ** -0.5),
    }


def reference(x, c, rel_bias, hgrn_lb_logits, w_ada, b_ada, mix_norm_pre, mix_norm_post,
              w_in, w_gate, b_gate, hgrn_norm_w, conv_w, conv_b, conv_ln_g, conv_ln_b,
              w_a_out, w_b_out, w_c_out, w_o, mlp_norm_pre, mlp_norm_post, w_up, w_down):
    Bsz, S, D = x.shape
    sm = jax.nn.softmax(hgrn_lb_logits.astype(jnp.float32), axis=0)
    lower_bounds = jnp.cumsum(sm, axis=0) - sm[0]
    c_act = jax.nn.silu(c)
    for l in range(DEPTH):
        mod = c_act @ w_ada[l] + b_ada[l]
        sh1, sc1, g1, sh2, sc2, g2 = jnp.split(mod[:, None, :], 6, axis=-1)

        h = rms_norm(x, mix_norm_pre[l]) * (1.0 + sc1) + sh1
        a_q, a_f, a_i, a_g, b_q, b_k, b_v, c_a, c_g = split_cols(h @ w_in[l], IN_SPLITS)
        ya = hgrn2_mixer(a_q, a_f, a_i, a_g, lower_bounds[l], hgrn_norm_w[l])
        yb = dilated_attention_mixer(b_q, b_k, b_v, rel_bias)
        yc = conv_module(c_a, c_g, conv_w[l], conv_b[l], conv_ln_g[l], conv_ln_b[l])
        gates = jax.nn.sigmoid(h @ w_gate[l] + b_gate[l]).reshape(Bsz, S, N_BRANCH, D)
        merged = (gates[:, :, 0] * (ya @ w_a_out[l])
                  + gates[:, :, 1] * (yb @ w_b_out[l])
                  + gates[:, :, 2] * (yc @ w_c_out[l]))
        x = x + g1 * rms_norm(merged @ w_o[l], mix_norm_post[l])

        h = rms_norm(x, mlp_norm_pre[l]) * (1.0 + sc2) + sh2
        y = jnp.square(jax.nn.relu(h @ w_up[l])) @ w_down[l]
        x = x + g2 * rms_norm(y, mlp_norm_post[l])
    return x
```

```python
import numpy as np
from contextlib import ExitStack
import concourse.bass as bass
import concourse.mybir as mybir
from concourse.bass_utils import run_bass_kernel_spmd

F32 = mybir.dt.float32
BF16 = mybir.dt.bfloat16
AF = mybir.ActivationFunctionType
ALU = mybir.AluOpType
AX = mybir.AxisListType

D = 2048
KC = 16
TC = 2048
TO = 1024
DFF = 8192
INW = 7936
EPS = 1e-6
SB_BASE = 16512
SB_TOP = 229344
NSLOT = 4
SLOT_BYTES = 16384
CONST_BYTES = 10240
RA_BYTES = 65536
RB_BYTES = 32768

C_IDENT = 0
C_SWAP = 128
C_FVALID = 256
C_CAUS = 256 + 383
C_RESET = C_CAUS + 64
C_ONES = C_RESET + 512
NCONST = C_ONES + 128
V_PRE = 0
V_MLPPRE = 16
V_BGATE = 32
V_NORMW = 80
V_CONVB = 81
V_LNG = 87
V_LNB = 93
V_CONVW = 99
NVEC = V_CONVW + 6 * 31
R_BADA = 0
R_POST = 12288
R_MLPPOST = 12288 + 2048
NROW = 12288 + 4096


class Buf:
    __slots__ = ("name", "w", "r", "excl")

    def __init__(self, name="", excl=False):
        self.name = name
        self.w = []
        self.r = {}
        self.excl = excl


class Region:
    def __init__(self, base, size):
        self.base = base
        self.size = size
        self.off = 0

    def reset(self):
        self.off = 0


class Prog:
    def __init__(self, nc, stack):
        self.nc = nc
        self.E = dict(pe=nc.tensor, act=nc.scalar, dve=nc.vector, pool=nc.gpsimd, sp=nc.sync)
        self.sem = {}
        self.cnt = {}
        for k in ("pe", "act", "dve", "pool"):
            self.sem[k] = stack.enter_context(nc.semaphore("s_" + k))
            self.cnt[k] = 0
        self.dsem = {}
        self.dcnt = {}
        self.dnext = {}
        for q, n in (("sp", 16), ("pool", 8)):
            self.dsem[q] = [stack.enter_context(nc.semaphore(f"d_{q}{i}")) for i in range(n)]
            self.dcnt[q] = [0] * n
            self.dnext[q] = 0
        self.waited = {e: {} for e in self.E}
        self.nalloc = 0
        self.n_ins = 0

    def _semof(self, key):
        if isinstance(key, tuple):
            return self.dsem[key[1]][key[2]]
        return self.sem[key]

    def _wait(self, e, key, val):
        if e == "pe" and key == "pe":
            return
        if self.waited[e].get(key, 0) >= val:
            return
        self.E[e].wait_ge(self._semof(key), val)
        self.waited[e][key] = val

    def _deps(self, e, reads, writes):
        best = {}
        for b in reads:
            for k, v in b.w:
                if best.get(k, 0) < v:
                    best[k] = v
            if b.excl:
                for k, v in b.r.items():
                    if k != e and best.get(k, 0) < v:
                        best[k] = v
        for b in writes:
            for k, v in b.w:
                if best.get(k, 0) < v:
                    best[k] = v
            for k, v in b.r.items():
                if best.get(k, 0) < v:
                    best[k] = v
        for k, v in best.items():
            self._wait(e, k, v)

    def _commit(self, tok, reads, writes):
        for b in writes:
            b.w = [tok]
            b.r = {}
        for b in reads:
            if b.r.get(tok[0], 0) < tok[1]:
                b.r[tok[0]] = tok[1]

    def op(self, e, fn, reads=(), writes=()):
        self._deps(e, reads, writes)
        ins = fn(self.E[e])
        self.cnt[e] += 1
        ins.then_inc(self.sem[e], 1)
        self._commit((e, self.cnt[e]), reads, writes)
        self.n_ins += 1
        return ins

    def mm(self, mms, reads=(), writes=()):
        self._deps("pe", reads, writes)
        ins = None
        for f in mms:
            ins = f(self.nc.tensor)
        self.cnt["pe"] += 1
        ins.then_inc(self.sem["pe"], 1)
        self._commit(("pe", self.cnt["pe"]), reads, writes)
        self.n_ins += len(mms)

    def dma(self, q, out, in_, reads=(), writes=(), add_write=False):
        i = self.dnext[q]
        self.dnext[q] = (i + 1) % len(self.dsem[q])
        key = ("d", q, i)
        self._wait(q, key, self.dcnt[q][i])
        self._deps(q, reads, writes)
        self.E[q].dma_start(out=out, in_=in_).then_inc(self.dsem[q][i], 16)
        self.dcnt[q][i] += 16
        tok = (key, self.dcnt[q][i])
        for b in writes:
            if add_write:
                b.w = b.w + [tok]
            else:
                b.w = [tok]
                b.r = {}
        for b in reads:
            b.r[key] = tok[1]
        self.n_ins += 1
        return tok

    def barrier(self, engines=("pe", "act", "dve", "sp")):
        toks = [(k, self.cnt[k]) for k in ("pe", "act", "dve") if self.cnt[k] > 0]
        for i, v in enumerate(self.dcnt["sp"]):
            if v > 0:
                toks.append((("d", "sp", i), v))
        for e in engines:
            for k, v in toks:
                self._wait(e, k, v)

    def final_wait(self):
        toks = [(k, self.cnt[k]) for k in ("pe", "act", "dve", "pool") if self.cnt[k] > 0]
        for q in ("sp", "pool"):
            for i, v in enumerate(self.dcnt[q]):
                if v > 0:
                    toks.append((("d", q, i), v))
        for e in ("sp", "pool", "act", "dve", "pe"):
            for k, v in toks:
                self._wait(e, k, v)

    def sb(self, region, shape, dtype, name="t"):
        esz = 4 if dtype == F32 else 2
        n = 1
        for s in shape[1:]:
            n *= s
        nbytes = (n * esz + 31) // 32 * 32
        regs = region if isinstance(region, (list, tuple)) else [region]
        for r in regs:
            if r.off + nbytes <= r.size:
                off = r.base + r.off
                r.off += nbytes
                self.nalloc += 1
                return self.nc.alloc_sbuf_tensor_at(f"{name}_{self.nalloc}", list(shape), dtype, offset=off)
        raise RuntimeError(f"SBUF region overflow allocating {name} {shape}")


class WStream:
    def __init__(self, P, slots):
        self.P = P
        self.slots = slots
        self.plan = []
        self.issued = 0
        self.taken = 0

    def add(self, pieces):
        self.plan.append(pieces)

    def _issue(self, j):
        t, b = self.slots[j % len(self.slots)]
        first = True
        for dst_fn, src in self.plan[j]:
            self.P.dma("pool", dst_fn(t), src, writes=(b,), add_write=not first)
            first = False

    def get(self, hold_from=None):
        j = self.taken
        self.taken += 1
        base = j if hold_from is None else hold_from
        lim = min(len(self.plan), base + len(self.slots))
        while self.issued < lim:
            self._issue(self.issued)
            self.issued += 1
        return self.slots[j % len(self.slots)]


def slot_view(t, kc, n, kc0=0, c0=0, width=512):
    return bass.AP(t, kc0 * width + c0, [[8192, 128], [width, kc], [1, n]])


class Ctx:
    pass


class LV:
    def __init__(self, t, idx=None, rows=None):
        self.t, self.idx, self.rows = t, idx, rows
        shp = list(t.shape)
        self.shape = shp[1:] if idx is not None else shp
        self.dtype = t.dtype

    def ap(self):
        a = self.t.ap()
        if self.idx is not None:
            a = a[self.idx]
        if self.rows is not None:
            a = a[self.rows[0]:self.rows[1]]
        return a


def build_fused(dbg=None, passes=("A", "B", "L1"), stop_after=None):
    dbg = dbg or set()
    nc = bass.Bass("TRN2", target_bir_lowering=False)
    stack = ExitStack()
    with stack:
        P = Prog(nc, stack)
        g = Ctx()
        g.nc, g.P, g.dbg = nc, P, dbg
        di = lambda name, shape, dt=F32: nc.dram_tensor(name, list(shape), dt, kind="ExternalInput")
        g.x3 = di("x3", [3 * TO, D])
        g.c_b = di("c_b", [128, KC])
        g.tmask2 = di("tmask", [2, 128, 4])
        g.consts = di("consts", [128, NCONST])
        g.vecs2 = di("vecs", [2, 128, NVEC])
        g.rows2 = di("rows", [2, 1, NROW])
        g.lbl = di("lbl", [128, 2, 8])
        g.rel_bias = di("rel_bias", [32, 12])
        g.onehot = di("onehot", [3, 32, 383])
        g.Wf = dict(
            w_ada=di("w_ada", [2, D, 6 * D]), w_in=di("w_in", [2, D, INW]), w_gate=di("w_gate", [2, D, 3 * D]),
            w_a_out=di("w_a_out", [2, 1024, D]), w_b_out=di("w_b_out", [2, 256, D]), w_c_out=di("w_c_out", [2, 768, D]),
            w_o=di("w_o", [2, D, D]), w_up=di("w_up", [2, D, DFF]), w_down=di("w_down", [2, DFF, D]))
        g.xout_t = nc.dram_tensor("xout", [TO, D], F32, kind="ExternalOutput")
        ds = lambda name, shape, dt: nc.dram_tensor(name, list(shape), dt)
        g.AQ = ds("AQ", [1024, TO], BF16)
        g.AF = ds("AFz", [1024, TC], F32)
        g.AG = ds("AG", [1024, TO], BF16)
        g.AI = ds("AI", [TC, 1024], BF16)
        g.BQ = ds("BQ", [768, TO], BF16)
        g.BK = ds("BK", [768, TC], BF16)
        g.BV = ds("BV", [3, 16, 128, 512], BF16)
        g.U = ds("U", [768, TO + 32], BF16)
        g.GATES = ds("GATES", [3 * D, TO], BF16)
        g.X1 = ds("X1", [TO, D], F32)
        g.FT = ds("FT", [12, 128 * 383], F32)
        g.GROWS = ds("GROWS", [2, D], F32)
        g.XL1 = ds("XL1", [TC, D], F32)
        g.taps = {}

        def tap(name, shape, dt=F32):
            return None
        g.tap = tap
        off = SB_BASE
        g.slots = []
        for i in range(NSLOT):
            t = nc.alloc_sbuf_tensor_at(f"wslot{i}", [128, 8192], BF16, offset=off)
            g.slots.append((t, Buf(f"slot{i}")))
            off += SLOT_BYTES
        g.RK = Region(off, CONST_BYTES); off += CONST_BYTES
        g.RA1 = Region(off, RA_BYTES // 2)
        g.RA2 = Region(off + RA_BYTES // 2, RA_BYTES // 2)
        g.RA = Region(off, RA_BYTES); off += RA_BYTES
        g.RB = Region(off, RB_BYTES); off += RB_BYTES
        g.RC = Region(off, SB_TOP - off)
        g.RBC = Region(g.RB.base, RB_BYTES + g.RC.size)
        g.banks = [(nc.alloc_psum_tensor(f"ps{i}", [128, 512], F32), Buf(f"bank{i}", excl=True)) for i in range(8)]
        g.bank_i = 0

        def bank():
            b = g.banks[g.bank_i]
            g.bank_i = (g.bank_i + 1) % 8
            return b
        g.bank = bank
        g.W = WStream(P, g.slots)
        g.cst = P.sb(g.RK, [128, NCONST], F32, "cst")
        g.identb = P.sb(g.RK, [128, 128], BF16, "identb")
        g.vec = P.sb(g.RK, [128, NVEC], F32, "vec")
        g.tm = P.sb(g.RK, [128, 4], F32, "tm")
        g.modc = P.sb(g.RK, [128, 4, 16], F32, "modc")
        g.AB = P.sb(g.RK, [128, 4, 16], F32, "AB")
        g.lbv = P.sb(g.RK, [128, 3, 8], F32, "lbv")
        g.kbuf = Buf("consts")
        P.dma("sp", g.cst[:], g.consts.ap(), writes=(g.kbuf,))
        P.op("dve", lambda e: e.tensor_copy(out=g.identb[:], in_=g.cst[:, C_IDENT:C_IDENT + 128]),
             reads=(g.kbuf,), writes=(g.kbuf,))
        g.ident = g.cst[:, C_IDENT:C_IDENT + 128]
        g.ones = g.cst[:, C_ONES:C_ONES + 128]

        for ps in passes:
            l = 1 if ps == "L1" else 0
            for k, t in g.Wf.items():
                setattr(g, k, LV(t, l))
            if ps in ("A", "L1"):
                plan_weights(g, 0, 8)
            plan_weights_p2(g)
            if ps in ("A", "L1"):
                plan_weights(g, 8, 24)
            plan_weights_p45(g)

        build_bias_tables(g)
        P.barrier()
        for ps in passes:
            l = 1 if ps == "L1" else 0
            g.l = l
            g.vecs = LV(g.vecs2, l)
            g.rows = LV(g.rows2, l)
            if ps == "A":
                g.xctx = LV(g.x3, None, (0, TC))
                g.xout = LV(g.XL1, None, (0, TO))
                tmi = 0
            elif ps == "B":
                g.xctx = LV(g.x3, None, (TO, TO + TC))
                g.xout = LV(g.XL1, None, (TO, TC))
                tmi = 1
            else:
                g.xctx = LV(g.XL1)
                g.xout = LV(g.xout_t)
                tmi = 1
            if "L1" not in passes and ps == passes[-1]:
                g.xout = LV(g.xout_t)
            P.dma("sp", g.vec[:], g.vecs.ap(), writes=(g.kbuf,))
            P.dma("sp", g.tm[:], g.tmask2.ap()[tmi], writes=(g.kbuf,), add_write=True)
            if ps in ("A", "L1"):
                phase0_setup(g)
                for _ in range(8):
                    phase0_tile(g)
                P.barrier()
            phase1(g)
            P.barrier()
            phase2(g)
            P.barrier()
            g.RA1.reset()
            g.YT = P.sb(g.RA1, [128, 16, TO], BF16, "YT")
            phase3_hgrn(g)
            while getattr(g, "p0_next", 24) < 24:
                phase0_tile(g)
            P.barrier()
            phase3_attn(g)
            P.barrier()
            phase3_conv(g)
            P.barrier()
            phase4a(g)
            P.barrier()
            phase4b(g)
            P.barrier()
            phase5(g)
            P.barrier()
        P.final_wait()
    return nc, g


def plan_weights(g, nt0, nt1):
    W = g.W
    wv = lambda w, c0, n: w.ap().rearrange("(kc p) n -> p kc n", p=128)[:, :, c0:c0 + n]
    for nt in range(nt0, nt1):
        W.add([(lambda t: slot_view(t, 16, 512), wv(g.w_ada, nt * 512, 512))])


def phase0_setup(g):
    P = g.P
    if not hasattr(g, "csb"):
        g.csb = P.sb(g.RK, [128, KC], F32, "csb")
        g.cact = P.sb(g.RK, [128, KC], BF16, "cact")
        g.stg0 = P.sb(g.RK, [1, 512], F32, "stg0")
        g.lb_in = P.sb(g.RK, [128, 2, 8], F32, "lb_in")
        g.b_c, g.b_stg0, g.b_lb, g.b_modc = Buf("c"), Buf("stg0"), Buf("lb"), Buf("modc")
    P.dma("sp", g.csb[:], g.c_b.ap(), writes=(g.b_c,))
    P.op("act", lambda e: e.activation(out=g.cact[:], in_=g.csb[:], func=AF.Silu), reads=(g.b_c,), writes=(g.b_c,))
    b_lb = g.b_lb
    P.dma("sp", g.lb_in[:], g.lbl.ap(), writes=(b_lb,))
    P.op("dve", lambda e: e.tensor_tensor(out=g.lbv[:, 2, :], in0=g.lb_in[:, 1, :], in1=g.lb_in[:, 0, :], op=ALU.subtract),
         reads=(b_lb,), writes=(b_lb,))
    P.op("act", lambda e: e.activation(out=g.lbv[:, 2, :], in_=g.lbv[:, 2, :], func=AF.Sigmoid), reads=(b_lb,), writes=(b_lb,))
    P.op("dve", lambda e: e.tensor_scalar(out=g.lbv[:, 0, :], in0=g.lbv[:, 2, :], scalar1=float(g.l), scalar2=None, op0=ALU.mult),
         reads=(b_lb,), writes=(b_lb,))
    P.op("dve", lambda e: e.tensor_scalar(out=g.lbv[:, 1, :], in0=g.lbv[:, 0, :], scalar1=-1.0, scalar2=1.0, op0=ALU.mult, op1=ALU.add),
         reads=(b_lb,), writes=(b_lb,))
    g.p0_next = 0


def phase0_tile(g):
    P = g.P
    nt = g.p0_next
    g.p0_next += 1
    stg, b_stg = g.stg0, g.b_stg0
    one11 = g.cst[0:1, C_ONES:C_ONES + 1]
    st, sb_ = g.W.get()
    pt, pb = g.bank()
    P.dma("sp", stg[:], g.rows.ap()[0:1, R_BADA + nt * 512:R_BADA + (nt + 1) * 512], writes=(b_stg,))
    P.mm([lambda e, kc=kc: e.matmul(pt[0:1, :], lhsT=g.cact[:, kc:kc + 1], rhs=slot_view(st, 16, 512)[:, kc, :],
                                    start=(kc == 0), stop=(kc == KC - 1)) for kc in range(KC)], reads=(g.b_c, sb_), writes=(pb,))
    P.op("dve", lambda e: e.tensor_tensor(out=stg[:], in0=pt[0:1, :], in1=stg[:], op=ALU.add), reads=(pb, b_stg), writes=(b_stg,))
    seg, j4 = nt // 4, nt % 4
    if seg in (2, 5):
        P.dma("sp", g.GROWS.ap()[(0 if seg == 2 else 1):(1 if seg == 2 else 2), j4 * 512:(j4 + 1) * 512], stg[:], reads=(b_stg,))
    else:
        si = {0: 0, 1: 1, 3: 2, 4: 3}[seg]
        pt2, pb2 = g.bank()
        P.mm([lambda e, q=q: e.matmul(pt2[:, q:q + 1], lhsT=stg[0:1, q * 128:(q + 1) * 128], rhs=one11, start=True, stop=True)
              for q in range(4)], reads=(b_stg, g.kbuf), writes=(pb2,))
        P.op("dve", lambda e: e.tensor_copy(out=g.modc[:, si, j4 * 4:(j4 + 1) * 4], in_=pt2[:, 0:4]), reads=(pb2,), writes=(g.b_modc,))
    if nt == 7:
        P.op("dve", lambda e: e.scalar_tensor_tensor(out=g.AB[:, 0, :], in0=g.modc[:, 1, :], scalar=1.0,
                                                     in1=g.vec[:, V_PRE:V_PRE + 16], op0=ALU.add, op1=ALU.mult),
             reads=(g.b_modc, g.kbuf), writes=(g.b_modc,))
        P.op("dve", lambda e: e.tensor_copy(out=g.AB[:, 1, :], in_=g.modc[:, 0, :]), reads=(g.b_modc,), writes=(g.b_modc,))
    if nt == 19:
        P.op("dve", lambda e: e.scalar_tensor_tensor(out=g.AB[:, 2, :], in0=g.modc[:, 3, :], scalar=1.0,
                                                     in1=g.vec[:, V_MLPPRE:V_MLPPRE + 16], op0=ALU.add, op1=ALU.mult),
             reads=(g.b_modc, g.kbuf), writes=(g.b_modc,))
        P.op("dve", lambda e: e.tensor_copy(out=g.AB[:, 3, :], in_=g.modc[:, 2, :]), reads=(g.b_modc,), writes=(g.b_modc,))


def phase0_hook(g):
    if getattr(g, "p0_next", 24) < 24:
        phase0_tile(g)


def norm_transpose(g, xs, b_xs, ntile, AB_a, AB_b, dst_fn, junk, st, b_st):
    P = g.P
    for j in range(ntile):
        P.op("act", lambda e, j=j: e.activation(out=junk[:], in_=xs[:, j, :], func=AF.Square, accum_out=st[:, j:j + 1]),
             reads=(b_xs,), writes=(b_st,))
    P.op("dve", lambda e: e.tensor_scalar(out=st[:, ntile:2 * ntile], in0=st[:, 0:ntile], scalar1=1.0 / D, scalar2=EPS,
                                          op0=ALU.mult, op1=ALU.add), reads=(b_st,), writes=(b_st,))
    P.op("act", lambda e: e.activation(out=st[:, 2 * ntile:3 * ntile], in_=st[:, ntile:2 * ntile], func=AF.Sqrt),
         reads=(b_st,), writes=(b_st,))
    P.op("dve", lambda e: e.reciprocal(out=st[:, 3 * ntile:4 * ntile], in_=st[:, 2 * ntile:3 * ntile]),
         reads=(b_st,), writes=(b_st,))
    for j in range(ntile):
        eng = "act" if j % 2 == 0 else "dve"
        if eng == "act":
            P.op("act", lambda e, j=j: e.activation(out=xs[:, j, :], in_=xs[:, j, :], func=AF.Copy,
                                                    scale=st[:, 3 * ntile + j:3 * ntile + j + 1]),
                 reads=(b_st, b_xs), writes=(b_xs,))
        else:
            P.op("dve", lambda e, j=j: e.tensor_scalar(out=xs[:, j, :], in0=xs[:, j, :],
                                                       scalar1=st[:, 3 * ntile + j:3 * ntile + j + 1], scalar2=None, op0=ALU.mult),
                 reads=(b_st, b_xs), writes=(b_xs,))
    for fc in range(KC):
        pt, pb = g.bank()
        mms = [lambda e, j=j, fc=fc: e.transpose(pt[:, j * 128:(j + 1) * 128], xs[:, j, fc * 128:(fc + 1) * 128], g.ident)
               for j in range(ntile)]
        P.mm(mms, reads=(b_xs, g.kbuf), writes=(pb,))
        n = ntile * 128
        if fc % 2 == 0:
            P.op("act", lambda e, fc=fc: e.activation(out=dst_fn(fc), in_=pt[:, 0:n], func=AF.Identity,
                                                      scale=AB_a[:, fc:fc + 1], bias=AB_b[:, fc:fc + 1]),
                 reads=(pb, g.b_modc))
        else:
            P.op("dve", lambda e, fc=fc: e.tensor_scalar(out=dst_fn(fc), in0=pt[:, 0:n], scalar1=AB_a[:, fc:fc + 1],
                                                         scalar2=AB_b[:, fc:fc + 1], op0=ALU.mult, op1=ALU.add),
                 reads=(pb, g.b_modc))


def phase1(g):
    P = g.P
    g.RA.reset()
    g.hT = P.sb(g.RA, [128, KC, TC], BF16, "hT")
    R = g.RBC
    R.reset()
    xs2 = [P.sb(R, [128, 2, D], F32, f"xs{i}") for i in range(2)]
    bx = [Buf("xs0"), Buf("xs1")]
    xv = g.xctx.ap().rearrange("(t p) d -> p t d", p=128)
    junk = P.sb(R, [128, D], BF16, "junk")
    st = P.sb(R, [128, 8], F32, "st")
    b_st = Buf("st")
    for gi in range(8):
        xs, b_xs = xs2[gi % 2], bx[gi % 2]
        P.dma("sp", xs[:], xv[:, gi * 2:gi * 2 + 2, :], writes=(b_xs,))
        norm_transpose(g, xs, b_xs, 2, g.AB[:, 0, :], g.AB[:, 1, :],
                       lambda fc, gi=gi: g.hT[:, fc, gi * 256:(gi + 1) * 256], junk, st, b_st)
    t = g.tap("hT", [D, TC], BF16)
    if t is not None:
        P.barrier()
        P.dma("sp", t.ap().rearrange("(kc p) t -> p kc t", p=128), g.hT[:])


def plan_weights_p2(g):
    W = g.W
    wv = lambda w, c0, n: w.ap().rearrange("(kc p) n -> p kc n", p=128)[:, :, c0:c0 + n]
    full = lambda c0, n: [(lambda t, n=n: slot_view(t, 16, n), wv(g.w_in, c0, n))]
    for c0 in (0, 512):
        W.add(full(c0, 512))
    for c0 in (1024, 1536):
        W.add(full(c0, 512))
    for c0 in (3072, 3584):
        W.add(full(c0, 512))
    for c0 in (2048, 2560):
        W.add(full(c0, 512))
    W.add(full(4096, 512)); W.add(full(4608, 256))
    W.add(full(4864, 512)); W.add(full(5376, 256))
    for gi in range(3):
        W.add(full(5632 + gi * 256, 256))
    for i in range(3):
        W.add([(lambda t: slot_view(t, 16, 256, c0=0), wv(g.w_in, 6400 + i * 256, 256)),
               (lambda t: slot_view(t, 16, 256, c0=256), wv(g.w_in, 7168 + i * 256, 256))])
    for i in range(12):
        W.add([(lambda t: slot_view(t, 16, 512), wv(g.w_gate, i * 512, 512))])


def phase2(g):
    P = g.P
    R = g.RBC
    R.reset()
    hT = g.hT
    NSTG = 8
    stg = [(P.sb(R, [128, 512], F32, f"stg{i}"), Buf(f"stg{i}")) for i in range(NSTG)]
    tmp = [(P.sb(R, [128, 512], F32, f"tmp{i}"), Buf(f"tmp{i}")) for i in range(4)]
    onesv = P.sb(R, [128, 256], F32, "onesv")
    b_ones = Buf("onesv")
    P.op("dve", lambda e: e.memset(onesv[:], 1.0), writes=(b_ones,))
    st = {"i": 0, "t": 0, "e": 0}

    def nstg():
        s = stg[st["i"] % NSTG]
        st["i"] += 1
        return s

    def ntmp():
        s = tmp[st["t"] % 4]
        st["t"] += 1
        return s

    def eng2():
        st["e"] += 1
        return "act" if st["e"] % 2 == 0 else "dve"

    OWN = [(1024, 512), (1536, 512)]
    CTX = [(0, 512), (512, 512), (1024, 512), (1536, 512)]

    def fm(slot, c0, ncols, toks, evac):
        stt, sbuf = slot
        sv = slot_view(stt, 16, 512)
        for j in range(ncols // 128):
            for (t0, nt) in toks:
                pt, pb = g.bank()
                mms = [lambda e, kc=kc, j=j, t0=t0, nt=nt: e.matmul(
                    pt[:, 0:nt], lhsT=sv[:, kc, c0 + j * 128:c0 + (j + 1) * 128], rhs=hT[:, kc, t0:t0 + nt],
                    start=(kc == 0), stop=(kc == KC - 1)) for kc in range(KC)]
                P.mm(mms, reads=(sbuf,), writes=(pb,))
                evac(pt, pb, j, t0, nt)

    def spill(dram_ap, src_ap, b_src):
        P.dma("sp", dram_ap, src_ap, reads=(b_src,))

    def ev_aq(f0):
        def ev(pt, pb, j, t0, nt):
            s, sb_ = nstg()
            sv = s.bitcast(BF16)
            P.op("act", lambda e: e.activation(out=sv[:, 0:nt], in_=pt[:, 0:nt], func=AF.Silu), reads=(pb,), writes=(sb_,))
            spill(g.AQ.ap()[f0 + j * 128:f0 + (j + 1) * 128, t0 - 1024:t0 - 1024 + nt], sv[:, 0:nt], sb_)
        return ev
    for i in range(2):
        fm(g.W.get(), 0, 512, OWN, ev_aq(i * 512))

    def ev_af(f0):
        def ev(pt, pb, j, t0, nt):
            s, sb_ = nstg()
            en = eng2()
            if en == "act":
                P.op("act", lambda e: e.copy(out=s[:, 0:nt], in_=pt[:, 0:nt]), reads=(pb,), writes=(sb_,))
            else:
                P.op("dve", lambda e: e.tensor_copy(out=s[:, 0:nt], in_=pt[:, 0:nt]), reads=(pb,), writes=(sb_,))
            spill(g.AF.ap()[f0 + j * 128:f0 + (j + 1) * 128, t0:t0 + nt], s[:, 0:nt], sb_)
        return ev
    for i in range(2):
        fm(g.W.get(), 0, 512, CTX, ev_af(i * 512))

    def ev_ag(f0):
        def ev(pt, pb, j, t0, nt):
            t_, tb = ntmp()
            s, sb_ = nstg()
            sv = s.bitcast(BF16)
            P.op("act", lambda e: e.activation(out=t_[:, 0:nt], in_=pt[:, 0:nt], func=AF.Silu), reads=(pb,), writes=(tb,))
            P.op("dve", lambda e: e.tensor_scalar(out=sv[:, 0:nt], in0=t_[:, 0:nt], scalar1=g.vec[:, V_NORMW:V_NORMW + 1],
                                                  scalar2=None, op0=ALU.mult), reads=(tb, g.kbuf), writes=(sb_,))
            spill(g.AG.ap()[f0 + j * 128:f0 + (j + 1) * 128, t0 - 1024:t0 - 1024 + nt], sv[:, 0:nt], sb_)
        return ev
    for i in range(2):
        fm(g.W.get(), 0, 512, OWN, ev_ag(i * 512))

    for i in range(2):
        stt, sbuf = g.W.get()
        sv = slot_view(stt, 16, 512)
        for ti in range(16):
            pt, pb = g.bank()
            mms = [lambda e, kc=kc, ti=ti: e.matmul(pt[:, :], lhsT=hT[:, kc, ti * 128:(ti + 1) * 128], rhs=sv[:, kc, :],
                                                   start=(kc == 0), stop=(kc == KC - 1)) for kc in range(KC)]
            P.mm(mms, reads=(sbuf,), writes=(pb,))
            s, sb_ = nstg()
            sv2 = s.bitcast(BF16)
            mc = 0 if ti < 8 else 1
            en = eng2()
            if en == "act":
                P.op("act", lambda e: e.activation(out=sv2[:, 0:512], in_=pt[:, :], func=AF.Copy, scale=g.tm[:, mc:mc + 1]),
                     reads=(pb, g.kbuf), writes=(sb_,))
            else:
                P.op("dve", lambda e: e.tensor_scalar(out=sv2[:, 0:512], in0=pt[:, :], scalar1=g.tm[:, mc:mc + 1], scalar2=None,
                                                      op0=ALU.mult), reads=(pb, g.kbuf), writes=(sb_,))
            spill(g.AI.ap()[ti * 128:(ti + 1) * 128, i * 512:(i + 1) * 512], sv2[:, 0:512], sb_)

    def ev_b(dst, f0, scale, own):
        def ev(pt, pb, j, t0, nt):
            s, sb_ = nstg()
            sv = s.bitcast(BF16)
            en = eng2()
            if en == "act":
                P.op("act", lambda e: e.activation(out=sv[:, 0:nt], in_=pt[:, 0:nt], func=AF.Copy, scale=scale), reads=(pb,), writes=(sb_,))
            else:
                P.op("dve", lambda e: e.tensor_scalar(out=sv[:, 0:nt], in0=pt[:, 0:nt], scalar1=scale, scalar2=None, op0=ALU.mult),
                     reads=(pb,), writes=(sb_,))
            tt = t0 - 1024 if own else t0
            spill(dst.ap()[f0 + j * 128:f0 + (j + 1) * 128, tt:tt + nt], sv[:, 0:nt], sb_)
        return ev
    fm(g.W.get(), 0, 512, OWN, ev_b(g.BQ, 0, 0.125, True))
    fm(g.W.get(), 0, 256, OWN, ev_b(g.BQ, 512, 0.125, True))
    fm(g.W.get(), 0, 512, CTX, ev_b(g.BK, 0, 1.0, False))
    fm(g.W.get(), 0, 256, CTX, ev_b(g.BK, 512, 1.0, False))

    for gi, dil in enumerate((1, 4, 16)):
        stt, sbuf = g.W.get()
        sv = slot_view(stt, 16, 256)
        for bi in range(16):
            if gi == 0:
                start, mc = bi * 128, (0 if bi < 8 else 1)
            elif gi == 1:
                r, n = bi // 4, bi % 4
                start, mc = n * 512 + r, (0 if n < 2 else 1)
            else:
                start, mc = bi, 2
            pt, pb = g.bank()
            mms = [lambda e, kc=kc, start=start, dil=dil: e.matmul(
                pt[:, 0:256], lhsT=hT[:, kc, start:start + 127 * dil + 1:dil], rhs=sv[:, kc, :],
                start=(kc == 0), stop=(kc == KC - 1)) for kc in range(KC)]
            P.mm(mms, reads=(sbuf,), writes=(pb,))
            s, sb_ = nstg()
            sv2 = s.bitcast(BF16)
            vdst = bass.AP(sv2, 0, [[1024, 128], [256, 2], [192, 2], [1, 64]])
            mdst = bass.AP(sv2, 64, [[1024, 128], [256, 2], [64, 2], [1, 64]])
            vsrc = bass.AP(pt, 0, [[512, 128], [128, 2], [64, 2], [1, 64]])
            osrc = bass.AP(onesv, 0, [[256, 128], [128, 2], [64, 2], [1, 64]])
            P.op("dve", lambda e: e.tensor_scalar(out=vdst, in0=vsrc, scalar1=g.tm[:, mc:mc + 1], scalar2=None, op0=ALU.mult),
                 reads=(pb, g.kbuf), writes=(sb_,))
            P.op("dve", lambda e: e.tensor_scalar(out=mdst, in0=osrc, scalar1=g.tm[:, mc:mc + 1], scalar2=None, op0=ALU.mult),
                 reads=(b_ones, g.kbuf, sb_), writes=(sb_,))
            spill(g.BV.ap()[gi, bi], sv2[:, 0:512], sb_)

    CT = [(994, 30), (1024, 512), (1536, 512)]
    for i in range(3):
        slot = g.W.get()
        stt, sbuf = slot
        sv = slot_view(stt, 16, 512)
        for j in range(2):
            for (t0, nt) in CT:
                pg, pgb = g.bank()
                P.mm([lambda e, kc=kc: e.matmul(pg[:, 0:nt], lhsT=sv[:, kc, 256 + j * 128:256 + (j + 1) * 128], rhs=hT[:, kc, t0:t0 + nt],
                                               start=(kc == 0), stop=(kc == KC - 1)) for kc in range(KC)], reads=(sbuf,), writes=(pgb,))
                pa, pab = g.bank()
                P.mm([lambda e, kc=kc: e.matmul(pa[:, 0:nt], lhsT=sv[:, kc, j * 128:(j + 1) * 128], rhs=hT[:, kc, t0:t0 + nt],
                                               start=(kc == 0), stop=(kc == KC - 1)) for kc in range(KC)], reads=(sbuf,), writes=(pab,))
                t_, tb = ntmp()
                s, sb_ = nstg()
                sv2 = s.bitcast(BF16)
                P.op("act", lambda e: e.activation(out=t_[:, 0:nt], in_=pg[:, 0:nt], func=AF.Sigmoid), reads=(pgb,), writes=(tb,))
                if nt == 30:
                    P.op("dve", lambda e: e.scalar_tensor_tensor(out=sv2[:, 0:nt], in0=pa[:, 0:nt], scalar=g.tm[:, 3:4], in1=t_[:, 0:nt],
                                                                 op0=ALU.mult, op1=ALU.mult), reads=(pab, tb, g.kbuf), writes=(sb_,))
                    c0 = 2
                else:
                    P.op("dve", lambda e: e.tensor_tensor(out=sv2[:, 0:nt], in0=pa[:, 0:nt], in1=t_[:, 0:nt], op=ALU.mult),
                         reads=(pab, tb), writes=(sb_,))
                    c0 = 32 + t0 - 1024
                f0 = i * 256 + j * 128
                spill(g.U.ap()[f0:f0 + 128, c0:c0 + nt], sv2[:, 0:nt], sb_)

    for i in range(12):
        def ev(pt, pb, j, t0, nt, i=i):
            s, sb_ = nstg()
            sv = s.bitcast(BF16)
            fcol = V_BGATE + i * 4 + j
            P.op("act", lambda e: e.activation(out=sv[:, 0:nt], in_=pt[:, 0:nt], func=AF.Sigmoid, bias=g.vec[:, fcol:fcol + 1]),
                 reads=(pb, g.kbuf), writes=(sb_,))
            f0 = i * 512 + j * 128
            spill(g.GATES.ap()[f0:f0 + 128, t0 - 1024:t0 - 1024 + nt], sv[:, 0:nt], sb_)
        fm(g.W.get(), 0, 512, OWN, ev)

    for name, src in (("AQ", g.AQ), ("AF", g.AF), ("AG", g.AG), ("AI", g.AI), ("BQ", g.BQ), ("BK", g.BK), ("BV", g.BV),
                      ("U", g.U), ("GATES", g.GATES)):
        t = g.tap(name, src.shape, src.dtype)
        if t is not None:
            P.barrier()
            P.dma("sp", t.ap(), src.ap())


def phase3_hgrn(g):
    P = g.P
    g.RBC.reset(); g.RA2.reset()
    RR = [g.RBC, g.RA2]
    sb = lambda shape, dt, name: P.sb(RR, shape, dt, name)
    YT = g.YT
    cst = g.cst
    S = sb([128, 8, 128], F32, "S"); Sbf = sb([128, 8, 128], BF16, "Sbf")
    vseg = sb([64, 8, 1024], BF16, "vseg")
    QpT = sb([128, 8, 512], BF16, "QpT"); KpT = sb([128, 8, 512], BF16, "KpT")
    Q2T = sb([128, 8, 512], BF16, "Q2T"); bQ2 = Buf("Q2")
    Ktok = sb([64, 8, 8, 128], BF16, "Ktok")
    attnT = sb([64, 8, 8, 64], BF16, "attnT")
    gseg = sb([128, 8, 512], BF16, "gseg")
    dec = sb([128, 8, 8], F32, "dec")
    onesb = sb([128, 128], BF16, "onesb")
    zt = [sb([128, 512], F32, f"zt{i}") for i in range(2)]
    qh = [sb([128, 512], BF16, f"qh{i}") for i in range(2)]
    tf = sb([128, 512], F32, "tf"); tl = sb([128, 512], F32, "tl"); tk = sb([128, 512], F32, "tk")
    tA = sb([128, 512], F32, "tA"); tB = sb([128, 512], F32, "tB"); tC = sb([128, 512], F32, "tC")
    KppT = sb([128, 512], BF16, "KppT")
    rr, to = tk, tf
    osq2 = [KppT, sb([128, 512], BF16, "osqB")]
    bS, bSbf, bV, bQ, bK, bKt, bAt, bG, bDec, bOnes = (Buf(n) for n in "S Sbf V Q K Kt At G Dec Ones".split())
    bz = [Buf("z0"), Buf("z1")]; bq = [Buf("q0"), Buf("q1")]
    btf, btl, btk, btA, btB, btC, bKpp = (Buf(n) for n in "tf tl tk tA tB tC Kpp".split())
    brr, bto = btk, btf
    bosq2 = [bKpp, Buf("osqB")]
    resetm = cst[:, C_RESET:C_RESET + 512]
    caus = bass.AP(cst, C_CAUS, [[NCONST, 64], [0, 8], [1, 64]])
    P.op("dve", lambda e: e.memset(S[:], 0.0), writes=(bS,))
    P.op("dve", lambda e: e.memset(Sbf[:], 0.0), writes=(bSbf,))
    P.op("dve", lambda e: e.memset(onesb[:], 1.0), writes=(bOnes,))
    v3 = lambda t: t[:].rearrange("p (c s) -> p c s", s=64)
    hcount = 0
    for sg in range(4):
        own = sg >= 2
        o0 = (sg - 2) * 512
        P.dma("sp", vseg[:], g.AI.ap()[sg * 512:(sg + 1) * 512, :].rearrange("(c p) n -> p c n", p=64), writes=(bV,))
        if own:
            P.dma("sp", gseg[:], g.AG.ap().rearrange("(h v) t -> v h t", v=128)[:, :, o0:o0 + 512], writes=(bG,))
        for h in range(8):
            z, b_z = zt[hcount % 2], bz[hcount % 2]
            q, b_q = qh[hcount % 2], bq[hcount % 2]
            hcount += 1
            P.dma("sp", z[:], g.AF.ap()[h * 128:(h + 1) * 128, sg * 512:(sg + 1) * 512], writes=(b_z,))
            if own:
                P.dma("sp", q[:], g.AQ.ap()[h * 128:(h + 1) * 128, o0:o0 + 512], writes=(b_q,))
            lb = g.lbv[:, 0, h:h + 1]; oml = g.lbv[:, 1, h:h + 1]
            P.op("act", lambda e: e.activation(out=tf[:], in_=z[:], func=AF.Exp, scale=-1.0), reads=(b_z,), writes=(btf,))
            P.op("dve", lambda e: e.tensor_scalar(out=tf[:], in0=tf[:], scalar1=1.0, scalar2=None, op0=ALU.add), reads=(btf,), writes=(btf,))
            P.op("dve", lambda e: e.reciprocal(out=tf[:], in_=tf[:]), reads=(btf,), writes=(btf,))
            P.op("dve", lambda e: e.tensor_scalar(out=tf[:], in0=tf[:], scalar1=oml, scalar2=lb, op0=ALU.mult, op1=ALU.add),
                 reads=(btf, g.b_lb), writes=(btf,))
            P.op("act", lambda e: e.activation(out=tl[:], in_=tf[:], func=AF.Ln), reads=(btf,), writes=(btl,))
            P.op("dve", lambda e: e.tensor_scalar(out=tk[:], in0=tf[:], scalar1=-1.0, scalar2=1.0, op0=ALU.mult, op1=ALU.add),
                 reads=(btf,), writes=(btk,))
            P.op("dve", lambda e: e.tensor_tensor_scan(out=tl[:], data0=resetm, data1=tl[:], initial=0.0, op0=ALU.mult, op1=ALU.add),
                 reads=(btl, g.kbuf), writes=(btl,))
            Bl = v3(tl)[:, :, 63:64]
            P.op("dve", lambda e: e.tensor_tensor(out=v3(tA), in0=Bl.to_broadcast([128, 8, 64]), in1=v3(tl), op=ALU.subtract),
                 reads=(btl,), writes=(btA,))
            P.op("act", lambda e: e.activation(out=tA[:], in_=tA[:], func=AF.Exp), reads=(btA,), writes=(btA,))
            P.op("dve", lambda e: e.tensor_tensor(out=KppT[:], in0=tk[:], in1=tA[:], op=ALU.mult), reads=(btk, btA), writes=(bKpp,))
            P.op("act", lambda e: e.activation(out=dec[:, :, h], in_=v3(tl)[:, :, 63], func=AF.Exp), reads=(btl,), writes=(bDec,))
            if own:
                Bm = v3(tl)[:, :, 31:32]
                P.op("dve", lambda e: e.tensor_tensor(out=v3(tB), in0=v3(tl), in1=Bm.to_broadcast([128, 8, 64]), op=ALU.subtract),
                     reads=(btl,), writes=(btB,))
                P.op("act", lambda e: e.activation(out=tC[:], in_=tB[:], func=AF.Exp), reads=(btB,), writes=(btC,))
                P.op("act", lambda e: e.activation(out=tB[:], in_=tB[:], func=AF.Exp, scale=-1.0), reads=(btB,), writes=(btB,))
                P.op("dve", lambda e: e.tensor_tensor(out=QpT[:, h, :], in0=q[:], in1=tC[:], op=ALU.mult), reads=(b_q, btC), writes=(bQ,))
                P.op("dve", lambda e: e.tensor_tensor(out=KpT[:, h, :], in0=tk[:], in1=tB[:], op=ALU.mult), reads=(btk, btB), writes=(bK,))
                P.op("act", lambda e: e.activation(out=tA[:], in_=tl[:], func=AF.Exp), reads=(btl,), writes=(btA,))
                P.op("dve", lambda e: e.tensor_tensor(out=Q2T[:, h, :], in0=q[:], in1=tA[:], op=ALU.mult), reads=(b_q, btA), writes=(bQ2,))
            pt, pb = g.bank()
            ptb = pt.bitcast(BF16)
            P.mm([lambda e, c=c: e.transpose(ptb[0:64, c * 128:(c + 1) * 128], KppT[:, c * 64:(c + 1) * 64], g.identb[:])
                  for c in range(8)], reads=(bKpp, g.kbuf), writes=(pb,))
            P.op("act", lambda e: e.copy(out=Ktok[:, h, :, :].rearrange("p c k -> p (c k)"), in_=ptb[0:64, 0:1024]),
                 reads=(pb,), writes=(bKt,))
            if h % 2 == 1:
                phase0_hook(g)
            if own:
                pt2, pb2 = g.bank()
                P.mm([lambda e, c=c: e.matmul(pt2[0:64, c * 64:(c + 1) * 64], lhsT=KpT[:, h, c * 64:(c + 1) * 64],
                                              rhs=QpT[:, h, c * 64:(c + 1) * 64], start=True, stop=True) for c in range(8)],
                     reads=(bK, bQ), writes=(pb2,))
                P.op("dve", lambda e: e.tensor_tensor(out=attnT[:, h, :, :], in0=pt2[0:64, :].rearrange("p (c t) -> p c t", t=64),
                                                      in1=caus, op=ALU.mult), reads=(pb2, g.kbuf), writes=(bAt,))
        pending = None
        for c in range(8):
            last = (sg == 3 and c == 7)
            if not last:
                pdA, pdAb = g.bank()
                pdB, pdBb = g.bank()
                mms = []
                for h in range(8):
                    pd = pdA if h < 4 else pdB
                    hh = h % 4
                    mms.append(lambda e, h=h, pd=pd, hh=hh: e.matmul(pd[:, hh * 128:(hh + 1) * 128], lhsT=Ktok[0:64, h, c, :],
                                                                       rhs=vseg[0:64, c, h * 128:(h + 1) * 128], start=True, stop=True))
                P.mm(mms, reads=(bKt, bV), writes=(pdAb, pdBb))
            if own:
                po, pob = g.bank()
                mms = []
                for h in range(8):
                    mms.append(lambda e, h=h: e.matmul(po[:, h * 64:(h + 1) * 64], lhsT=vseg[0:64, c, h * 128:(h + 1) * 128],
                                                       rhs=attnT[0:64, h, c, :], start=True, stop=False))
                    mms.append(lambda e, h=h: e.matmul(po[:, h * 64:(h + 1) * 64], lhsT=Sbf[:, h, :],
                                                       rhs=Q2T[:, h, c * 64:(c + 1) * 64], start=False, stop=True))
                P.mm(mms, reads=(bV, bAt, bSbf, bQ2), writes=(pob,))
                oq, boq = osq2[c % 2], bosq2[c % 2]
                P.op("act", lambda e: e.activation(out=oq[:], in_=po[:, :], func=AF.Square), reads=(pob,), writes=(boq,))
            if not last:
                P.op("dve", lambda e: e.tensor_tensor(out=S[:], in0=S[:], in1=dec[:, c, :].unsqueeze(2).to_broadcast([128, 8, 128]),
                                                      op=ALU.mult), reads=(bS, bDec), writes=(bS,))
                P.op("dve", lambda e: e.tensor_tensor(out=S[:, 0:4, :], in0=S[:, 0:4, :], in1=pdA[:, :].rearrange("p (h v) -> p h v", v=128),
                                                      op=ALU.add), reads=(bS, pdAb), writes=(bS,))
                P.op("dve", lambda e: e.tensor_tensor(out=S[:, 4:8, :], in0=S[:, 4:8, :], in1=pdB[:, :].rearrange("p (h v) -> p h v", v=128),
                                                      op=ALU.add), reads=(bS, pdBb), writes=(bS,))
                P.op("act", lambda e: e.copy(out=Sbf[:], in_=S[:]), reads=(bS,), writes=(bSbf,))

            def tail(po, pob, oq, boq, c):
                ps, psb = g.bank()
                P.mm([lambda e: e.matmul(ps[:, :], lhsT=onesb[:], rhs=oq[:], start=True, stop=True)], reads=(boq, bOnes), writes=(psb,))
                P.op("act", lambda e: e.activation(out=rr[:], in_=ps[:, :], func=AF.Sqrt, scale=1.0 / 128, bias=EPS), reads=(psb,), writes=(brr,))
                P.op("dve", lambda e: e.reciprocal(out=rr[:], in_=rr[:]), reads=(brr,), writes=(brr,))
                P.op("dve", lambda e: e.tensor_tensor(out=to[:], in0=po[:, :], in1=rr[:], op=ALU.mult), reads=(pob, brr), writes=(bto,))
                tok0 = o0 + c * 64
                P.op("dve", lambda e: e.tensor_tensor(out=YT[:, 0:8, tok0:tok0 + 64], in0=to[:].rearrange("p (h t) -> p h t", t=64),
                                                      in1=gseg[:, :, c * 64:(c + 1) * 64], op=ALU.mult), reads=(bto, bG))
            if pending is not None:
                tail(*pending)
                pending = None
            if own:
                pending = (po, pob, oq, boq, c)
        if pending is not None:
            tail(*pending)
            pending = None
    t = g.tap("YA", [1024, TO], BF16)
    if t is not None:
        P.barrier()
        P.dma("sp", t.ap().rearrange("(h v) t -> v h t", v=128), YT[:, 0:8, :])


def build_bias_tables(g):
    P = g.P
    R = g.RBC
    R.reset()
    relb = P.sb(R, [32, 12], F32, "relb")
    relrep = P.sb(R, [32, 12, 128], F32, "relrep")
    oh = P.sb(R, [32, 3, 383], F32, "oh")
    fr = [P.sb(R, [128, 383], F32, f"fr{i}") for i in range(2)]
    bfr = [Buf("fr0"), Buf("fr1")]
    b_in = Buf("relb")
    P.dma("sp", relb[:], g.rel_bias.ap(), writes=(b_in,))
    P.dma("sp", oh[:], g.onehot.ap().rearrange("g b j -> b g j"), writes=(b_in,), add_write=True)
    P.op("dve", lambda e: e.tensor_copy(out=relrep[:], in_=relb[:].unsqueeze(2).to_broadcast([32, 12, 128])), reads=(b_in,), writes=(b_in,))
    for h in range(12):
        gi = h // 4
        pt, pb = g.bank()
        P.mm([lambda e: e.matmul(pt[:, 0:383], lhsT=relrep[:, h, :], rhs=oh[:, gi, :], start=True, stop=True)], reads=(b_in,), writes=(pb,))
        f, bf_ = fr[h % 2], bfr[h % 2]
        P.op("act", lambda e: e.activation(out=f[:], in_=pt[:, 0:383], func=AF.Exp), reads=(pb,), writes=(bf_,))
        P.op("dve", lambda e: e.tensor_tensor(out=f[:], in0=f[:], in1=g.cst[:, C_FVALID:C_FVALID + 383], op=ALU.mult),
             reads=(bf_, g.kbuf), writes=(bf_,))
        P.dma("sp", g.FT.ap()[h].rearrange("(p j) -> p j", j=383), f[:], reads=(bf_,))


def phase3_attn(g):
    P = g.P
    g.RBC.reset(); g.RA2.reset()
    RR = [g.RBC, g.RA2]
    sb = lambda shape, dt, name: P.sb(RR, shape, dt, name)
    YT = g.YT
    EB = sb([128, 12, 2, 128], F32, "EB")
    QT = sb([64, 4, TO], BF16, "QT"); KT = sb([64, 4, TC], BF16, "KT")
    Vg = sb([128, 16, 512], BF16, "Vg")
    UZ = sb([128, 4, TO], F32, "UZ")
    pe = [sb([128, 512], F32, f"pe{i}") for i in range(2)]
    pT = [sb([128, 512], BF16, f"pT{i}") for i in range(2)]
    rz = sb([128, 512], F32, "rz")
    bEB, bQ, bK, bV, bUZ, brz = (Buf(n) for n in "EB Q K V UZ rz".split())
    bpe = [Buf("pe0"), Buf("pe1")]; bpT = [Buf("pT0"), Buf("pT1")]
    first = True
    for h in range(12):
        for ty in range(2):
            src = bass.AP(g.FT, h * 128 * 383 + (127 if ty == 1 else 255), [[382, 128], [1, 128]])
            P.dma("sp", EB[:, h, ty, :], src, writes=(bEB,), add_write=not first)
            first = False
    cnt = 0
    for gi, dil in enumerate((1, 4, 16)):
        P.dma("sp", QT[:], g.BQ.ap()[gi * 256:(gi + 1) * 256, :].rearrange("(h d) t -> d h t", d=64), writes=(bQ,))
        P.dma("sp", KT[:], g.BK.ap()[gi * 256:(gi + 1) * 256, :].rearrange("(h d) t -> d h t", d=64), writes=(bK,))
        P.dma("sp", Vg[:], g.BV.ap()[gi].rearrange("b p n -> p b n"), writes=(bV,))
        for hh in range(4):
            h = gi * 4 + hh
            if gi < 2:
                nbatch = 4
            else:
                nbatch = 2
            for bt in range(nbatch):
                pt, pb = g.bank()
                po, pob = g.bank()
                e_, be_ = pe[cnt % 2], bpe[cnt % 2]
                p_, bp_ = pT[cnt % 2], bpT[cnt % 2]
                cnt += 1
                mm1, mm2 = [], []
                if gi == 0:
                    for j in range(2):
                        qb = bt * 2 + j
                        qs = QT[:, hh, qb * 128:(qb + 1) * 128]
                        for ty in range(2):
                            k0 = 896 + qb * 128 + ty * 128
                            reg = (j * 2 + ty) * 128
                            mm1.append(lambda e, k0=k0, reg=reg, qs=qs: e.matmul(pt[:, reg:reg + 128], lhsT=KT[:, hh, k0:k0 + 128], rhs=qs,
                                                                                 start=True, stop=True))
                            blk = 7 + qb + ty
                            mm2.append(lambda e, blk=blk, reg=reg, j=j, ty=ty: e.matmul(
                                po[:, j * 128:(j + 1) * 128], lhsT=Vg[:, blk, hh * 128:(hh + 1) * 128], rhs=p_[:, reg:reg + 128],
                                start=(ty == 0), stop=(ty == 1)))
                    eb_in = bass.AP(EB, h * 256, [[12 * 256, 128], [0, 2], [1, 256]])
                    uz_view = UZ[:, hh, bt * 256:(bt + 1) * 256]
                    ncol = 512
                elif gi == 1:
                    r = bt
                    for j in range(2):
                        n = 2 + j
                        qs = QT[:, hh, j * 512 + r:j * 512 + r + 509:4]
                        for ty in range(2):
                            m = n - 1 + ty
                            k0 = m * 512 + r
                            reg = (j * 2 + ty) * 128
                            mm1.append(lambda e, k0=k0, reg=reg, qs=qs: e.matmul(pt[:, reg:reg + 128], lhsT=KT[:, hh, k0:k0 + 509:4], rhs=qs,
                                                                                 start=True, stop=True))
                            blk = r * 4 + m
                            mm2.append(lambda e, blk=blk, reg=reg, j=j, ty=ty: e.matmul(
                                po[:, j * 128:(j + 1) * 128], lhsT=Vg[:, blk, hh * 128:(hh + 1) * 128], rhs=p_[:, reg:reg + 128],
                                start=(ty == 0), stop=(ty == 1)))
                    eb_in = bass.AP(EB, h * 256, [[12 * 256, 128], [0, 2], [1, 256]])
                    uz_view = UZ[:, hh, r:TO:4]
                    ncol = 512
                else:
                    for rr_ in range(8):
                        r = bt * 8 + rr_
                        reg = rr_ * 64
                        mm1.append(lambda e, r=r, reg=reg: e.matmul(pt[:, reg:reg + 64], lhsT=KT[:, hh, r:TC:16], rhs=QT[:, hh, r:TO:16],
                                                                    start=True, stop=True))
                        mm2.append(lambda e, r=r, reg=reg: e.matmul(po[:, reg:reg + 64], lhsT=Vg[:, r, hh * 128:(hh + 1) * 128],
                                                                    rhs=p_[:, reg:reg + 64], start=True, stop=True))
                    eb_in = bass.AP(EB, h * 256 + 128 + 64, [[12 * 256, 128], [0, 8], [1, 64]])
                    uz_view = bass.AP(UZ, hh * TO + bt * 8, [[4 * TO, 128], [1, 8], [16, 64]])
                    ncol = 512
                P.mm(mm1, reads=(bQ, bK), writes=(pb,))
                P.op("act", lambda e: e.activation(out=e_[:, 0:ncol], in_=pt[:, 0:ncol], func=AF.Exp), reads=(pb,), writes=(be_,))
                if gi < 2:
                    e_v = e_[:, 0:512].rearrange("p (j c) -> p j c", j=2)
                    p_v = p_[:, 0:512].rearrange("p (j c) -> p j c", j=2)
                else:
                    e_v = e_[:, 0:512].rearrange("p (j c) -> p j c", j=8)
                    p_v = p_[:, 0:512].rearrange("p (j c) -> p j c", j=8)
                P.op("dve", lambda e: e.tensor_tensor(out=p_v, in0=e_v, in1=eb_in, op=ALU.mult), reads=(be_, bEB), writes=(bp_,))
                P.mm(mm2, reads=(bV, bp_), writes=(pob,))
                nout = 256 if gi < 2 else 512
                if gi == 0:
                    P.op("act", lambda e: e.copy(out=uz_view, in_=po[:, 0:nout]), reads=(pob,), writes=(bUZ,))
                elif gi == 1:
                    P.op("dve", lambda e: e.tensor_tensor(out=uz_view, in0=po[:, 0:nout], in1=uz_view, op=ALU.add), reads=(pob, bUZ), writes=(bUZ,))
                else:
                    P.op("dve", lambda e: e.tensor_tensor(out=uz_view, in0=po[:, 0:nout].rearrange("p (r l) -> p r l", r=8), in1=uz_view,
                                                          op=ALU.add), reads=(pob, bUZ), writes=(bUZ,))
    for s in range(4):
        for tb in range(2):
            pz, pzb = g.bank()
            P.mm([lambda e: e.matmul(pz[:, :], lhsT=g.cst[:, C_SWAP:C_SWAP + 128], rhs=UZ[:, s, tb * 512:(tb + 1) * 512], start=True, stop=True)],
                 reads=(bUZ, g.kbuf), writes=(pzb,))
            lo = 0 if s % 2 == 0 else 64
            P.op("dve", lambda e: e.reciprocal(out=rz[lo:lo + 64, :], in_=pz[lo:lo + 64, :]), reads=(pzb,), writes=(brz,))
            P.op("dve", lambda e: e.tensor_tensor(out=YT[lo:lo + 64, 8 + s // 2, tb * 512:(tb + 1) * 512], in0=UZ[lo:lo + 64, s, tb * 512:(tb + 1) * 512],
                                                  in1=rz[lo:lo + 64, :], op=ALU.mult), reads=(brz, bUZ))
    t = g.tap("YB", [256, TO], BF16)
    if t is not None:
        P.barrier()
        P.dma("sp", t.ap().rearrange("(c p) t -> p c t", p=128), YT[:, 8:10, :])


def phase3_conv(g):
    P = g.P
    g.RBC.reset(); g.RA2.reset()
    RR = [g.RBC, g.RA2]
    sb = lambda shape, dt, name: P.sb(RR, shape, dt, name)
    YT = g.YT
    UT = sb([128, 6, TO + 32], BF16, "UT")
    yT = sb([128, 6, TO], F32, "yT")
    ybf = sb([128, 6, 512], BF16, "ybf"); ysq = sb([128, 6, 512], BF16, "ysq")
    m_ = sb([128, 512], F32, "m"); v_ = sb([128, 512], F32, "v"); rs = sb([128, 512], F32, "rs")
    tt = [sb([128, 512], F32, f"tt{i}") for i in range(2)]
    onesb = sb([128, 128], BF16, "onesb")
    dg = [sb([128, 128], BF16, f"dg{i}") for i in range(4)]
    bdg = [Buf(f"dg{i}") for i in range(4)]
    bU, by, bybf, bysq, bm, bv, brs, bOnes = (Buf(n) for n in "U y ybf ysq m v rs ones".split())
    btt = [Buf("tt0"), Buf("tt1")]
    P.op("dve", lambda e: e.memset(onesb[:], 1.0), writes=(bOnes,))
    P.dma("sp", UT[:, :, 2:TO + 32], g.U.ap().rearrange("(t p) n -> p t n", p=128)[:, :, 2:TO + 32], writes=(bU,))
    k = 0
    for ct in range(6):
        pA, pAb = g.bank()
        pB, pBb = g.bank()
        for j in range(31):
            d, bd = dg[k % 4], bdg[k % 4]
            k += 1
            col = V_CONVW + ct * 31 + j
            P.op("dve", lambda e: e.tensor_scalar(out=d[:], in0=g.identb[:], scalar1=g.vec[:, col:col + 1], scalar2=None, op0=ALU.mult),
                 reads=(g.kbuf,), writes=(bd,))
            P.mm([lambda e: e.matmul(pA[:, :], lhsT=d[:], rhs=UT[:, ct, 2 + j:2 + j + 512], start=(j == 0), stop=(j == 30)),
                  lambda e: e.matmul(pB[:, :], lhsT=d[:], rhs=UT[:, ct, 512 + 2 + j:512 + 2 + j + 512], start=(j == 0), stop=(j == 30))],
                 reads=(bd, bU), writes=(pAb, pBb))
        cb = g.vec[:, V_CONVB + ct:V_CONVB + ct + 1]
        P.op("act", lambda e: e.activation(out=yT[:, ct, 0:512], in_=pA[:, :], func=AF.Identity, bias=cb), reads=(pAb, g.kbuf), writes=(by,))
        P.op("act", lambda e: e.activation(out=yT[:, ct, 512:1024], in_=pB[:, :], func=AF.Identity, bias=cb), reads=(pBb, g.kbuf, by), writes=(by,))
    for tb in range(2):
        ts = slice(tb * 512, (tb + 1) * 512)
        P.op("act", lambda e: e.copy(out=ybf[:], in_=yT[:, :, ts]), reads=(by,), writes=(bybf,))
        P.op("act", lambda e: e.activation(out=ysq[:], in_=yT[:, :, ts], func=AF.Square), reads=(by,), writes=(bysq,))
        p1, p1b = g.bank()
        p2, p2b = g.bank()
        P.mm([lambda e, ct=ct: e.matmul(p1[:, :], lhsT=onesb[:], rhs=ybf[:, ct, :], start=(ct == 0), stop=(ct == 5)) for ct in range(6)],
             reads=(bybf, bOnes), writes=(p1b,))
        P.mm([lambda e, ct=ct: e.matmul(p2[:, :], lhsT=onesb[:], rhs=ysq[:, ct, :], start=(ct == 0), stop=(ct == 5)) for ct in range(6)],
             reads=(bysq, bOnes), writes=(p2b,))
        P.op("dve", lambda e: e.tensor_scalar(out=m_[:], in0=p1[:, :], scalar1=1.0 / 768, scalar2=None, op0=ALU.mult), reads=(p1b,), writes=(bm,))
        P.op("dve", lambda e: e.tensor_tensor(out=v_[:], in0=m_[:], in1=m_[:], op=ALU.mult), reads=(bm,), writes=(bv,))
        P.op("dve", lambda e: e.scalar_tensor_tensor(out=v_[:], in0=p2[:, :], scalar=1.0 / 768, in1=v_[:], op0=ALU.mult, op1=ALU.subtract),
             reads=(p2b, bv), writes=(bv,))
        P.op("act", lambda e: e.activation(out=rs[:], in_=v_[:], func=AF.Sqrt, bias=EPS), reads=(bv,), writes=(brs,))
        P.op("dve", lambda e: e.reciprocal(out=rs[:], in_=rs[:]), reads=(brs,), writes=(brs,))
        for ct in range(6):
            t_, bt_ = tt[ct % 2], btt[ct % 2]
            P.op("dve", lambda e: e.tensor_tensor(out=t_[:], in0=yT[:, ct, ts], in1=m_[:], op=ALU.subtract), reads=(by, bm), writes=(bt_,))
            P.op("dve", lambda e: e.tensor_tensor(out=t_[:], in0=t_[:], in1=rs[:], op=ALU.mult), reads=(bt_, brs), writes=(bt_,))
            P.op("act", lambda e: e.activation(out=YT[:, 10 + ct, ts], in_=t_[:], func=AF.Silu, scale=g.vec[:, V_LNG + ct:V_LNG + ct + 1],
                                               bias=g.vec[:, V_LNB + ct:V_LNB + ct + 1]), reads=(bt_, g.kbuf))
    t = g.tap("YC", [768, TO], BF16)
    if t is not None:
        P.barrier()
        P.dma("sp", t.ap().rearrange("(c p) t -> p c t", p=128), YT[:, 10:16, :])


def plan_weights_p45(g):
    W = g.W
    wv = lambda w, c0, n: w.ap().rearrange("(kc p) n -> p kc n", p=128)[:, :, c0:c0 + n]
    for cb in range(4):
        W.add([(lambda t: slot_view(t, 8, 512, kc0=0), wv(g.w_a_out, cb * 512, 512)),
               (lambda t: slot_view(t, 2, 512, kc0=8), wv(g.w_b_out, cb * 512, 512)),
               (lambda t: slot_view(t, 6, 512, kc0=10), wv(g.w_c_out, cb * 512, 512))])
    for cb in range(4):
        W.add([(lambda t: slot_view(t, 16, 512), wv(g.w_o, cb * 512, 512))])
    for gq in range(8):
        for i in range(2):
            W.add([(lambda t: slot_view(t, 16, 512), wv(g.w_up, gq * 1024 + i * 512, 512))])
        for half in range(2):
            src = g.w_down.ap()[gq * 1024:(gq + 1) * 1024, half * 1024:(half + 1) * 1024].rearrange("(kc p) n -> p kc n", p=128)
            W.add([(lambda t: slot_view(t, 8, 1024, width=1024), src)])


def phase4a(g):
    P = g.P
    g.RBC.reset(); g.RA2.reset()
    YT = g.YT
    g.mT = P.sb(g.RA2, [128, 16, TO], BF16, "mT")
    R = g.RBC
    gt = [P.sb(R, [128, 3, TO], BF16, f"gt{i}") for i in range(2)]
    bgt = [Buf("gt0"), Buf("gt1")]
    t1 = [P.sb(R, [128, 512], F32, f"t1{i}") for i in range(2)]
    t2 = [P.sb(R, [128, 512], F32, f"t2{i}") for i in range(2)]
    bt1 = [Buf("t10"), Buf("t11")]; bt2 = [Buf("t20"), Buf("t21")]
    gv = g.GATES.ap().rearrange("(i f) t -> f i t", i=3)
    k = 0
    for cb in range(4):
        stt, sbuf = g.W.get()
        sv = slot_view(stt, 16, 512)
        for j in range(4):
            fo = cb * 4 + j
            gg, bg = gt[fo % 2], bgt[fo % 2]
            P.dma("sp", gg[:], gv[fo * 128:(fo + 1) * 128], writes=(bg,))
            for tb in range(2):
                ts = slice(tb * 512, (tb + 1) * 512)
                banks = []
                for (k0, k1) in ((0, 8), (8, 10), (10, 16)):
                    pt, pb = g.bank()
                    P.mm([lambda e, kc=kc, pt=pt, k0=k0, k1=k1: e.matmul(pt[:, :], lhsT=sv[:, kc, j * 128:(j + 1) * 128], rhs=YT[:, kc, ts],
                                                                         start=(kc == k0), stop=(kc == k1 - 1)) for kc in range(k0, k1)],
                         reads=(sbuf,), writes=(pb,))
                    banks.append((pt, pb))
                a, ba = t1[k % 2], bt1[k % 2]
                b, bb = t2[k % 2], bt2[k % 2]
                k += 1
                (pA, pAb), (pB, pBb), (pC, pCb) = banks
                P.op("dve", lambda e: e.tensor_tensor(out=a[:], in0=pA[:, :], in1=gg[:, 0, ts], op=ALU.mult), reads=(pAb, bg), writes=(ba,))
                P.op("dve", lambda e: e.tensor_tensor(out=b[:], in0=pB[:, :], in1=gg[:, 1, ts], op=ALU.mult), reads=(pBb, bg), writes=(bb,))
                P.op("dve", lambda e: e.tensor_tensor(out=a[:], in0=a[:], in1=b[:], op=ALU.add), reads=(ba, bb), writes=(ba,))
                P.op("dve", lambda e: e.tensor_tensor(out=b[:], in0=pC[:, :], in1=gg[:, 2, ts], op=ALU.mult), reads=(pCb, bg, bb), writes=(bb,))
                P.op("dve", lambda e: e.tensor_tensor(out=g.mT[:, fo, ts], in0=a[:], in1=b[:], op=ALU.add), reads=(ba, bb))
    t = g.tap("MT", [D, TO], BF16)
    if t is not None:
        P.barrier()
        P.dma("sp", t.ap().rearrange("(c p) t -> p c t", p=128), g.mT[:])


def phase4b(g):
    P = g.P
    j0 = g.W.taken
    slots = [g.W.get(hold_from=j0) for _ in range(4)]
    g.RA1.reset(); g.RB.reset(); g.RC.reset()
    g.h2T = P.sb(g.RB, [128, 16, TO], BF16, "h2T")
    xt = [P.sb(g.RA1, [128, D], F32, f"xt{i}") for i in range(2)]
    bxt = [Buf("xt0"), Buf("xt1")]
    Grow = P.sb(g.RA1, [128, D], F32, "Grow")
    tq = P.sb(g.RA1, [128, D], F32, "tq")
    x1t = [P.sb(g.RC, [128, 1, D], F32, f"x1t{i}") for i in range(2)]
    bx1 = [Buf("x1t0"), Buf("x1t1")]
    junk = P.sb(g.RC, [128, D], BF16, "junk")
    st = P.sb(g.RC, [128, 8], F32, "st"); st2 = P.sb(g.RC, [128, 8], F32, "st2")
    bst, bst2, bG, btq = Buf("st"), Buf("st2"), Buf("Grow"), Buf("tq")
    postbc = P.sb(g.RC, [128, D], F32, "postbc")
    P.dma("sp", Grow[:], bass.AP(g.GROWS, 0, [[0, 128], [1, D]]), writes=(bG,))
    P.dma("sp", postbc[:], bass.AP(g.rows2, g.l * NROW + R_POST, [[0, 128], [1, D]]), writes=(bG,), add_write=True)
    P.op("dve", lambda e: e.tensor_tensor(out=Grow[:], in0=Grow[:], in1=postbc[:], op=ALU.mult), reads=(bG,), writes=(bG,))
    xv = g.xctx.ap().rearrange("(t p) d -> p t d", p=128)
    for ti in range(8):
        x_, bx_ = xt[ti % 2], bxt[ti % 2]
        x1, b1 = x1t[ti % 2], bx1[ti % 2]
        P.dma("sp", x_[:], xv[:, 8 + ti, :], writes=(bx_,))
        bks = []
        for cb in range(4):
            pt, pb = g.bank()
            stt, sbuf = slots[cb]
            sv = slot_view(stt, 16, 512)
            P.mm([lambda e, kc=kc, pt=pt, sv=sv: e.matmul(pt[:, :], lhsT=g.mT[:, kc, ti * 128:(ti + 1) * 128], rhs=sv[:, kc, :],
                                                          start=(kc == 0), stop=(kc == KC - 1)) for kc in range(KC)], reads=(sbuf,), writes=(pb,))
            bks.append((pt, pb))
        for cb, (pt, pb) in enumerate(bks):
            P.op("act", lambda e, pt=pt, cb=cb: e.activation(out=junk[:, 0:512], in_=pt[:, :], func=AF.Square, accum_out=st2[:, cb:cb + 1]),
                 reads=(pb,), writes=(bst2,))
            P.op("dve", lambda e, pt=pt, cb=cb: e.tensor_tensor(out=tq[:, cb * 512:(cb + 1) * 512], in0=pt[:, :], in1=Grow[:, cb * 512:(cb + 1) * 512],
                                                               op=ALU.mult), reads=(pb, bG), writes=(btq,))
        P.op("dve", lambda e: e.reduce_sum(out=st2[:, 4:5], in_=st2[:, 0:4], axis=AX.X), reads=(bst2,), writes=(bst2,))
        P.op("dve", lambda e: e.tensor_scalar(out=st2[:, 5:6], in0=st2[:, 4:5], scalar1=1.0 / D, scalar2=EPS, op0=ALU.mult, op1=ALU.add),
             reads=(bst2,), writes=(bst2,))
        P.op("act", lambda e: e.activation(out=st2[:, 6:7], in_=st2[:, 5:6], func=AF.Sqrt), reads=(bst2,), writes=(bst2,))
        P.op("dve", lambda e: e.reciprocal(out=st2[:, 7:8], in_=st2[:, 6:7]), reads=(bst2,), writes=(bst2,))
        P.op("dve", lambda e: e.scalar_tensor_tensor(out=x1[:, 0, :], in0=tq[:], scalar=st2[:, 7:8], in1=x_[:], op0=ALU.mult, op1=ALU.add),
             reads=(btq, bst2, bx_), writes=(b1,))
        P.dma("sp", g.X1.ap()[ti * 128:(ti + 1) * 128, :], x1[:, 0, :], reads=(b1,))
        norm_transpose(g, x1, b1, 1, g.AB[:, 2, :], g.AB[:, 3, :], lambda fc, ti=ti: g.h2T[:, fc, ti * 128:(ti + 1) * 128], junk, st, bst)
    t = g.tap("X1", [TO, D], F32)
    if t is not None:
        P.barrier()
        P.dma("sp", t.ap(), g.X1.ap())
    t = g.tap("H2T", [D, TO], BF16)
    if t is not None:
        P.barrier()
        P.dma("sp", t.ap().rearrange("(c p) t -> p c t", p=128), g.h2T[:])


def phase5(g):
    P = g.P
    g.RA.reset(); g.RC.reset()
    Y = P.sb(g.RA, [128, 8, D], F32, "Y")
    aT = P.sb(g.RC, [128, 8, TO], BF16, "aT")
    rst = [P.sb(g.RC, [128, 512], F32, f"rst{i}") for i in range(2)]
    brst = [Buf("rst0"), Buf("rst1")]
    baT = [Buf(f"aT{i}") for i in range(8)]
    bY = [[Buf(f"Y{ti}_{cb}") for cb in range(4)] for ti in range(8)]
    h2T = g.h2T
    k = 0
    for gq in range(8):
        for i in range(2):
            stt, sbuf = g.W.get()
            sv = slot_view(stt, 16, 512)
            for j in range(4):
                ffc = i * 4 + j
                for tb in range(2):
                    ts = slice(tb * 512, (tb + 1) * 512)
                    pt, pb = g.bank()
                    P.mm([lambda e, kc=kc, pt=pt: e.matmul(pt[:, :], lhsT=sv[:, kc, j * 128:(j + 1) * 128], rhs=h2T[:, kc, ts],
                                                           start=(kc == 0), stop=(kc == KC - 1)) for kc in range(KC)], reads=(sbuf,), writes=(pb,))
                    r_, br_ = rst[k % 2], brst[k % 2]
                    k += 1
                    P.op("act", lambda e, pt=pt, r_=r_: e.activation(out=r_[:], in_=pt[:, :], func=AF.Relu), reads=(pb,), writes=(br_,))
                    P.op("dve", lambda e, r_=r_, ffc=ffc, ts=ts: e.tensor_tensor(out=aT[:, ffc, ts], in0=r_[:], in1=r_[:], op=ALU.mult),
                         reads=(br_,), writes=(baT[ffc],))
        for half in range(2):
            stt, sbuf = g.W.get()
            sv8 = slot_view(stt, 8, 1024, width=1024)
            for cbh in range(2):
                cb = half * 2 + cbh
                for ti in range(8):
                    pt, pb = g.bank()
                    P.mm([lambda e, kc=kc, pt=pt: e.matmul(pt[:, :], lhsT=aT[:, kc, ti * 128:(ti + 1) * 128], rhs=sv8[:, kc, cbh * 512:(cbh + 1) * 512],
                                                           start=(kc == 0), stop=(kc == 7)) for kc in range(8)],
                         reads=(sbuf,) + tuple(baT), writes=(pb,))
                    yv = Y[:, ti, cb * 512:(cb + 1) * 512]
                    if gq == 0:
                        P.op("act", lambda e, pt=pt, yv=yv: e.copy(out=yv, in_=pt[:, :]), reads=(pb,), writes=(bY[ti][cb],))
                    else:
                        P.op("dve", lambda e, pt=pt, yv=yv: e.tensor_tensor(out=yv, in0=pt[:, :], in1=yv, op=ALU.add),
                             reads=(pb, bY[ti][cb]), writes=(bY[ti][cb],))
    P.barrier()
    g.RB.reset(); g.RC.reset()
    Grow = P.sb(g.RB, [128, D], F32, "Grow2")
    x1t = [P.sb(g.RB, [128, D], F32, f"x1f{i}") for i in range(2)]
    bx1 = [Buf("x1f0"), Buf("x1f1")]
    ot = [P.sb(g.RC, [128, D], F32, f"ot{i}") for i in range(2)]
    bot = [Buf("ot0"), Buf("ot1")]
    junk = P.sb(g.RC, [128, D], BF16, "junkf")
    tq = P.sb(g.RC, [128, D], F32, "tqf")
    postbc = P.sb(g.RC, [128, D], F32, "postbc2")
    st = P.sb(g.RB, [128, 8], F32, "stf")
    bG, bst, btq = Buf("G2"), Buf("stf"), Buf("tqf")
    P.dma("sp", Grow[:], bass.AP(g.GROWS, D, [[0, 128], [1, D]]), writes=(bG,))
    P.dma("sp", postbc[:], bass.AP(g.rows2, g.l * NROW + R_MLPPOST, [[0, 128], [1, D]]), writes=(bG,), add_write=True)
    P.op("dve", lambda e: e.tensor_tensor(out=Grow[:], in0=Grow[:], in1=postbc[:], op=ALU.mult), reads=(bG,), writes=(bG,))
    for ti in range(8):
        x1, b1 = x1t[ti % 2], bx1[ti % 2]
        o_, bo_ = ot[ti % 2], bot[ti % 2]
        P.dma("sp", x1[:], g.X1.ap()[ti * 128:(ti + 1) * 128, :], writes=(b1,))
        P.op("act", lambda e: e.activation(out=junk[:], in_=Y[:, ti, :], func=AF.Square, accum_out=st[:, 0:1]), writes=(bst,))
        P.op("dve", lambda e: e.tensor_scalar(out=st[:, 1:2], in0=st[:, 0:1], scalar1=1.0 / D, scalar2=EPS, op0=ALU.mult, op1=ALU.add),
             reads=(bst,), writes=(bst,))
        P.op("act", lambda e: e.activation(out=st[:, 2:3], in_=st[:, 1:2], func=AF.Sqrt), reads=(bst,), writes=(bst,))
        P.op("dve", lambda e: e.reciprocal(out=st[:, 3:4], in_=st[:, 2:3]), reads=(bst,), writes=(bst,))
        P.op("dve", lambda e: e.tensor_tensor(out=tq[:], in0=Y[:, ti, :], in1=Grow[:], op=ALU.mult), reads=(bG,), writes=(btq,))
        P.op("dve", lambda e: e.scalar_tensor_tensor(out=o_[:], in0=tq[:], scalar=st[:, 3:4], in1=x1[:], op0=ALU.mult, op1=ALU.add),
             reads=(btq, bst, b1), writes=(bo_,))
        P.dma("sp", g.xout.ap()[ti * 128:(ti + 1) * 128, :], o_[:], reads=(bo_,))


def make_consts():
    c = np.zeros((128, NCONST), np.float32)
    c[:, C_IDENT:C_IDENT + 128] = np.eye(128, dtype=np.float32)
    sw = np.zeros((128, 128), np.float32)
    for m in range(128):
        sw[(m + 64) % 128, m] = 1.0
    c[:, C_SWAP:C_SWAP + 128] = sw
    fv = np.zeros(383, np.float32); fv[127:127 + 129] = 1.0
    c[:, C_FVALID:C_FVALID + 383] = fv[None, :]
    s = np.arange(64)[:, None]; t = np.arange(64)[None, :]
    c[:64, C_CAUS:C_CAUS + 64] = (s <= t).astype(np.float32)
    rm = np.ones(512, np.float32); rm[0::64] = 0.0
    c[:, C_RESET:C_RESET + 512] = rm[None, :]
    c[:, C_ONES:C_ONES + 128] = 1.0
    return c

def t5_bucket_np(dist):
    import math
    exact = 16
    d = np.maximum(dist, 1).astype(np.float32)
    large = exact + (np.log(d / np.float32(exact)) / np.float32(math.log(2048 / exact)) * np.float32(32 - exact)).astype(np.int32)
    return np.where(dist < exact, dist, np.clip(large, exact, 31))

def make_onehot():
    oh = np.zeros((3, 32, 383), np.float32)
    for gi, dil in enumerate((1, 4, 16)):
        for s in range(129):
            b = int(t5_bucket_np(np.array([s * dil]))[0])
            oh[gi, b, 127 + s] = 1.0
    return oh

def col(v):
    return np.ascontiguousarray(np.asarray(v, np.float32).reshape(-1, 128).T)

def layer_small(inputs, l):
    vec = np.zeros((128, NVEC), np.float32)
    vec[:, V_PRE:V_PRE + 16] = col(inputs["mix_norm_pre"][l])
    vec[:, V_MLPPRE:V_MLPPRE + 16] = col(inputs["mlp_norm_pre"][l])
    vec[:, V_BGATE:V_BGATE + 48] = col(inputs["b_gate"][l])
    vec[:, V_NORMW:V_NORMW + 1] = col(inputs["hgrn_norm_w"][l])
    vec[:, V_CONVB:V_CONVB + 6] = col(inputs["conv_b"][l])
    vec[:, V_LNG:V_LNG + 6] = col(inputs["conv_ln_g"][l])
    vec[:, V_LNB:V_LNB + 6] = col(inputs["conv_ln_b"][l])
    cw = np.asarray(inputs["conv_w"][l], np.float32)
    vec[:, V_CONVW:V_CONVW + 186] = cw.reshape(31, 6, 128).transpose(2, 1, 0).reshape(128, 186)
    rows = np.zeros((1, NROW), np.float32)
    rows[0, R_BADA:R_BADA + 12288] = inputs["b_ada"][l]
    rows[0, R_POST:R_POST + 2048] = inputs["mix_norm_post"][l]
    rows[0, R_MLPPOST:R_MLPPOST + 2048] = inputs["mlp_norm_post"][l]
    return vec, rows

def core_inputs(inputs, l, b, half, xfull):
    xb = np.asarray(xfull[b], np.float32)
    if half == 0:
        xctx = np.concatenate([np.zeros((1024, D), np.float32), xb[:1024]], axis=0)
    else:
        xctx = xb
    flag = float(half)
    tm = np.zeros((128, 4), np.float32)
    tm[:, 0] = flag; tm[:, 1] = 1.0; tm[:64, 2] = flag; tm[64:, 2] = 1.0; tm[:, 3] = flag
    vec, rows = layer_small(inputs, l)
    lbl = np.asarray(inputs["hgrn_lb_logits"], np.float32).reshape(2, 8, 128).transpose(2, 0, 1)
    return {
        "xctx": np.ascontiguousarray(xctx), "c_b": col(inputs["c"][b]), "tmask": tm,
        "consts": make_consts(), "vecs": vec, "rows": rows, "lbl": np.ascontiguousarray(lbl),
        "rel_bias": np.asarray(inputs["rel_bias"], np.float32), "onehot": make_onehot(),
        "w_ada": np.asarray(inputs["w_ada"][l]), "w_in": np.asarray(inputs["w_in"][l]),
        "w_gate": np.asarray(inputs["w_gate"][l]), "w_a_out": np.asarray(inputs["w_a_out"][l]),
        "w_b_out": np.asarray(inputs["w_b_out"][l]), "w_c_out": np.asarray(inputs["w_c_out"][l]),
        "w_o": np.asarray(inputs["w_o"][l]), "w_up": np.asarray(inputs["w_up"][l]),
        "w_down": np.asarray(inputs["w_down"][l]),
        "layer_is1": np.full((128, 1), float(l), np.float32),
    }


def fused_core_inputs(inputs, b, half, shared):
    xb = np.asarray(inputs["x"][b], np.float32)
    z = np.zeros((1024, D), np.float32)
    if half == 1:
        x3 = np.concatenate([z, xb[:1024], xb[1024:]], axis=0)
    else:
        x3 = np.concatenate([z, z, xb[:1024]], axis=0)
    tm = np.zeros((2, 128, 4), np.float32)
    for i, flag in enumerate((0.0, float(half))):
        tm[i, :, 0] = flag; tm[i, :, 1] = 1.0; tm[i, :64, 2] = flag; tm[i, 64:, 2] = 1.0; tm[i, :, 3] = flag
    m = dict(shared)
    m["x3"] = np.ascontiguousarray(x3)
    m["c_b"] = col(inputs["c"][b])
    m["tmask"] = tm
    return m


def shared_inputs(inputs):
    vl = [layer_small(inputs, l) for l in range(2)]
    lbl = np.asarray(inputs["hgrn_lb_logits"], np.float32).reshape(2, 8, 128).transpose(2, 0, 1)
    sh = {
        "consts": make_consts(), "vecs": np.stack([v[0] for v in vl]), "rows": np.stack([v[1] for v in vl]),
        "lbl": np.ascontiguousarray(lbl), "rel_bias": np.asarray(inputs["rel_bias"], np.float32), "onehot": make_onehot(),
    }
    for k in ("w_ada", "w_in", "w_gate", "w_a_out", "w_b_out", "w_c_out", "w_o", "w_up", "w_down"):
        sh[k] = np.asarray(inputs[k], np.float32)
    return sh


_NC_CACHE = {}


def kernel(**inputs):
    inputs = {k: np.asarray(v) for k, v in inputs.items()}
    if "nc" not in _NC_CACHE:
        _NC_CACHE["nc"] = build_fused()[0]
    nc = _NC_CACHE["nc"]
    sh = shared_inputs(inputs)
    maps = [fused_core_inputs(inputs, b, half, sh) for b in range(4) for half in range(2)]
    res = run_bass_kernel_spmd(nc, maps, core_ids=list(range(8)))
    out = np.empty((4, 2048, D), np.float32)
    for b in range(4):
        for half in range(2):
            out[b, half * 1024:(half + 1) * 1024] = res.results[b * 2 + half]["xout"]
    return out
```

```python
import numpy as np
from contextlib import ExitStack
import concourse.bass as bass
import concourse.mybir as mybir
from concourse.bass_utils import run_bass_kernel_spmd

F32 = mybir.dt.float32
BF16 = mybir.dt.bfloat16
AF = mybir.ActivationFunctionType
ALU = mybir.AluOpType
AX = mybir.AxisListType

D = 2048
KC = 16
TC = 2048
TO = 1024
DFF = 8192
INW = 7936
EPS = 1e-6
SB_BASE = 16512
SB_TOP = 229344
NSLOT = 4
SLOT_BYTES = 16384
CONST_BYTES = 10240
RA_BYTES = 65536
RB_BYTES = 32768

C_IDENT = 0
C_SWAP = 128
C_FVALID = 256
C_CAUS = 256 + 383
C_RESET = C_CAUS + 64
C_ONES = C_RESET + 512
NCONST = C_ONES + 128
V_PRE = 0
V_MLPPRE = 16
V_BGATE = 32
V_NORMW = 80
V_CONVB = 81
V_LNG = 87
V_LNB = 93
V_CONVW = 99
NVEC = V_CONVW + 6 * 31
R_BADA = 0
R_POST = 12288
R_MLPPOST = 12288 + 2048
NROW = 12288 + 4096


class Buf:
    __slots__ = ("name", "w", "r", "excl")

    def __init__(self, name="", excl=False):
        self.name = name
        self.w = []
        self.r = {}
        self.excl = excl


class Region:
    def __init__(self, base, size):
        self.base = base
        self.size = size
        self.off = 0

    def reset(self):
        self.off = 0


class Prog:
    def __init__(self, nc, stack):
        self.nc = nc
        self.E = dict(pe=nc.tensor, act=nc.scalar, dve=nc.vector, pool=nc.gpsimd, sp=nc.sync)
        self.sem = {}
        self.cnt = {}
        for k in ("pe", "act", "dve", "pool"):
            self.sem[k] = stack.enter_context(nc.semaphore("s_" + k))
            self.cnt[k] = 0
        self.dsem = {}
        self.dcnt = {}
        self.dnext = {}
        for q, n in (("sp", 16), ("pool", 8)):
            self.dsem[q] = [stack.enter_context(nc.semaphore(f"d_{q}{i}")) for i in range(n)]
            self.dcnt[q] = [0] * n
            self.dnext[q] = 0
        self.waited = {e: {} for e in self.E}
        self.nalloc = 0
        self.n_ins = 0

    def _semof(self, key):
        if isinstance(key, tuple):
            return self.dsem[key[1]][key[2]]
        return self.sem[key]

    def _wait(self, e, key, val):
        if e == "pe" and key == "pe":
            return
        if self.waited[e].get(key, 0) >= val:
            return
        self.E[e].wait_ge(self._semof(key), val)
        self.waited[e][key] = val

    def _deps(self, e, reads, writes):
        best = {}
        for b in reads:
            for k, v in b.w:
                if best.get(k, 0) < v:
                    best[k] = v
            if b.excl:
                for k, v in b.r.items():
                    if k != e and best.get(k, 0) < v:
                        best[k] = v
        for b in writes:
            for k, v in b.w:
                if best.get(k, 0) < v:
                    best[k] = v
            for k, v in b.r.items():
                if best.get(k, 0) < v:
                    best[k] = v
        for k, v in best.items():
            self._wait(e, k, v)

    def _commit(self, tok, reads, writes):
        for b in writes:
            b.w = [tok]
            b.r = {}
        for b in reads:
            if b.r.get(tok[0], 0) < tok[1]:
                b.r[tok[0]] = tok[1]

    def op(self, e, fn, reads=(), writes=()):
        self._deps(e, reads, writes)
        ins = fn(self.E[e])
        self.cnt[e] += 1
        ins.then_inc(self.sem[e], 1)
        self._commit((e, self.cnt[e]), reads, writes)
        self.n_ins += 1
        return ins

    def mm(self, mms, reads=(), writes=()):
        self._deps("pe", reads, writes)
        ins = None
        for f in mms:
            ins = f(self.nc.tensor)
        self.cnt["pe"] += 1
        ins.then_inc(self.sem["pe"], 1)
        self._commit(("pe", self.cnt["pe"]), reads, writes)
        self.n_ins += len(mms)

    def dma(self, q, out, in_, reads=(), writes=(), add_write=False):
        i = self.dnext[q]
        self.dnext[q] = (i + 1) % len(self.dsem[q])
        key = ("d", q, i)
        self._wait(q, key, self.dcnt[q][i])
        self._deps(q, reads, writes)
        self.E[q].dma_start(out=out, in_=in_).then_inc(self.dsem[q][i], 16)
        self.dcnt[q][i] += 16
        tok = (key, self.dcnt[q][i])
        for b in writes:
            if add_write:
                b.w = b.w + [tok]
            else:
                b.w = [tok]
                b.r = {}
        for b in reads:
            b.r[key] = tok[1]
        self.n_ins += 1
        return tok

    def barrier(self, engines=("pe", "act", "dve", "sp")):
        toks = [(k, self.cnt[k]) for k in ("pe", "act", "dve") if self.cnt[k] > 0]
        for i, v in enumerate(self.dcnt["sp"]):
            if v > 0:
                toks.append((("d", "sp", i), v))
        for e in engines:
            for k, v in toks:
                self._wait(e, k, v)

    def final_wait(self):
        toks = [(k, self.cnt[k]) for k in ("pe", "act", "dve", "pool") if self.cnt[k] > 0]
        for q in ("sp", "pool"):
            for i, v in enumerate(self.dcnt[q]):
                if v > 0:
                    toks.append((("d", q, i), v))
        for e in ("sp", "pool", "act", "dve", "pe"):
            for k, v in toks:
                self._wait(e, k, v)

    def sb(self, region, shape, dtype, name="t"):
        esz = 4 if dtype == F32 else 2
        n = 1
        for s in shape[1:]:
            n *= s
        nbytes = (n * esz + 31) // 32 * 32
        regs = region if isinstance(region, (list, tuple)) else [region]
        for r in regs:
            if r.off + nbytes <= r.size:
                off = r.base + r.off
                r.off += nbytes
                self.nalloc += 1
                return self.nc.alloc_sbuf_tensor_at(f"{name}_{self.nalloc}", list(shape), dtype, offset=off)
        raise RuntimeError(f"SBUF region overflow allocating {name} {shape}")


class WStream:
    def __init__(self, P, slots):
        self.P = P
        self.slots = slots
        self.plan = []
        self.issued = 0
        self.taken = 0

    def add(self, pieces):
        self.plan.append(pieces)

    def _issue(self, j):
        t, b = self.slots[j % len(self.slots)]
        first = True
        for dst_fn, src in self.plan[j]:
            self.P.dma("pool", dst_fn(t), src, writes=(b,), add_write=not first)
            first = False

    def get(self, hold_from=None):
        j = self.taken
        self.taken += 1
        base = j if hold_from is None else hold_from
        lim = min(len(self.plan), base + len(self.slots))
        while self.issued < lim:
            self._issue(self.issued)
            self.issued += 1
        return self.slots[j % len(self.slots)]


def slot_view(t, kc, n, kc0=0, c0=0, width=512):
    return bass.AP(t, kc0 * width + c0, [[8192, 128], [width, kc], [1, n]])


class Ctx:
    pass


class LV:
    def __init__(self, t, idx=None, rows=None):
        self.t, self.idx, self.rows = t, idx, rows
        shp = list(t.shape)
        self.shape = shp[1:] if idx is not None else shp
        self.dtype = t.dtype

    def ap(self):
        a = self.t.ap()
        if self.idx is not None:
            a = a[self.idx]
        if self.rows is not None:
            a = a[self.rows[0]:self.rows[1]]
        return a


def build_fused(dbg=None, passes=("A", "B", "L1"), stop_after=None):
    dbg = dbg or set()
    nc = bass.Bass("TRN2", target_bir_lowering=False)
    stack = ExitStack()
    with stack:
        P = Prog(nc, stack)
        g = Ctx()
        g.nc, g.P, g.dbg = nc, P, dbg
        di = lambda name, shape, dt=F32: nc.dram_tensor(name, list(shape), dt, kind="ExternalInput")
        g.x3 = di("x3", [3 * TO, D])
        g.c_b = di("c_b", [128, KC])
        g.tmask2 = di("tmask", [2, 128, 4])
        g.consts = di("consts", [128, NCONST])
        g.vecs2 = di("vecs", [2, 128, NVEC])
        g.rows2 = di("rows", [2, 1, NROW])
        g.lbl = di("lbl", [128, 2, 8])
        g.rel_bias = di("rel_bias", [32, 12])
        g.onehot = di("onehot", [3, 32, 383])
        g.Wf = dict(
            w_ada=di("w_ada", [2, D, 6 * D]), w_in=di("w_in", [2, D, INW]), w_gate=di("w_gate", [2, D, 3 * D]),
            w_a_out=di("w_a_out", [2, 1024, D]), w_b_out=di("w_b_out", [2, 256, D]), w_c_out=di("w_c_out", [2, 768, D]),
            w_o=di("w_o", [2, D, D]), w_up=di("w_up", [2, D, DFF]), w_down=di("w_down", [2, DFF, D]))
        g.xout_t = nc.dram_tensor("xout", [TO, D], F32, kind="ExternalOutput")
        ds = lambda name, shape, dt: nc.dram_tensor(name, list(shape), dt)
        g.AQ = ds("AQ", [1024, TO], BF16)
        g.AF = ds("AFz", [1024, TC], F32)
        g.AG = ds("AG", [1024, TO], BF16)
        g.AI = ds("AI", [TC, 1024], BF16)
        g.BQ = ds("BQ", [768, TO], BF16)
        g.BK = ds("BK", [768, TC], BF16)
        g.BV = ds("BV", [3, 16, 128, 512], BF16)
        g.U = ds("U", [768, TO + 32], BF16)
        g.GATES = ds("GATES", [3 * D, TO], BF16)
        g.X1 = ds("X1", [TO, D], F32)
        g.FT = ds("FT", [12, 128 * 383], F32)
        g.GROWS = ds("GROWS", [2, D], F32)
        g.XL1 = ds("XL1", [TC, D], F32)
        g.taps = {}

        def tap(name, shape, dt=F32):
            return None
        g.tap = tap
        off = SB_BASE
        g.slots = []
        for i in range(NSLOT):
            t = nc.alloc_sbuf_tensor_at(f"wslot{i}", [128, 8192], BF16, offset=off)
            g.slots.append((t, Buf(f"slot{i}")))
            off += SLOT_BYTES
        g.RK = Region(off, CONST_BYTES); off += CONST_BYTES
        g.RA1 = Region(off, RA_BYTES // 2)
        g.RA2 = Region(off + RA_BYTES // 2, RA_BYTES // 2)
        g.RA = Region(off, RA_BYTES); off += RA_BYTES
        g.RB = Region(off, RB_BYTES); off += RB_BYTES
        g.RC = Region(off, SB_TOP - off)
        g.RBC = Region(g.RB.base, RB_BYTES + g.RC.size)
        g.banks = [(nc.alloc_psum_tensor(f"ps{i}", [128, 512], F32), Buf(f"bank{i}", excl=True)) for i in range(8)]
        g.bank_i = 0

        def bank():
            b = g.banks[g.bank_i]
            g.bank_i = (g.bank_i + 1) % 8
            return b
        g.bank = bank
        g.W = WStream(P, g.slots)
        g.cst = P.sb(g.RK, [128, NCONST], F32, "cst")
        g.identb = P.sb(g.RK, [128, 128], BF16, "identb")
        g.vec = P.sb(g.RK, [128, NVEC], F32, "vec")
        g.tm = P.sb(g.RK, [128, 4], F32, "tm")
        g.modc = P.sb(g.RK, [128, 4, 16], F32, "modc")
        g.AB = P.sb(g.RK, [128, 4, 16], F32, "AB")
        g.lbv = P.sb(g.RK, [128, 3, 8], F32, "lbv")
        g.kbuf = Buf("consts")
        P.dma("sp", g.cst[:], g.consts.ap(), writes=(g.kbuf,))
        P.op("dve", lambda e: e.tensor_copy(out=g.identb[:], in_=g.cst[:, C_IDENT:C_IDENT + 128]),
             reads=(g.kbuf,), writes=(g.kbuf,))
        g.ident = g.cst[:, C_IDENT:C_IDENT + 128]
        g.ones = g.cst[:, C_ONES:C_ONES + 128]

        for ps in passes:
            l = 1 if ps == "L1" else 0
            for k, t in g.Wf.items():
                setattr(g, k, LV(t, l))
            if ps in ("A", "L1"):
                plan_weights(g, 0, 8)
            plan_weights_p2(g)
            if ps in ("A", "L1"):
                plan_weights(g, 8, 24)
            plan_weights_p45(g)

        build_bias_tables(g)
        P.barrier()
        for ps in passes:
            l = 1 if ps == "L1" else 0
            g.l = l
            g.vecs = LV(g.vecs2, l)
            g.rows = LV(g.rows2, l)
            if ps == "A":
                g.xctx = LV(g.x3, None, (0, TC))
                g.xout = LV(g.XL1, None, (0, TO))
                tmi = 0
            elif ps == "B":
                g.xctx = LV(g.x3, None, (TO, TO + TC))
                g.xout = LV(g.XL1, None, (TO, TC))
                tmi = 1
            else:
                g.xctx = LV(g.XL1)
                g.xout = LV(g.xout_t)
                tmi = 1
            if "L1" not in passes and ps == passes[-1]:
                g.xout = LV(g.xout_t)
            P.dma("sp", g.vec[:], g.vecs.ap(), writes=(g.kbuf,))
            P.dma("sp", g.tm[:], g.tmask2.ap()[tmi], writes=(g.kbuf,), add_write=True)
            if ps in ("A", "L1"):
                phase0_setup(g)
                for _ in range(8):
                    phase0_tile(g)
                P.barrier()
            phase1(g)
            P.barrier()
            phase2(g)
            P.barrier()
            g.RA1.reset()
            g.YT = P.sb(g.RA1, [128, 16, TO], BF16, "YT")
            phase3_hgrn(g)
            while getattr(g, "p0_next", 24) < 24:
                phase0_tile(g)
            P.barrier()
            phase3_attn(g)
            P.barrier()
            phase3_conv(g)
            P.barrier()
            phase4a(g)
            P.barrier()
            phase4b(g)
            P.barrier()
            phase5(g)
            P.barrier()
        P.final_wait()
    return nc, g


def plan_weights(g, nt0, nt1):
    W = g.W
    wv = lambda w, c0, n: w.ap().rearrange("(kc p) n -> p kc n", p=128)[:, :, c0:c0 + n]
    for nt in range(nt0, nt1):
        W.add([(lambda t: slot_view(t, 16, 512), wv(g.w_ada, nt * 512, 512))])


def phase0_setup(g):
    P = g.P
    if not hasattr(g, "csb"):
        g.csb = P.sb(g.RK, [128, KC], F32, "csb")
        g.cact = P.sb(g.RK, [128, KC], BF16, "cact")
        g.stg0 = P.sb(g.RK, [1, 512], F32, "stg0")
        g.lb_in = P.sb(g.RK, [128, 2, 8], F32, "lb_in")
        g.b_c, g.b_stg0, g.b_lb, g.b_modc = Buf("c"), Buf("stg0"), Buf("lb"), Buf("modc")
    P.dma("sp", g.csb[:], g.c_b.ap(), writes=(g.b_c,))
    P.op("act", lambda e: e.activation(out=g.cact[:], in_=g.csb[:], func=AF.Silu), reads=(g.b_c,), writes=(g.b_c,))
    b_lb = g.b_lb
    P.dma("sp", g.lb_in[:], g.lbl.ap(), writes=(b_lb,))
    P.op("dve", lambda e: e.tensor_tensor(out=g.lbv[:, 2, :], in0=g.lb_in[:, 1, :], in1=g.lb_in[:, 0, :], op=ALU.subtract),
         reads=(b_lb,), writes=(b_lb,))
    P.op("act", lambda e: e.activation(out=g.lbv[:, 2, :], in_=g.lbv[:, 2, :], func=AF.Sigmoid), reads=(b_lb,), writes=(b_lb,))
    P.op("dve", lambda e: e.tensor_scalar(out=g.lbv[:, 0, :], in0=g.lbv[:, 2, :], scalar1=float(g.l), scalar2=None, op0=ALU.mult),
         reads=(b_lb,), writes=(b_lb,))
    P.op("dve", lambda e: e.tensor_scalar(out=g.lbv[:, 1, :], in0=g.lbv[:, 0, :], scalar1=-1.0, scalar2=1.0, op0=ALU.mult, op1=ALU.add),
         reads=(b_lb,), writes=(b_lb,))
    g.p0_next = 0


def phase0_tile(g):
    P = g.P
    nt = g.p0_next
    g.p0_next += 1
    stg, b_stg = g.stg0, g.b_stg0
    one11 = g.cst[0:1, C_ONES:C_ONES + 1]
    st, sb_ = g.W.get()
    pt, pb = g.bank()
    P.dma("sp", stg[:], g.rows.ap()[0:1, R_BADA + nt * 512:R_BADA + (nt + 1) * 512], writes=(b_stg,))
    P.mm([lambda e, kc=kc: e.matmul(pt[0:1, :], lhsT=g.cact[:, kc:kc + 1], rhs=slot_view(st, 16, 512)[:, kc, :],
                                    start=(kc == 0), stop=(kc == KC - 1)) for kc in range(KC)], reads=(g.b_c, sb_), writes=(pb,))
    P.op("dve", lambda e: e.tensor_tensor(out=stg[:], in0=pt[0:1, :], in1=stg[:], op=ALU.add), reads=(pb, b_stg), writes=(b_stg,))
    seg, j4 = nt // 4, nt % 4
    if seg in (2, 5):
        P.dma("sp", g.GROWS.ap()[(0 if seg == 2 else 1):(1 if seg == 2 else 2), j4 * 512:(j4 + 1) * 512], stg[:], reads=(b_stg,))
    else:
        si = {0: 0, 1: 1, 3: 2, 4: 3}[seg]
        pt2, pb2 = g.bank()
        P.mm([lambda e, q=q: e.matmul(pt2[:, q:q + 1], lhsT=stg[0:1, q * 128:(q + 1) * 128], rhs=one11, start=True, stop=True)
              for q in range(4)], reads=(b_stg, g.kbuf), writes=(pb2,))
        P.op("dve", lambda e: e.tensor_copy(out=g.modc[:, si, j4 * 4:(j4 + 1) * 4], in_=pt2[:, 0:4]), reads=(pb2,), writes=(g.b_modc,))
    if nt == 7:
        P.op("dve", lambda e: e.scalar_tensor_tensor(out=g.AB[:, 0, :], in0=g.modc[:, 1, :], scalar=1.0,
                                                     in1=g.vec[:, V_PRE:V_PRE + 16], op0=ALU.add, op1=ALU.mult),
             reads=(g.b_modc, g.kbuf), writes=(g.b_modc,))
        P.op("dve", lambda e: e.tensor_copy(out=g.AB[:, 1, :], in_=g.modc[:, 0, :]), reads=(g.b_modc,), writes=(g.b_modc,))
    if nt == 19:
        P.op("dve", lambda e: e.scalar_tensor_tensor(out=g.AB[:, 2, :], in0=g.modc[:, 3, :], scalar=1.0,
                                                     in1=g.vec[:, V_MLPPRE:V_MLPPRE + 16], op0=ALU.add, op1=ALU.mult),
             reads=(g.b_modc, g.kbuf), writes=(g.b_modc,))
        P.op("dve", lambda e: e.tensor_copy(out=g.AB[:, 3, :], in_=g.modc[:, 2, :]), reads=(g.b_modc,), writes=(g.b_modc,))


def phase0_hook(g):
    if getattr(g, "p0_next", 24) < 24:
        phase0_tile(g)


def norm_stats(g, xs, b_xs, ntile, junk, st, b_st):
    P = g.P
    for j in range(ntile):
        P.op("act", lambda e, j=j: e.activation(out=junk[:], in_=xs[:, j, :], func=AF.Square, accum_out=st[:, j:j + 1]),
             reads=(b_xs,), writes=(b_st,))
    P.op("dve", lambda e: e.tensor_scalar(out=st[:, ntile:2 * ntile], in0=st[:, 0:ntile], scalar1=1.0 / D, scalar2=EPS,
                                          op0=ALU.mult, op1=ALU.add), reads=(b_st,), writes=(b_st,))
    P.op("act", lambda e: e.activation(out=st[:, 2 * ntile:3 * ntile], in_=st[:, ntile:2 * ntile], func=AF.Sqrt),
         reads=(b_st,), writes=(b_st,))
    P.op("dve", lambda e: e.reciprocal(out=st[:, 3 * ntile:4 * ntile], in_=st[:, 2 * ntile:3 * ntile]),
         reads=(b_st,), writes=(b_st,))
    for j in range(ntile):
        eng = "act" if j % 2 == 0 else "dve"
        if eng == "act":
            P.op("act", lambda e, j=j: e.activation(out=xs[:, j, :], in_=xs[:, j, :], func=AF.Copy,
                                                    scale=st[:, 3 * ntile + j:3 * ntile + j + 1]),
                 reads=(b_st, b_xs), writes=(b_xs,))
        else:
            P.op("dve", lambda e, j=j: e.tensor_scalar(out=xs[:, j, :], in0=xs[:, j, :],
                                                       scalar1=st[:, 3 * ntile + j:3 * ntile + j + 1], scalar2=None, op0=ALU.mult),
                 reads=(b_st, b_xs), writes=(b_xs,))


def norm_xpose(g, xs, b_xs, ntile, AB_a, AB_b, dst_fn):
    P = g.P
    for fc in range(KC):
        pt, pb = g.bank()
        mms = [lambda e, j=j, fc=fc: e.transpose(pt[:, j * 128:(j + 1) * 128], xs[:, j, fc * 128:(fc + 1) * 128], g.ident)
               for j in range(ntile)]
        P.mm(mms, reads=(b_xs, g.kbuf), writes=(pb,))
        n = ntile * 128
        if fc % 2 == 0:
            P.op("act", lambda e, fc=fc: e.activation(out=dst_fn(fc), in_=pt[:, 0:n], func=AF.Identity,
                                                      scale=AB_a[:, fc:fc + 1], bias=AB_b[:, fc:fc + 1]),
                 reads=(pb, g.b_modc))
        else:
            P.op("dve", lambda e, fc=fc: e.tensor_scalar(out=dst_fn(fc), in0=pt[:, 0:n], scalar1=AB_a[:, fc:fc + 1],
                                                         scalar2=AB_b[:, fc:fc + 1], op0=ALU.mult, op1=ALU.add),
                 reads=(pb, g.b_modc))


def norm_transpose(g, xs, b_xs, ntile, AB_a, AB_b, dst_fn, junk, st, b_st):
    norm_stats(g, xs, b_xs, ntile, junk, st, b_st)
    norm_xpose(g, xs, b_xs, ntile, AB_a, AB_b, dst_fn)


def phase1(g):
    P = g.P
    g.RA.reset()
    g.hT = P.sb(g.RA, [128, KC, TC], BF16, "hT")
    R = g.RBC
    R.reset()
    NB = 3
    xsN = [P.sb(R, [128, 2, D], F32, f"xs{i}") for i in range(NB)]
    bx = [Buf(f"xs{i}") for i in range(NB)]
    stN = [P.sb(R, [128, 8], F32, f"st{i}") for i in range(NB)]
    bst = [Buf(f"st{i}") for i in range(NB)]
    junk = P.sb(R, [128, D], BF16, "junk")
    xv = g.xctx.ap().rearrange("(t p) d -> p t d", p=128)

    def stage_a(gi):
        xs, b_xs = xsN[gi % NB], bx[gi % NB]
        P.dma("sp", xs[:], xv[:, gi * 2:gi * 2 + 2, :], writes=(b_xs,))
        norm_stats(g, xs, b_xs, 2, junk, stN[gi % NB], bst[gi % NB])

    def stage_b(gi):
        norm_xpose(g, xsN[gi % NB], bx[gi % NB], 2, g.AB[:, 0, :], g.AB[:, 1, :],
                   lambda fc, gi=gi: g.hT[:, fc, gi * 256:(gi + 1) * 256])
    stage_a(0)
    for gi in range(8):
        if gi + 1 < 8:
            stage_a(gi + 1)
        stage_b(gi)


def plan_weights_p2(g):
    W = g.W
    wv = lambda w, c0, n: w.ap().rearrange("(kc p) n -> p kc n", p=128)[:, :, c0:c0 + n]
    full = lambda c0, n: [(lambda t, n=n: slot_view(t, 16, n), wv(g.w_in, c0, n))]
    for c0 in (0, 512):
        W.add(full(c0, 512))
    for c0 in (1024, 1536):
        W.add(full(c0, 512))
    for c0 in (3072, 3584):
        W.add(full(c0, 512))
    for c0 in (2048, 2560):
        W.add(full(c0, 512))
    W.add(full(4096, 512)); W.add(full(4608, 256))
    W.add(full(4864, 512)); W.add(full(5376, 256))
    for gi in range(3):
        W.add(full(5632 + gi * 256, 256))
    for i in range(3):
        W.add([(lambda t: slot_view(t, 16, 256, c0=0), wv(g.w_in, 6400 + i * 256, 256)),
               (lambda t: slot_view(t, 16, 256, c0=256), wv(g.w_in, 7168 + i * 256, 256))])
    for i in range(12):
        W.add([(lambda t: slot_view(t, 16, 512), wv(g.w_gate, i * 512, 512))])


def phase2(g):
    P = g.P
    R = g.RBC
    R.reset()
    hT = g.hT
    NSTG = 8
    stg = [(P.sb(R, [128, 512], F32, f"stg{i}"), Buf(f"stg{i}")) for i in range(NSTG)]
    tmp = [(P.sb(R, [128, 512], F32, f"tmp{i}"), Buf(f"tmp{i}")) for i in range(4)]
    onesv = P.sb(R, [128, 256], F32, "onesv")
    b_ones = Buf("onesv")
    P.op("dve", lambda e: e.memset(onesv[:], 1.0), writes=(b_ones,))
    st = {"i": 0, "t": 0, "e": 0}

    def nstg():
        s = stg[st["i"] % NSTG]
        st["i"] += 1
        return s

    def ntmp():
        s = tmp[st["t"] % 4]
        st["t"] += 1
        return s

    def eng2():
        st["e"] += 1
        return "act" if st["e"] % 2 == 0 else "dve"

    OWN = [(1024, 512), (1536, 512)]
    CTX = [(0, 512), (512, 512), (1024, 512), (1536, 512)]

    def fm(slot, c0, ncols, toks, evac):
        stt, sbuf = slot
        sv = slot_view(stt, 16, 512)
        for j in range(ncols // 128):
            for (t0, nt) in toks:
                pt, pb = g.bank()
                mms = [lambda e, kc=kc, j=j, t0=t0, nt=nt: e.matmul(
                    pt[:, 0:nt], lhsT=sv[:, kc, c0 + j * 128:c0 + (j + 1) * 128], rhs=hT[:, kc, t0:t0 + nt],
                    start=(kc == 0), stop=(kc == KC - 1)) for kc in range(KC)]
                P.mm(mms, reads=(sbuf,), writes=(pb,))
                evac(pt, pb, j, t0, nt)

    def spill(dram_ap, src_ap, b_src):
        P.dma("sp", dram_ap, src_ap, reads=(b_src,))

    def ev_aq(f0):
        def ev(pt, pb, j, t0, nt):
            s, sb_ = nstg()
            sv = s.bitcast(BF16)
            P.op("act", lambda e: e.activation(out=sv[:, 0:nt], in_=pt[:, 0:nt], func=AF.Silu), reads=(pb,), writes=(sb_,))
            spill(g.AQ.ap()[f0 + j * 128:f0 + (j + 1) * 128, t0 - 1024:t0 - 1024 + nt], sv[:, 0:nt], sb_)
        return ev
    for i in range(2):
        fm(g.W.get(), 0, 512, OWN, ev_aq(i * 512))

    def ev_af(f0):
        def ev(pt, pb, j, t0, nt):
            s, sb_ = nstg()
            en = eng2()
            if en == "act":
                P.op("act", lambda e: e.copy(out=s[:, 0:nt], in_=pt[:, 0:nt]), reads=(pb,), writes=(sb_,))
            else:
                P.op("dve", lambda e: e.tensor_copy(out=s[:, 0:nt], in_=pt[:, 0:nt]), reads=(pb,), writes=(sb_,))
            spill(g.AF.ap()[f0 + j * 128:f0 + (j + 1) * 128, t0:t0 + nt], s[:, 0:nt], sb_)
        return ev
    for i in range(2):
        fm(g.W.get(), 0, 512, CTX, ev_af(i * 512))

    def ev_ag(f0):
        def ev(pt, pb, j, t0, nt):
            t_, tb = ntmp()
            s, sb_ = nstg()
            sv = s.bitcast(BF16)
            P.op("act", lambda e: e.activation(out=t_[:, 0:nt], in_=pt[:, 0:nt], func=AF.Silu), reads=(pb,), writes=(tb,))
            P.op("dve", lambda e: e.tensor_scalar(out=sv[:, 0:nt], in0=t_[:, 0:nt], scalar1=g.vec[:, V_NORMW:V_NORMW + 1],
                                                  scalar2=None, op0=ALU.mult), reads=(tb, g.kbuf), writes=(sb_,))
            spill(g.AG.ap()[f0 + j * 128:f0 + (j + 1) * 128, t0 - 1024:t0 - 1024 + nt], sv[:, 0:nt], sb_)
        return ev
    for i in range(2):
        fm(g.W.get(), 0, 512, OWN, ev_ag(i * 512))

    for i in range(2):
        stt, sbuf = g.W.get()
        sv = slot_view(stt, 16, 512)
        for ti in range(16):
            pt, pb = g.bank()
            mms = [lambda e, kc=kc, ti=ti: e.matmul(pt[:, :], lhsT=hT[:, kc, ti * 128:(ti + 1) * 128], rhs=sv[:, kc, :],
                                                   start=(kc == 0), stop=(kc == KC - 1)) for kc in range(KC)]
            P.mm(mms, reads=(sbuf,), writes=(pb,))
            s, sb_ = nstg()
            sv2 = s.bitcast(BF16)
            mc = 0 if ti < 8 else 1
            en = eng2()
            if en == "act":
                P.op("act", lambda e: e.activation(out=sv2[:, 0:512], in_=pt[:, :], func=AF.Copy, scale=g.tm[:, mc:mc + 1]),
                     reads=(pb, g.kbuf), writes=(sb_,))
            else:
                P.op("dve", lambda e: e.tensor_scalar(out=sv2[:, 0:512], in0=pt[:, :], scalar1=g.tm[:, mc:mc + 1], scalar2=None,
                                                      op0=ALU.mult), reads=(pb, g.kbuf), writes=(sb_,))
            spill(g.AI.ap()[ti * 128:(ti + 1) * 128, i * 512:(i + 1) * 512], sv2[:, 0:512], sb_)

    def ev_b(dst, f0, scale, own):
        def ev(pt, pb, j, t0, nt):
            s, sb_ = nstg()
            sv = s.bitcast(BF16)
            en = eng2()
            if en == "act":
                P.op("act", lambda e: e.activation(out=sv[:, 0:nt], in_=pt[:, 0:nt], func=AF.Copy, scale=scale), reads=(pb,), writes=(sb_,))
            else:
                P.op("dve", lambda e: e.tensor_scalar(out=sv[:, 0:nt], in0=pt[:, 0:nt], scalar1=scale, scalar2=None, op0=ALU.mult),
                     reads=(pb,), writes=(sb_,))
            tt = t0 - 1024 if own else t0
            spill(dst.ap()[f0 + j * 128:f0 + (j + 1) * 128, tt:tt + nt], sv[:, 0:nt], sb_)
        return ev
    fm(g.W.get(), 0, 512, OWN, ev_b(g.BQ, 0, 0.125, True))
    fm(g.W.get(), 0, 256, OWN, ev_b(g.BQ, 512, 0.125, True))
    fm(g.W.get(), 0, 512, CTX, ev_b(g.BK, 0, 1.0, False))
    fm(g.W.get(), 0, 256, CTX, ev_b(g.BK, 512, 1.0, False))

    for gi, dil in enumerate((1, 4, 16)):
        stt, sbuf = g.W.get()
        sv = slot_view(stt, 16, 256)
        for bi in range(16):
            if gi == 0:
                start, mc = bi * 128, (0 if bi < 8 else 1)
            elif gi == 1:
                r, n = bi // 4, bi % 4
                start, mc = n * 512 + r, (0 if n < 2 else 1)
            else:
                start, mc = bi, 2
            pt, pb = g.bank()
            mms = [lambda e, kc=kc, start=start, dil=dil: e.matmul(
                pt[:, 0:256], lhsT=hT[:, kc, start:start + 127 * dil + 1:dil], rhs=sv[:, kc, :],
                start=(kc == 0), stop=(kc == KC - 1)) for kc in range(KC)]
            P.mm(mms, reads=(sbuf,), writes=(pb,))
            s, sb_ = nstg()
            sv2 = s.bitcast(BF16)
            vdst = bass.AP(sv2, 0, [[1024, 128], [256, 2], [192, 2], [1, 64]])
            mdst = bass.AP(sv2, 64, [[1024, 128], [256, 2], [64, 2], [1, 64]])
            vsrc = bass.AP(pt, 0, [[512, 128], [128, 2], [64, 2], [1, 64]])
            osrc = bass.AP(onesv, 0, [[256, 128], [128, 2], [64, 2], [1, 64]])
            P.op("dve", lambda e: e.tensor_scalar(out=vdst, in0=vsrc, scalar1=g.tm[:, mc:mc + 1], scalar2=None, op0=ALU.mult),
                 reads=(pb, g.kbuf), writes=(sb_,))
            P.op("dve", lambda e: e.tensor_scalar(out=mdst, in0=osrc, scalar1=g.tm[:, mc:mc + 1], scalar2=None, op0=ALU.mult),
                 reads=(b_ones, g.kbuf, sb_), writes=(sb_,))
            spill(g.BV.ap()[gi, bi], sv2[:, 0:512], sb_)

    CT = [(994, 30), (1024, 512), (1536, 512)]
    for i in range(3):
        slot = g.W.get()
        stt, sbuf = slot
        sv = slot_view(stt, 16, 512)
        for j in range(2):
            for (t0, nt) in CT:
                pg, pgb = g.bank()
                P.mm([lambda e, kc=kc: e.matmul(pg[:, 0:nt], lhsT=sv[:, kc, 256 + j * 128:256 + (j + 1) * 128], rhs=hT[:, kc, t0:t0 + nt],
                                               start=(kc == 0), stop=(kc == KC - 1)) for kc in range(KC)], reads=(sbuf,), writes=(pgb,))
                pa, pab = g.bank()
                P.mm([lambda e, kc=kc: e.matmul(pa[:, 0:nt], lhsT=sv[:, kc, j * 128:(j + 1) * 128], rhs=hT[:, kc, t0:t0 + nt],
                                               start=(kc == 0), stop=(kc == KC - 1)) for kc in range(KC)], reads=(sbuf,), writes=(pab,))
                t_, tb = ntmp()
                s, sb_ = nstg()
                sv2 = s.bitcast(BF16)
                P.op("act", lambda e: e.activation(out=t_[:, 0:nt], in_=pg[:, 0:nt], func=AF.Sigmoid), reads=(pgb,), writes=(tb,))
                if nt == 30:
                    P.op("dve", lambda e: e.scalar_tensor_tensor(out=sv2[:, 0:nt], in0=pa[:, 0:nt], scalar=g.tm[:, 3:4], in1=t_[:, 0:nt],
                                                                 op0=ALU.mult, op1=ALU.mult), reads=(pab, tb, g.kbuf), writes=(sb_,))
                    c0 = 2
                else:
                    P.op("dve", lambda e: e.tensor_tensor(out=sv2[:, 0:nt], in0=pa[:, 0:nt], in1=t_[:, 0:nt], op=ALU.mult),
                         reads=(pab, tb), writes=(sb_,))
                    c0 = 32 + t0 - 1024
                f0 = i * 256 + j * 128
                spill(g.U.ap()[f0:f0 + 128, c0:c0 + nt], sv2[:, 0:nt], sb_)

    for i in range(12):
        def ev(pt, pb, j, t0, nt, i=i):
            s, sb_ = nstg()
            sv = s.bitcast(BF16)
            fcol = V_BGATE + i * 4 + j
            P.op("act", lambda e: e.activation(out=sv[:, 0:nt], in_=pt[:, 0:nt], func=AF.Sigmoid, bias=g.vec[:, fcol:fcol + 1]),
                 reads=(pb, g.kbuf), writes=(sb_,))
            f0 = i * 512 + j * 128
            spill(g.GATES.ap()[f0:f0 + 128, t0 - 1024:t0 - 1024 + nt], sv[:, 0:nt], sb_)
        fm(g.W.get(), 0, 512, OWN, ev)

    for name, src in (("AQ", g.AQ), ("AF", g.AF), ("AG", g.AG), ("AI", g.AI), ("BQ", g.BQ), ("BK", g.BK), ("BV", g.BV),
                      ("U", g.U), ("GATES", g.GATES)):
        t = g.tap(name, src.shape, src.dtype)
        if t is not None:
            P.barrier()
            P.dma("sp", t.ap(), src.ap())


def phase3_hgrn(g):
    P = g.P
    g.RBC.reset(); g.RA2.reset()
    RR = [g.RBC, g.RA2]
    sb = lambda shape, dt, name: P.sb(RR, shape, dt, name)
    YT = g.YT
    cst = g.cst
    S = sb([128, 8, 128], F32, "S"); Sbf = sb([128, 8, 128], BF16, "Sbf")
    vseg = sb([64, 8, 1024], BF16, "vseg")
    QpT = sb([128, 8, 512], BF16, "QpT"); KpT = sb([128, 8, 512], BF16, "KpT")
    Q2T = sb([128, 8, 512], BF16, "Q2T"); bQ2 = Buf("Q2")
    Ktok = sb([64, 8, 8, 128], BF16, "Ktok")
    attnT = sb([64, 8, 8, 64], BF16, "attnT")
    gseg = sb([128, 8, 512], BF16, "gseg")
    dec = sb([128, 8, 8], F32, "dec")
    onesb = sb([128, 128], BF16, "onesb")
    zt = [sb([128, 512], F32, f"zt{i}") for i in range(2)]
    qh = [sb([128, 512], BF16, f"qh{i}") for i in range(2)]
    tf = sb([128, 512], F32, "tf"); tl = sb([128, 512], F32, "tl"); tk = sb([128, 512], F32, "tk")
    tA = sb([128, 512], F32, "tA"); tB = sb([128, 512], F32, "tB"); tC = sb([128, 512], F32, "tC")
    KppT = sb([128, 512], BF16, "KppT")
    rr, to = tk, tf
    osq2 = [KppT, sb([128, 512], BF16, "osqB")]
    bS, bSbf, bV, bQ, bK, bKt, bAt, bG, bDec, bOnes = (Buf(n) for n in "S Sbf V Q K Kt At G Dec Ones".split())
    bz = [Buf("z0"), Buf("z1")]; bq = [Buf("q0"), Buf("q1")]
    btf, btl, btk, btA, btB, btC, bKpp = (Buf(n) for n in "tf tl tk tA tB tC Kpp".split())
    brr, bto = btk, btf
    bosq2 = [bKpp, Buf("osqB")]
    resetm = cst[:, C_RESET:C_RESET + 512]
    caus = bass.AP(cst, C_CAUS, [[NCONST, 64], [0, 8], [1, 64]])
    P.op("dve", lambda e: e.memset(S[:], 0.0), writes=(bS,))
    P.op("dve", lambda e: e.memset(Sbf[:], 0.0), writes=(bSbf,))
    P.op("dve", lambda e: e.memset(onesb[:], 1.0), writes=(bOnes,))
    v3 = lambda t: t[:].rearrange("p (c s) -> p c s", s=64)
    hcount = 0
    for sg in range(4):
        own = sg >= 2
        o0 = (sg - 2) * 512
        P.dma("sp", vseg[:], g.AI.ap()[sg * 512:(sg + 1) * 512, :].rearrange("(c p) n -> p c n", p=64), writes=(bV,))
        if own:
            P.dma("sp", gseg[:], g.AG.ap().rearrange("(h v) t -> v h t", v=128)[:, :, o0:o0 + 512], writes=(bG,))
        for h in range(8):
            z, b_z = zt[hcount % 2], bz[hcount % 2]
            q, b_q = qh[hcount % 2], bq[hcount % 2]
            hcount += 1
            P.dma("sp", z[:], g.AF.ap()[h * 128:(h + 1) * 128, sg * 512:(sg + 1) * 512], writes=(b_z,))
            if own:
                P.dma("sp", q[:], g.AQ.ap()[h * 128:(h + 1) * 128, o0:o0 + 512], writes=(b_q,))
            lb = g.lbv[:, 0, h:h + 1]; oml = g.lbv[:, 1, h:h + 1]
            P.op("act", lambda e: e.activation(out=tf[:], in_=z[:], func=AF.Exp, scale=-1.0), reads=(b_z,), writes=(btf,))
            P.op("dve", lambda e: e.tensor_scalar(out=tf[:], in0=tf[:], scalar1=1.0, scalar2=None, op0=ALU.add), reads=(btf,), writes=(btf,))
            P.op("dve", lambda e: e.reciprocal(out=tf[:], in_=tf[:]), reads=(btf,), writes=(btf,))
            P.op("dve", lambda e: e.tensor_scalar(out=tf[:], in0=tf[:], scalar1=oml, scalar2=lb, op0=ALU.mult, op1=ALU.add),
                 reads=(btf, g.b_lb), writes=(btf,))
            P.op("act", lambda e: e.activation(out=tl[:], in_=tf[:], func=AF.Ln), reads=(btf,), writes=(btl,))
            P.op("dve", lambda e: e.tensor_scalar(out=tk[:], in0=tf[:], scalar1=-1.0, scalar2=1.0, op0=ALU.mult, op1=ALU.add),
                 reads=(btf,), writes=(btk,))
            P.op("dve", lambda e: e.tensor_tensor_scan(out=tl[:], data0=resetm, data1=tl[:], initial=0.0, op0=ALU.mult, op1=ALU.add),
                 reads=(btl, g.kbuf), writes=(btl,))
            Bl = v3(tl)[:, :, 63:64]
            P.op("dve", lambda e: e.tensor_tensor(out=v3(tA), in0=Bl.to_broadcast([128, 8, 64]), in1=v3(tl), op=ALU.subtract),
                 reads=(btl,), writes=(btA,))
            P.op("act", lambda e: e.activation(out=tA[:], in_=tA[:], func=AF.Exp), reads=(btA,), writes=(btA,))
            P.op("dve", lambda e: e.tensor_tensor(out=KppT[:], in0=tk[:], in1=tA[:], op=ALU.mult), reads=(btk, btA), writes=(bKpp,))
            P.op("act", lambda e: e.activation(out=dec[:, :, h], in_=v3(tl)[:, :, 63], func=AF.Exp), reads=(btl,), writes=(bDec,))
            if own:
                Bm = v3(tl)[:, :, 31:32]
                P.op("dve", lambda e: e.tensor_tensor(out=v3(tB), in0=v3(tl), in1=Bm.to_broadcast([128, 8, 64]), op=ALU.subtract),
                     reads=(btl,), writes=(btB,))
                P.op("act", lambda e: e.activation(out=tC[:], in_=tB[:], func=AF.Exp), reads=(btB,), writes=(btC,))
                P.op("act", lambda e: e.activation(out=tB[:], in_=tB[:], func=AF.Exp, scale=-1.0), reads=(btB,), writes=(btB,))
                P.op("dve", lambda e: e.tensor_tensor(out=QpT[:, h, :], in0=q[:], in1=tC[:], op=ALU.mult), reads=(b_q, btC), writes=(bQ,))
                P.op("dve", lambda e: e.tensor_tensor(out=KpT[:, h, :], in0=tk[:], in1=tB[:], op=ALU.mult), reads=(btk, btB), writes=(bK,))
                P.op("act", lambda e: e.activation(out=tA[:], in_=tl[:], func=AF.Exp), reads=(btl,), writes=(btA,))
                P.op("dve", lambda e: e.tensor_tensor(out=Q2T[:, h, :], in0=q[:], in1=tA[:], op=ALU.mult), reads=(b_q, btA), writes=(bQ2,))
            pt, pb = g.bank()
            ptb = pt.bitcast(BF16)
            P.mm([lambda e, c=c: e.transpose(ptb[0:64, c * 128:(c + 1) * 128], KppT[:, c * 64:(c + 1) * 64], g.identb[:])
                  for c in range(8)], reads=(bKpp, g.kbuf), writes=(pb,))
            P.op("act", lambda e: e.copy(out=Ktok[:, h, :, :].rearrange("p c k -> p (c k)"), in_=ptb[0:64, 0:1024]),
                 reads=(pb,), writes=(bKt,))
            if h % 2 == 1:
                phase0_hook(g)
            if own:
                pt2, pb2 = g.bank()
                P.mm([lambda e, c=c: e.matmul(pt2[0:64, c * 64:(c + 1) * 64], lhsT=KpT[:, h, c * 64:(c + 1) * 64],
                                              rhs=QpT[:, h, c * 64:(c + 1) * 64], start=True, stop=True) for c in range(8)],
                     reads=(bK, bQ), writes=(pb2,))
                P.op("dve", lambda e: e.tensor_tensor(out=attnT[:, h, :, :], in0=pt2[0:64, :].rearrange("p (c t) -> p c t", t=64),
                                                      in1=caus, op=ALU.mult), reads=(pb2, g.kbuf), writes=(bAt,))
        pending = None
        for c in range(8):
            last = (sg == 3 and c == 7)
            if not last:
                pdA, pdAb = g.bank()
                pdB, pdBb = g.bank()
                mms = []
                for h in range(8):
                    pd = pdA if h < 4 else pdB
                    hh = h % 4
                    mms.append(lambda e, h=h, pd=pd, hh=hh: e.matmul(pd[:, hh * 128:(hh + 1) * 128], lhsT=Ktok[0:64, h, c, :],
                                                                       rhs=vseg[0:64, c, h * 128:(h + 1) * 128], start=True, stop=True))
                P.mm(mms, reads=(bKt, bV), writes=(pdAb, pdBb))
            if own:
                po, pob = g.bank()
                mms = []
                for h in range(8):
                    mms.append(lambda e, h=h: e.matmul(po[:, h * 64:(h + 1) * 64], lhsT=vseg[0:64, c, h * 128:(h + 1) * 128],
                                                       rhs=attnT[0:64, h, c, :], start=True, stop=False))
                    mms.append(lambda e, h=h: e.matmul(po[:, h * 64:(h + 1) * 64], lhsT=Sbf[:, h, :],
                                                       rhs=Q2T[:, h, c * 64:(c + 1) * 64], start=False, stop=True))
                P.mm(mms, reads=(bV, bAt, bSbf, bQ2), writes=(pob,))
                oq, boq = osq2[c % 2], bosq2[c % 2]
                P.op("act", lambda e: e.activation(out=oq[:], in_=po[:, :], func=AF.Square), reads=(pob,), writes=(boq,))
            if not last:
                P.op("dve", lambda e: e.tensor_tensor(out=S[:], in0=S[:], in1=dec[:, c, :].unsqueeze(2).to_broadcast([128, 8, 128]),
                                                      op=ALU.mult), reads=(bS, bDec), writes=(bS,))
                P.op("dve", lambda e: e.tensor_tensor(out=S[:, 0:4, :], in0=S[:, 0:4, :], in1=pdA[:, :].rearrange("p (h v) -> p h v", v=128),
                                                      op=ALU.add), reads=(bS, pdAb), writes=(bS,))
                P.op("dve", lambda e: e.tensor_tensor(out=S[:, 4:8, :], in0=S[:, 4:8, :], in1=pdB[:, :].rearrange("p (h v) -> p h v", v=128),
                                                      op=ALU.add), reads=(bS, pdBb), writes=(bS,))
                P.op("act", lambda e: e.copy(out=Sbf[:], in_=S[:]), reads=(bS,), writes=(bSbf,))

            def tail(po, pob, oq, boq, c):
                ps, psb = g.bank()
                P.mm([lambda e: e.matmul(ps[:, :], lhsT=onesb[:], rhs=oq[:], start=True, stop=True)], reads=(boq, bOnes), writes=(psb,))
                P.op("act", lambda e: e.activation(out=rr[:], in_=ps[:, :], func=AF.Sqrt, scale=1.0 / 128, bias=EPS), reads=(psb,), writes=(brr,))
                P.op("dve", lambda e: e.reciprocal(out=rr[:], in_=rr[:]), reads=(brr,), writes=(brr,))
                P.op("dve", lambda e: e.tensor_tensor(out=to[:], in0=po[:, :], in1=rr[:], op=ALU.mult), reads=(pob, brr), writes=(bto,))
                tok0 = o0 + c * 64
                P.op("dve", lambda e: e.tensor_tensor(out=YT[:, 0:8, tok0:tok0 + 64], in0=to[:].rearrange("p (h t) -> p h t", t=64),
                                                      in1=gseg[:, :, c * 64:(c + 1) * 64], op=ALU.mult), reads=(bto, bG))
            if pending is not None:
                tail(*pending)
                pending = None
            if own:
                pending = (po, pob, oq, boq, c)
        if pending is not None:
            tail(*pending)
            pending = None
    t = g.tap("YA", [1024, TO], BF16)
    if t is not None:
        P.barrier()
        P.dma("sp", t.ap().rearrange("(h v) t -> v h t", v=128), YT[:, 0:8, :])


def build_bias_tables(g):
    P = g.P
    R = g.RBC
    R.reset()
    relb = P.sb(R, [32, 12], F32, "relb")
    relrep = P.sb(R, [32, 12, 128], F32, "relrep")
    oh = P.sb(R, [32, 3, 383], F32, "oh")
    fr = [P.sb(R, [128, 383], F32, f"fr{i}") for i in range(2)]
    bfr = [Buf("fr0"), Buf("fr1")]
    b_in = Buf("relb")
    P.dma("sp", relb[:], g.rel_bias.ap(), writes=(b_in,))
    P.dma("sp", oh[:], g.onehot.ap().rearrange("g b j -> b g j"), writes=(b_in,), add_write=True)
    P.op("dve", lambda e: e.tensor_copy(out=relrep[:], in_=relb[:].unsqueeze(2).to_broadcast([32, 12, 128])), reads=(b_in,), writes=(b_in,))
    for h in range(12):
        gi = h // 4
        pt, pb = g.bank()
        P.mm([lambda e: e.matmul(pt[:, 0:383], lhsT=relrep[:, h, :], rhs=oh[:, gi, :], start=True, stop=True)], reads=(b_in,), writes=(pb,))
        f, bf_ = fr[h % 2], bfr[h % 2]
        P.op("act", lambda e: e.activation(out=f[:], in_=pt[:, 0:383], func=AF.Exp), reads=(pb,), writes=(bf_,))
        P.op("dve", lambda e: e.tensor_tensor(out=f[:], in0=f[:], in1=g.cst[:, C_FVALID:C_FVALID + 383], op=ALU.mult),
             reads=(bf_, g.kbuf), writes=(bf_,))
        P.dma("sp", g.FT.ap()[h].rearrange("(p j) -> p j", j=383), f[:], reads=(bf_,))


def phase3_attn(g):
    P = g.P
    g.RBC.reset(); g.RA2.reset()
    RR = [g.RBC, g.RA2]
    sb = lambda shape, dt, name: P.sb(RR, shape, dt, name)
    YT = g.YT
    EB = sb([128, 12, 2, 128], F32, "EB")
    QT = sb([64, 4, TO], BF16, "QT"); KT = sb([64, 4, TC], BF16, "KT")
    Vg = sb([128, 16, 512], BF16, "Vg")
    UZ = sb([128, 4, TO], F32, "UZ")
    pe = [sb([128, 512], F32, f"pe{i}") for i in range(2)]
    pT = [sb([128, 512], BF16, f"pT{i}") for i in range(2)]
    rz = sb([128, 512], F32, "rz")
    bEB, bQ, bK, bV, bUZ, brz = (Buf(n) for n in "EB Q K V UZ rz".split())
    bpe = [Buf("pe0"), Buf("pe1")]; bpT = [Buf("pT0"), Buf("pT1")]
    first = True
    for h in range(12):
        for ty in range(2):
            src = bass.AP(g.FT, h * 128 * 383 + (127 if ty == 1 else 255), [[382, 128], [1, 128]])
            P.dma("sp", EB[:, h, ty, :], src, writes=(bEB,), add_write=not first)
            first = False
    cnt = 0
    pendingB = None
    for gi, dil in enumerate((1, 4, 16)):
        P.dma("sp", QT[:], g.BQ.ap()[gi * 256:(gi + 1) * 256, :].rearrange("(h d) t -> d h t", d=64), writes=(bQ,))
        P.dma("sp", KT[:], g.BK.ap()[gi * 256:(gi + 1) * 256, :].rearrange("(h d) t -> d h t", d=64), writes=(bK,))
        P.dma("sp", Vg[:], g.BV.ap()[gi].rearrange("b p n -> p b n"), writes=(bV,))
        for hh in range(4):
            h = gi * 4 + hh
            if gi < 2:
                nbatch = 4
            else:
                nbatch = 2
            for bt in range(nbatch):
                pt, pb = g.bank()
                po, pob = g.bank()
                e_, be_ = pe[cnt % 2], bpe[cnt % 2]
                p_, bp_ = pT[cnt % 2], bpT[cnt % 2]
                cnt += 1
                mm1, mm2 = [], []
                if gi == 0:
                    for j in range(2):
                        qb = bt * 2 + j
                        qs = QT[:, hh, qb * 128:(qb + 1) * 128]
                        for ty in range(2):
                            k0 = 896 + qb * 128 + ty * 128
                            reg = (j * 2 + ty) * 128
                            mm1.append(lambda e, k0=k0, reg=reg, qs=qs: e.matmul(pt[:, reg:reg + 128], lhsT=KT[:, hh, k0:k0 + 128], rhs=qs,
                                                                                 start=True, stop=True))
                            blk = 7 + qb + ty
                            mm2.append(lambda e, blk=blk, reg=reg, j=j, ty=ty, po=po, p_=p_, hh=hh: e.matmul(
                                po[:, j * 128:(j + 1) * 128], lhsT=Vg[:, blk, hh * 128:(hh + 1) * 128], rhs=p_[:, reg:reg + 128],
                                start=(ty == 0), stop=(ty == 1)))
                    eb_in = bass.AP(EB, h * 256, [[12 * 256, 128], [0, 2], [1, 256]])
                    uz_view = UZ[:, hh, bt * 256:(bt + 1) * 256]
                    ncol = 512
                elif gi == 1:
                    r = bt
                    for j in range(2):
                        n = 2 + j
                        qs = QT[:, hh, j * 512 + r:j * 512 + r + 509:4]
                        for ty in range(2):
                            m = n - 1 + ty
                            k0 = m * 512 + r
                            reg = (j * 2 + ty) * 128
                            mm1.append(lambda e, k0=k0, reg=reg, qs=qs: e.matmul(pt[:, reg:reg + 128], lhsT=KT[:, hh, k0:k0 + 509:4], rhs=qs,
                                                                                 start=True, stop=True))
                            blk = r * 4 + m
                            mm2.append(lambda e, blk=blk, reg=reg, j=j, ty=ty, po=po, p_=p_, hh=hh: e.matmul(
                                po[:, j * 128:(j + 1) * 128], lhsT=Vg[:, blk, hh * 128:(hh + 1) * 128], rhs=p_[:, reg:reg + 128],
                                start=(ty == 0), stop=(ty == 1)))
                    eb_in = bass.AP(EB, h * 256, [[12 * 256, 128], [0, 2], [1, 256]])
                    uz_view = UZ[:, hh, r:TO:4]
                    ncol = 512
                else:
                    for rr_ in range(8):
                        r = bt * 8 + rr_
                        reg = rr_ * 64
                        mm1.append(lambda e, r=r, reg=reg: e.matmul(pt[:, reg:reg + 64], lhsT=KT[:, hh, r:TC:16], rhs=QT[:, hh, r:TO:16],
                                                                    start=True, stop=True))
                        mm2.append(lambda e, r=r, reg=reg, po=po, p_=p_, hh=hh: e.matmul(po[:, reg:reg + 64], lhsT=Vg[:, r, hh * 128:(hh + 1) * 128],
                                                                    rhs=p_[:, reg:reg + 64], start=True, stop=True))
                    eb_in = bass.AP(EB, h * 256 + 128 + 64, [[12 * 256, 128], [0, 8], [1, 64]])
                    uz_view = bass.AP(UZ, hh * TO + bt * 8, [[4 * TO, 128], [1, 8], [16, 64]])
                    ncol = 512
                P.mm(mm1, reads=(bQ, bK), writes=(pb,))
                P.op("act", lambda e: e.activation(out=e_[:, 0:ncol], in_=pt[:, 0:ncol], func=AF.Exp), reads=(pb,), writes=(be_,))
                if gi < 2:
                    e_v = e_[:, 0:512].rearrange("p (j c) -> p j c", j=2)
                    p_v = p_[:, 0:512].rearrange("p (j c) -> p j c", j=2)
                else:
                    e_v = e_[:, 0:512].rearrange("p (j c) -> p j c", j=8)
                    p_v = p_[:, 0:512].rearrange("p (j c) -> p j c", j=8)
                P.op("dve", lambda e: e.tensor_tensor(out=p_v, in0=e_v, in1=eb_in, op=ALU.mult), reads=(be_, bEB), writes=(bp_,))
                nout = 256 if gi < 2 else 512

                def make_stB(mm2, po, pob, bp_, uz_view, gi, nout):
                    def stB():
                        P.mm(mm2, reads=(bV, bp_), writes=(pob,))
                        if gi == 0:
                            P.op("act", lambda e: e.copy(out=uz_view, in_=po[:, 0:nout]), reads=(pob,), writes=(bUZ,))
                        elif gi == 1:
                            P.op("dve", lambda e: e.tensor_tensor(out=uz_view, in0=po[:, 0:nout], in1=uz_view, op=ALU.add),
                                 reads=(pob, bUZ), writes=(bUZ,))
                        else:
                            P.op("dve", lambda e: e.tensor_tensor(out=uz_view, in0=po[:, 0:nout].rearrange("p (r l) -> p r l", r=8),
                                                                  in1=uz_view, op=ALU.add), reads=(pob, bUZ), writes=(bUZ,))
                    return stB
                if pendingB is not None:
                    pendingB()
                pendingB = make_stB(mm2, po, pob, bp_, uz_view, gi, nout)
        if pendingB is not None:
            pendingB()
            pendingB = None
    for s in range(4):
        for tb in range(2):
            pz, pzb = g.bank()
            P.mm([lambda e: e.matmul(pz[:, :], lhsT=g.cst[:, C_SWAP:C_SWAP + 128], rhs=UZ[:, s, tb * 512:(tb + 1) * 512], start=True, stop=True)],
                 reads=(bUZ, g.kbuf), writes=(pzb,))
            lo = 0 if s % 2 == 0 else 64
            P.op("dve", lambda e: e.reciprocal(out=rz[lo:lo + 64, :], in_=pz[lo:lo + 64, :]), reads=(pzb,), writes=(brz,))
            P.op("dve", lambda e: e.tensor_tensor(out=YT[lo:lo + 64, 8 + s // 2, tb * 512:(tb + 1) * 512], in0=UZ[lo:lo + 64, s, tb * 512:(tb + 1) * 512],
                                                  in1=rz[lo:lo + 64, :], op=ALU.mult), reads=(brz, bUZ))
    t = g.tap("YB", [256, TO], BF16)
    if t is not None:
        P.barrier()
        P.dma("sp", t.ap().rearrange("(c p) t -> p c t", p=128), YT[:, 8:10, :])


def phase3_conv(g):
    P = g.P
    g.RBC.reset(); g.RA2.reset()
    RR = [g.RBC, g.RA2]
    sb = lambda shape, dt, name: P.sb(RR, shape, dt, name)
    YT = g.YT
    UT = sb([128, 6, TO + 32], BF16, "UT")
    yT = sb([128, 6, TO], F32, "yT")
    ybf = sb([128, 6, 512], BF16, "ybf"); ysq = sb([128, 6, 512], BF16, "ysq")
    m_ = sb([128, 512], F32, "m"); v_ = sb([128, 512], F32, "v"); rs = sb([128, 512], F32, "rs")
    tt = [sb([128, 512], F32, f"tt{i}") for i in range(2)]
    onesb = sb([128, 128], BF16, "onesb")
    dg = [sb([128, 128], BF16, f"dg{i}") for i in range(4)]
    bdg = [Buf(f"dg{i}") for i in range(4)]
    bU, by, bybf, bysq, bm, bv, brs, bOnes = (Buf(n) for n in "U y ybf ysq m v rs ones".split())
    btt = [Buf("tt0"), Buf("tt1")]
    P.op("dve", lambda e: e.memset(onesb[:], 1.0), writes=(bOnes,))
    P.dma("sp", UT[:, :, 2:TO + 32], g.U.ap().rearrange("(t p) n -> p t n", p=128)[:, :, 2:TO + 32], writes=(bU,))
    k = 0
    for ct in range(6):
        pA, pAb = g.bank()
        pB, pBb = g.bank()
        for j in range(31):
            d, bd = dg[k % 4], bdg[k % 4]
            k += 1
            col = V_CONVW + ct * 31 + j
            P.op("dve", lambda e: e.tensor_scalar(out=d[:], in0=g.identb[:], scalar1=g.vec[:, col:col + 1], scalar2=None, op0=ALU.mult),
                 reads=(g.kbuf,), writes=(bd,))
            P.mm([lambda e: e.matmul(pA[:, :], lhsT=d[:], rhs=UT[:, ct, 2 + j:2 + j + 512], start=(j == 0), stop=(j == 30)),
                  lambda e: e.matmul(pB[:, :], lhsT=d[:], rhs=UT[:, ct, 512 + 2 + j:512 + 2 + j + 512], start=(j == 0), stop=(j == 30))],
                 reads=(bd, bU), writes=(pAb, pBb))
        cb = g.vec[:, V_CONVB + ct:V_CONVB + ct + 1]
        P.op("act", lambda e: e.activation(out=yT[:, ct, 0:512], in_=pA[:, :], func=AF.Identity, bias=cb), reads=(pAb, g.kbuf), writes=(by,))
        P.op("act", lambda e: e.activation(out=yT[:, ct, 512:1024], in_=pB[:, :], func=AF.Identity, bias=cb), reads=(pBb, g.kbuf, by), writes=(by,))
    for tb in range(2):
        ts = slice(tb * 512, (tb + 1) * 512)
        P.op("act", lambda e: e.copy(out=ybf[:], in_=yT[:, :, ts]), reads=(by,), writes=(bybf,))
        P.op("act", lambda e: e.activation(out=ysq[:], in_=yT[:, :, ts], func=AF.Square), reads=(by,), writes=(bysq,))
        p1, p1b = g.bank()
        p2, p2b = g.bank()
        P.mm([lambda e, ct=ct: e.matmul(p1[:, :], lhsT=onesb[:], rhs=ybf[:, ct, :], start=(ct == 0), stop=(ct == 5)) for ct in range(6)],
             reads=(bybf, bOnes), writes=(p1b,))
        P.mm([lambda e, ct=ct: e.matmul(p2[:, :], lhsT=onesb[:], rhs=ysq[:, ct, :], start=(ct == 0), stop=(ct == 5)) for ct in range(6)],
             reads=(bysq, bOnes), writes=(p2b,))
        P.op("dve", lambda e: e.tensor_scalar(out=m_[:], in0=p1[:, :], scalar1=1.0 / 768, scalar2=None, op0=ALU.mult), reads=(p1b,), writes=(bm,))
        P.op("dve", lambda e: e.tensor_tensor(out=v_[:], in0=m_[:], in1=m_[:], op=ALU.mult), reads=(bm,), writes=(bv,))
        P.op("dve", lambda e: e.scalar_tensor_tensor(out=v_[:], in0=p2[:, :], scalar=1.0 / 768, in1=v_[:], op0=ALU.mult, op1=ALU.subtract),
             reads=(p2b, bv), writes=(bv,))
        P.op("act", lambda e: e.activation(out=rs[:], in_=v_[:], func=AF.Sqrt, bias=EPS), reads=(bv,), writes=(brs,))
        P.op("dve", lambda e: e.reciprocal(out=rs[:], in_=rs[:]), reads=(brs,), writes=(brs,))
        for ct in range(6):
            t_, bt_ = tt[ct % 2], btt[ct % 2]
            P.op("dve", lambda e: e.tensor_tensor(out=t_[:], in0=yT[:, ct, ts], in1=m_[:], op=ALU.subtract), reads=(by, bm), writes=(bt_,))
            P.op("dve", lambda e: e.tensor_tensor(out=t_[:], in0=t_[:], in1=rs[:], op=ALU.mult), reads=(bt_, brs), writes=(bt_,))
            P.op("act", lambda e: e.activation(out=YT[:, 10 + ct, ts], in_=t_[:], func=AF.Silu, scale=g.vec[:, V_LNG + ct:V_LNG + ct + 1],
                                               bias=g.vec[:, V_LNB + ct:V_LNB + ct + 1]), reads=(bt_, g.kbuf))
    t = g.tap("YC", [768, TO], BF16)
    if t is not None:
        P.barrier()
        P.dma("sp", t.ap().rearrange("(c p) t -> p c t", p=128), YT[:, 10:16, :])


def plan_weights_p45(g):
    W = g.W
    wv = lambda w, c0, n: w.ap().rearrange("(kc p) n -> p kc n", p=128)[:, :, c0:c0 + n]
    for cb in range(4):
        W.add([(lambda t: slot_view(t, 8, 512, kc0=0), wv(g.w_a_out, cb * 512, 512)),
               (lambda t: slot_view(t, 2, 512, kc0=8), wv(g.w_b_out, cb * 512, 512)),
               (lambda t: slot_view(t, 6, 512, kc0=10), wv(g.w_c_out, cb * 512, 512))])
    for cb in range(4):
        W.add([(lambda t: slot_view(t, 16, 512), wv(g.w_o, cb * 512, 512))])
    for gq in range(8):
        for i in range(2):
            W.add([(lambda t: slot_view(t, 16, 512), wv(g.w_up, gq * 1024 + i * 512, 512))])
        for half in range(2):
            src = g.w_down.ap()[gq * 1024:(gq + 1) * 1024, half * 1024:(half + 1) * 1024].rearrange("(kc p) n -> p kc n", p=128)
            W.add([(lambda t: slot_view(t, 8, 1024, width=1024), src)])


def phase4a(g):
    P = g.P
    g.RBC.reset(); g.RA2.reset()
    YT = g.YT
    g.mT = P.sb(g.RA2, [128, 16, TO], BF16, "mT")
    R = g.RBC
    gt = [P.sb(R, [128, 3, TO], BF16, f"gt{i}") for i in range(2)]
    bgt = [Buf("gt0"), Buf("gt1")]
    t1 = [P.sb(R, [128, 512], F32, f"t1{i}") for i in range(2)]
    t2 = [P.sb(R, [128, 512], F32, f"t2{i}") for i in range(2)]
    bt1 = [Buf("t10"), Buf("t11")]; bt2 = [Buf("t20"), Buf("t21")]
    gv = g.GATES.ap().rearrange("(i f) t -> f i t", i=3)
    k = 0
    for cb in range(4):
        stt, sbuf = g.W.get()
        sv = slot_view(stt, 16, 512)
        for j in range(4):
            fo = cb * 4 + j
            gg, bg = gt[fo % 2], bgt[fo % 2]
            P.dma("sp", gg[:], gv[fo * 128:(fo + 1) * 128], writes=(bg,))
            for tb in range(2):
                ts = slice(tb * 512, (tb + 1) * 512)
                banks = []
                for (k0, k1) in ((0, 8), (8, 10), (10, 16)):
                    pt, pb = g.bank()
                    P.mm([lambda e, kc=kc, pt=pt, k0=k0, k1=k1: e.matmul(pt[:, :], lhsT=sv[:, kc, j * 128:(j + 1) * 128], rhs=YT[:, kc, ts],
                                                                         start=(kc == k0), stop=(kc == k1 - 1)) for kc in range(k0, k1)],
                         reads=(sbuf,), writes=(pb,))
                    banks.append((pt, pb))
                a, ba = t1[k % 2], bt1[k % 2]
                b, bb = t2[k % 2], bt2[k % 2]
                k += 1
                (pA, pAb), (pB, pBb), (pC, pCb) = banks
                P.op("dve", lambda e: e.tensor_tensor(out=a[:], in0=pA[:, :], in1=gg[:, 0, ts], op=ALU.mult), reads=(pAb, bg), writes=(ba,))
                P.op("dve", lambda e: e.tensor_tensor(out=b[:], in0=pB[:, :], in1=gg[:, 1, ts], op=ALU.mult), reads=(pBb, bg), writes=(bb,))
                P.op("dve", lambda e: e.tensor_tensor(out=a[:], in0=a[:], in1=b[:], op=ALU.add), reads=(ba, bb), writes=(ba,))
                P.op("dve", lambda e: e.tensor_tensor(out=b[:], in0=pC[:, :], in1=gg[:, 2, ts], op=ALU.mult), reads=(pCb, bg, bb), writes=(bb,))
                P.op("dve", lambda e: e.tensor_tensor(out=g.mT[:, fo, ts], in0=a[:], in1=b[:], op=ALU.add), reads=(ba, bb))
    t = g.tap("MT", [D, TO], BF16)
    if t is not None:
        P.barrier()
        P.dma("sp", t.ap().rearrange("(c p) t -> p c t", p=128), g.mT[:])


def phase4b(g):
    P = g.P
    j0 = g.W.taken
    slots = [g.W.get(hold_from=j0) for _ in range(4)]
    g.RA1.reset(); g.RB.reset(); g.RC.reset()
    g.h2T = P.sb(g.RB, [128, 16, TO], BF16, "h2T")
    xt = [P.sb(g.RA1, [128, D], F32, f"xt{i}") for i in range(2)]
    bxt = [Buf("xt0"), Buf("xt1")]
    Grow = P.sb(g.RA1, [128, D], F32, "Grow")
    tq = P.sb(g.RA1, [128, D], F32, "tq")
    x1t = [P.sb(g.RC, [128, 1, D], F32, f"x1t{i}") for i in range(2)]
    bx1 = [Buf("x1t0"), Buf("x1t1")]
    junk = P.sb(g.RC, [128, D], BF16, "junk")
    st = P.sb(g.RC, [128, 8], F32, "st"); st2 = P.sb(g.RC, [128, 8], F32, "st2")
    bst, bst2, bG, btq = Buf("st"), Buf("st2"), Buf("Grow"), Buf("tq")
    postbc = P.sb(g.RC, [128, D], F32, "postbc")
    P.dma("sp", Grow[:], bass.AP(g.GROWS, 0, [[0, 128], [1, D]]), writes=(bG,))
    P.dma("sp", postbc[:], bass.AP(g.rows2, g.l * NROW + R_POST, [[0, 128], [1, D]]), writes=(bG,), add_write=True)
    P.op("dve", lambda e: e.tensor_tensor(out=Grow[:], in0=Grow[:], in1=postbc[:], op=ALU.mult), reads=(bG,), writes=(bG,))
    xv = g.xctx.ap().rearrange("(t p) d -> p t d", p=128)
    def stage_a(ti):
        x_, bx_ = xt[ti % 2], bxt[ti % 2]
        x1, b1 = x1t[ti % 2], bx1[ti % 2]
        P.dma("sp", x_[:], xv[:, 8 + ti, :], writes=(bx_,))
        bks = []
        for cb in range(4):
            pt, pb = g.bank()
            stt, sbuf = slots[cb]
            sv = slot_view(stt, 16, 512)
            P.mm([lambda e, kc=kc, pt=pt, sv=sv: e.matmul(pt[:, :], lhsT=g.mT[:, kc, ti * 128:(ti + 1) * 128], rhs=sv[:, kc, :],
                                                          start=(kc == 0), stop=(kc == KC - 1)) for kc in range(KC)], reads=(sbuf,), writes=(pb,))
            bks.append((pt, pb))
        for cb, (pt, pb) in enumerate(bks):
            P.op("act", lambda e, pt=pt, cb=cb: e.activation(out=junk[:, 0:512], in_=pt[:, :], func=AF.Square, accum_out=st2[:, cb:cb + 1]),
                 reads=(pb,), writes=(bst2,))
            P.op("dve", lambda e, pt=pt, cb=cb: e.tensor_tensor(out=tq[:, cb * 512:(cb + 1) * 512], in0=pt[:, :], in1=Grow[:, cb * 512:(cb + 1) * 512],
                                                               op=ALU.mult), reads=(pb, bG), writes=(btq,))
        P.op("dve", lambda e: e.reduce_sum(out=st2[:, 4:5], in_=st2[:, 0:4], axis=AX.X), reads=(bst2,), writes=(bst2,))
        P.op("dve", lambda e: e.tensor_scalar(out=st2[:, 5:6], in0=st2[:, 4:5], scalar1=1.0 / D, scalar2=EPS, op0=ALU.mult, op1=ALU.add),
             reads=(bst2,), writes=(bst2,))
        P.op("act", lambda e: e.activation(out=st2[:, 6:7], in_=st2[:, 5:6], func=AF.Sqrt), reads=(bst2,), writes=(bst2,))
        P.op("dve", lambda e: e.reciprocal(out=st2[:, 7:8], in_=st2[:, 6:7]), reads=(bst2,), writes=(bst2,))
        P.op("dve", lambda e: e.scalar_tensor_tensor(out=x1[:, 0, :], in0=tq[:], scalar=st2[:, 7:8], in1=x_[:], op0=ALU.mult, op1=ALU.add),
             reads=(btq, bst2, bx_), writes=(b1,))
        P.dma("sp", g.X1.ap()[ti * 128:(ti + 1) * 128, :], x1[:, 0, :], reads=(b1,))
        norm_stats(g, x1, b1, 1, junk, st, bst)

    def stage_b(ti):
        norm_xpose(g, x1t[ti % 2], bx1[ti % 2], 1, g.AB[:, 2, :], g.AB[:, 3, :], lambda fc, ti=ti: g.h2T[:, fc, ti * 128:(ti + 1) * 128])
    stage_a(0)
    for ti in range(8):
        if ti + 1 < 8:
            stage_a(ti + 1)
        stage_b(ti)
    t = g.tap("X1", [TO, D], F32)
    if t is not None:
        P.barrier()
        P.dma("sp", t.ap(), g.X1.ap())
    t = g.tap("H2T", [D, TO], BF16)
    if t is not None:
        P.barrier()
        P.dma("sp", t.ap().rearrange("(c p) t -> p c t", p=128), g.h2T[:])


def phase5(g):
    P = g.P
    g.RA.reset(); g.RC.reset()
    Y = P.sb(g.RA, [128, 8, D], F32, "Y")
    aT = P.sb(g.RC, [128, 8, TO], BF16, "aT")
    rst = [P.sb(g.RC, [128, 512], F32, f"rst{i}") for i in range(2)]
    brst = [Buf("rst0"), Buf("rst1")]
    baT = [Buf(f"aT{i}") for i in range(8)]
    bY = [[Buf(f"Y{ti}_{cb}") for cb in range(4)] for ti in range(8)]
    h2T = g.h2T
    k = 0
    for gq in range(8):
        for i in range(2):
            stt, sbuf = g.W.get()
            sv = slot_view(stt, 16, 512)
            for j in range(4):
                ffc = i * 4 + j
                for tb in range(2):
                    ts = slice(tb * 512, (tb + 1) * 512)
                    pt, pb = g.bank()
                    P.mm([lambda e, kc=kc, pt=pt: e.matmul(pt[:, :], lhsT=sv[:, kc, j * 128:(j + 1) * 128], rhs=h2T[:, kc, ts],
                                                           start=(kc == 0), stop=(kc == KC - 1)) for kc in range(KC)], reads=(sbuf,), writes=(pb,))
                    r_, br_ = rst[k % 2], brst[k % 2]
                    k += 1
                    P.op("act", lambda e, pt=pt, r_=r_: e.activation(out=r_[:], in_=pt[:, :], func=AF.Relu), reads=(pb,), writes=(br_,))
                    P.op("dve", lambda e, r_=r_, ffc=ffc, ts=ts: e.tensor_tensor(out=aT[:, ffc, ts], in0=r_[:], in1=r_[:], op=ALU.mult),
                         reads=(br_,), writes=(baT[ffc],))
        for half in range(2):
            stt, sbuf = g.W.get()
            sv8 = slot_view(stt, 8, 1024, width=1024)
            for cbh in range(2):
                cb = half * 2 + cbh
                for ti in range(8):
                    pt, pb = g.bank()
                    P.mm([lambda e, kc=kc, pt=pt: e.matmul(pt[:, :], lhsT=aT[:, kc, ti * 128:(ti + 1) * 128], rhs=sv8[:, kc, cbh * 512:(cbh + 1) * 512],
                                                           start=(kc == 0), stop=(kc == 7)) for kc in range(8)],
                         reads=(sbuf,) + tuple(baT), writes=(pb,))
                    yv = Y[:, ti, cb * 512:(cb + 1) * 512]
                    if gq == 0:
                        P.op("act", lambda e, pt=pt, yv=yv: e.copy(out=yv, in_=pt[:, :]), reads=(pb,), writes=(bY[ti][cb],))
                    else:
                        P.op("dve", lambda e, pt=pt, yv=yv: e.tensor_tensor(out=yv, in0=pt[:, :], in1=yv, op=ALU.add),
                             reads=(pb, bY[ti][cb]), writes=(bY[ti][cb],))
    P.barrier()
    g.RB.reset(); g.RC.reset()
    Grow = P.sb(g.RB, [128, D], F32, "Grow2")
    x1t = [P.sb(g.RB, [128, D], F32, f"x1f{i}") for i in range(2)]
    bx1 = [Buf("x1f0"), Buf("x1f1")]
    ot = [P.sb(g.RC, [128, D], F32, f"ot{i}") for i in range(2)]
    bot = [Buf("ot0"), Buf("ot1")]
    junk = P.sb(g.RC, [128, D], BF16, "junkf")
    tq = P.sb(g.RC, [128, D], F32, "tqf")
    postbc = P.sb(g.RC, [128, D], F32, "postbc2")
    st = P.sb(g.RB, [128, 8], F32, "stf")
    bG, bst, btq = Buf("G2"), Buf("stf"), Buf("tqf")
    P.dma("sp", Grow[:], bass.AP(g.GROWS, D, [[0, 128], [1, D]]), writes=(bG,))
    P.dma("sp", postbc[:], bass.AP(g.rows2, g.l * NROW + R_MLPPOST, [[0, 128], [1, D]]), writes=(bG,), add_write=True)
    P.op("dve", lambda e: e.tensor_tensor(out=Grow[:], in0=Grow[:], in1=postbc[:], op=ALU.mult), reads=(bG,), writes=(bG,))
    for ti in range(8):
        x1, b1 = x1t[ti % 2], bx1[ti % 2]
        o_, bo_ = ot[ti % 2], bot[ti % 2]
        P.dma("sp", x1[:], g.X1.ap()[ti * 128:(ti + 1) * 128, :], writes=(b1,))
        P.op("act", lambda e: e.activation(out=junk[:], in_=Y[:, ti, :], func=AF.Square, accum_out=st[:, 0:1]), writes=(bst,))
        P.op("dve", lambda e: e.tensor_scalar(out=st[:, 1:2], in0=st[:, 0:1], scalar1=1.0 / D, scalar2=EPS, op0=ALU.mult, op1=ALU.add),
             reads=(bst,), writes=(bst,))
        P.op("act", lambda e: e.activation(out=st[:, 2:3], in_=st[:, 1:2], func=AF.Sqrt), reads=(bst,), writes=(bst,))
        P.op("dve", lambda e: e.reciprocal(out=st[:, 3:4], in_=st[:, 2:3]), reads=(bst,), writes=(bst,))
        P.op("dve", lambda e: e.tensor_tensor(out=tq[:], in0=Y[:, ti, :], in1=Grow[:], op=ALU.mult), reads=(bG,), writes=(btq,))
        P.op("dve", lambda e: e.scalar_tensor_tensor(out=o_[:], in0=tq[:], scalar=st[:, 3:4], in1=x1[:], op0=ALU.mult, op1=ALU.add),
             reads=(btq, bst, b1), writes=(bo_,))
        P.dma("sp", g.xout.ap()[ti * 128:(ti + 1) * 128, :], o_[:], reads=(bo_,))


def make_consts():
    c = np.zeros((128, NCONST), np.float32)
    c[:, C_IDENT:C_IDENT + 128] = np.eye(128, dtype=np.float32)
    sw = np.zeros((128, 128), np.float32)
    for m in range(128):
        sw[(m + 64) % 128, m] = 1.0
    c[:, C_SWAP:C_SWAP + 128] = sw
    fv = np.zeros(383, np.float32); fv[127:127 + 129] = 1.0
    c[:, C_FVALID:C_FVALID + 383] = fv[None, :]
    s = np.arange(64)[:, None]; t = np.arange(64)[None, :]
    c[:64, C_CAUS:C_CAUS + 64] = (s <= t).astype(np.float32)
    rm = np.ones(512, np.float32); rm[0::64] = 0.0
    c[:, C_RESET:C_RESET + 512] = rm[None, :]
    c[:, C_ONES:C_ONES + 128] = 1.0
    return c

def t5_bucket_np(dist):
    import math
    exact = 16
    d = np.maximum(dist, 1).astype(np.float32)
    large = exact + (np.log(d / np.float32(exact)) / np.float32(math.log(2048 / exact)) * np.float32(32 - exact)).astype(np.int32)
    return np.where(dist < exact, dist, np.clip(large, exact, 31))

def make_onehot():
    oh = np.zeros((3, 32, 383), np.float32)
    for gi, dil in enumerate((1, 4, 16)):
        for s in range(129):
            b = int(t5_bucket_np(np.array([s * dil]))[0])
            oh[gi, b, 127 + s] = 1.0
    return oh

def col(v):
    return np.ascontiguousarray(np.asarray(v, np.float32).reshape(-1, 128).T)

def layer_small(inputs, l):
    vec = np.zeros((128, NVEC), np.float32)
    vec[:, V_PRE:V_PRE + 16] = col(inputs["mix_norm_pre"][l])
    vec[:, V_MLPPRE:V_MLPPRE + 16] = col(inputs["mlp_norm_pre"][l])
    vec[:, V_BGATE:V_BGATE + 48] = col(inputs["b_gate"][l])
    vec[:, V_NORMW:V_NORMW + 1] = col(inputs["hgrn_norm_w"][l])
    vec[:, V_CONVB:V_CONVB + 6] = col(inputs["conv_b"][l])
    vec[:, V_LNG:V_LNG + 6] = col(inputs["conv_ln_g"][l])
    vec[:, V_LNB:V_LNB + 6] = col(inputs["conv_ln_b"][l])
    cw = np.asarray(inputs["conv_w"][l], np.float32)
    vec[:, V_CONVW:V_CONVW + 186] = cw.reshape(31, 6, 128).transpose(2, 1, 0).reshape(128, 186)
    rows = np.zeros((1, NROW), np.float32)
    rows[0, R_BADA:R_BADA + 12288] = inputs["b_ada"][l]
    rows[0, R_POST:R_POST + 2048] = inputs["mix_norm_post"][l]
    rows[0, R_MLPPOST:R_MLPPOST + 2048] = inputs["mlp_norm_post"][l]
    return vec, rows

def core_inputs(inputs, l, b, half, xfull):
    xb = np.asarray(xfull[b], np.float32)
    if half == 0:
        xctx = np.concatenate([np.zeros((1024, D), np.float32), xb[:1024]], axis=0)
    else:
        xctx = xb
    flag = float(half)
    tm = np.zeros((128, 4), np.float32)
    tm[:, 0] = flag; tm[:, 1] = 1.0; tm[:64, 2] = flag; tm[64:, 2] = 1.0; tm[:, 3] = flag
    vec, rows = layer_small(inputs, l)
    lbl = np.asarray(inputs["hgrn_lb_logits"], np.float32).reshape(2, 8, 128).transpose(2, 0, 1)
    return {
        "xctx": np.ascontiguousarray(xctx), "c_b": col(inputs["c"][b]), "tmask": tm,
        "consts": make_consts(), "vecs": vec, "rows": rows, "lbl": np.ascontiguousarray(lbl),
        "rel_bias": np.asarray(inputs["rel_bias"], np.float32), "onehot": make_onehot(),
        "w_ada": np.asarray(inputs["w_ada"][l]), "w_in": np.asarray(inputs["w_in"][l]),
        "w_gate": np.asarray(inputs["w_gate"][l]), "w_a_out": np.asarray(inputs["w_a_out"][l]),
        "w_b_out": np.asarray(inputs["w_b_out"][l]), "w_c_out": np.asarray(inputs["w_c_out"][l]),
        "w_o": np.asarray(inputs["w_o"][l]), "w_up": np.asarray(inputs["w_up"][l]),
        "w_down": np.asarray(inputs["w_down"][l]),
        "layer_is1": np.full((128, 1), float(l), np.float32),
    }


def fused_core_inputs(inputs, b, half, shared):
    xb = np.asarray(inputs["x"][b], np.float32)
    z = np.zeros((1024, D), np.float32)
    if half == 1:
        x3 = np.concatenate([z, xb[:1024], xb[1024:]], axis=0)
    else:
        x3 = np.concatenate([z, z, xb[:1024]], axis=0)
    tm = np.zeros((2, 128, 4), np.float32)
    for i, flag in enumerate((0.0, float(half))):
        tm[i, :, 0] = flag; tm[i, :, 1] = 1.0; tm[i, :64, 2] = flag; tm[i, 64:, 2] = 1.0; tm[i, :, 3] = flag
    m = dict(shared)
    m["x3"] = np.ascontiguousarray(x3)
    m["c_b"] = col(inputs["c"][b])
    m["tmask"] = tm
    return m


def shared_inputs(inputs):
    vl = [layer_small(inputs, l) for l in range(2)]
    lbl = np.asarray(inputs["hgrn_lb_logits"], np.float32).reshape(2, 8, 128).transpose(2, 0, 1)
    sh = {
        "consts": make_consts(), "vecs": np.stack([v[0] for v in vl]), "rows": np.stack([v[1] for v in vl]),
        "lbl": np.ascontiguousarray(lbl), "rel_bias": np.asarray(inputs["rel_bias"], np.float32), "onehot": make_onehot(),
    }
    for k in ("w_ada", "w_in", "w_gate", "w_a_out", "w_b_out", "w_c_out", "w_o", "w_up", "w_down"):
        sh[k] = np.asarray(inputs[k], np.float32)
    return sh


_NC_CACHE = {}


def kernel(**inputs):
    inputs = {k: np.asarray(v) for k, v in inputs.items()}
    if "nc" not in _NC_CACHE:
        _NC_CACHE["nc"] = build_fused()[0]
    nc = _NC_CACHE["nc"]
    sh = shared_inputs(inputs)
    maps = [fused_core_inputs(inputs, b, half, sh) for b in range(4) for half in range(2)]
    res = run_bass_kernel_spmd(nc, maps, core_ids=list(range(8)))
    out = np.empty((4, 2048, D), np.float32)
    for b in range(4):
        for half in range(2):
            out[b, half * 1024:(half + 1) * 1024] = res.results[b * 2 + half]["xout"]
    return out
```

```python
import numpy as np
from contextlib import ExitStack
import concourse.bass as bass
import concourse.mybir as mybir
from concourse.bass_utils import run_bass_kernel_spmd

F32 = mybir.dt.float32
BF16 = mybir.dt.bfloat16
AF = mybir.ActivationFunctionType
ALU = mybir.AluOpType
AX = mybir.AxisListType

D = 2048
KC = 16
TC = 2048
TO = 1024
DFF = 8192
INW = 7936
EPS = 1e-6
SB_BASE = 16512
SB_TOP = 229344
NSLOT = 4
SLOT_BYTES = 16384
CONST_BYTES = 10240
RA_BYTES = 65536
RB_BYTES = 32768

C_IDENT = 0
C_SWAP = 128
C_FVALID = 256
C_CAUS = 256 + 383
C_RESET = C_CAUS + 64
C_ONES = C_RESET + 512
NCONST = C_ONES + 128
V_PRE = 0
V_MLPPRE = 16
V_BGATE = 32
V_NORMW = 80
V_CONVB = 81
V_LNG = 87
V_LNB = 93
V_CONVW = 99
NVEC = V_CONVW + 6 * 31
R_BADA = 0
R_POST = 12288
R_MLPPOST = 12288 + 2048
NROW = 12288 + 4096


class Buf:
    __slots__ = ("name", "w", "r", "excl")

    def __init__(self, name="", excl=False):
        self.name = name
        self.w = []
        self.r = {}
        self.excl = excl


class Region:
    def __init__(self, base, size):
        self.base = base
        self.size = size
        self.off = 0

    def reset(self):
        self.off = 0


class Prog:
    def __init__(self, nc, stack):
        self.nc = nc
        self.E = dict(pe=nc.tensor, act=nc.scalar, dve=nc.vector, pool=nc.gpsimd, sp=nc.sync)
        self.sem = {}
        self.cnt = {}
        for k in ("pe", "act", "dve", "pool"):
            self.sem[k] = stack.enter_context(nc.semaphore("s_" + k))
            self.cnt[k] = 0
        self.dsem = {}
        self.dcnt = {}
        self.dnext = {}
        for q, n in (("sp", 16), ("pool", 8)):
            self.dsem[q] = [stack.enter_context(nc.semaphore(f"d_{q}{i}")) for i in range(n)]
            self.dcnt[q] = [0] * n
            self.dnext[q] = 0
        self.waited = {e: {} for e in self.E}
        self.nalloc = 0
        self.n_ins = 0

    def _semof(self, key):
        if isinstance(key, tuple):
            return self.dsem[key[1]][key[2]]
        return self.sem[key]

    def _wait(self, e, key, val):
        if e == "pe" and key == "pe":
            return
        if self.waited[e].get(key, 0) >= val:
            return
        self.E[e].wait_ge(self._semof(key), val)
        self.waited[e][key] = val

    def _deps(self, e, reads, writes):
        best = {}
        for b in reads:
            for k, v in b.w:
                if best.get(k, 0) < v:
                    best[k] = v
            if b.excl:
                for k, v in b.r.items():
                    if k != e and best.get(k, 0) < v:
                        best[k] = v
        for b in writes:
            for k, v in b.w:
                if best.get(k, 0) < v:
                    best[k] = v
            for k, v in b.r.items():
                if best.get(k, 0) < v:
                    best[k] = v
        for k, v in best.items():
            self._wait(e, k, v)

    def _commit(self, tok, reads, writes):
        for b in writes:
            b.w = [tok]
            b.r = {}
        for b in reads:
            if b.r.get(tok[0], 0) < tok[1]:
                b.r[tok[0]] = tok[1]

    def op(self, e, fn, reads=(), writes=()):
        self._deps(e, reads, writes)
        ins = fn(self.E[e])
        self.cnt[e] += 1
        ins.then_inc(self.sem[e], 1)
        self._commit((e, self.cnt[e]), reads, writes)
        self.n_ins += 1
        return ins

    def mm(self, mms, reads=(), writes=()):
        self._deps("pe", reads, writes)
        ins = None
        for f in mms:
            ins = f(self.nc.tensor)
        self.cnt["pe"] += 1
        ins.then_inc(self.sem["pe"], 1)
        self._commit(("pe", self.cnt["pe"]), reads, writes)
        self.n_ins += len(mms)

    def dma(self, q, out, in_, reads=(), writes=(), add_write=False):
        i = self.dnext[q]
        self.dnext[q] = (i + 1) % len(self.dsem[q])
        key = ("d", q, i)
        self._wait(q, key, self.dcnt[q][i])
        self._deps(q, reads, writes)
        self.E[q].dma_start(out=out, in_=in_).then_inc(self.dsem[q][i], 16)
        self.dcnt[q][i] += 16
        tok = (key, self.dcnt[q][i])
        for b in writes:
            if add_write:
                b.w = b.w + [tok]
            else:
                b.w = [tok]
                b.r = {}
        for b in reads:
            b.r[key] = tok[1]
        self.n_ins += 1
        return tok

    def barrier(self, engines=("pe", "act", "dve", "sp", "pool")):
        toks = [(k, self.cnt[k]) for k in ("pe", "act", "dve", "pool") if self.cnt[k] > 0]
        for i, v in enumerate(self.dcnt["sp"]):
            if v > 0:
                toks.append((("d", "sp", i), v))
        for e in engines:
            for k, v in toks:
                self._wait(e, k, v)

    def final_wait(self):
        toks = [(k, self.cnt[k]) for k in ("pe", "act", "dve", "pool") if self.cnt[k] > 0]
        for q in ("sp", "pool"):
            for i, v in enumerate(self.dcnt[q]):
                if v > 0:
                    toks.append((("d", q, i), v))
        for e in ("sp", "pool", "act", "dve", "pe"):
            for k, v in toks:
                self._wait(e, k, v)

    def sb(self, region, shape, dtype, name="t"):
        esz = 4 if dtype == F32 else 2
        n = 1
        for s in shape[1:]:
            n *= s
        nbytes = (n * esz + 31) // 32 * 32
        regs = region if isinstance(region, (list, tuple)) else [region]
        for r in regs:
            if r.off + nbytes <= r.size:
                off = r.base + r.off
                r.off += nbytes
                self.nalloc += 1
                return self.nc.alloc_sbuf_tensor_at(f"{name}_{self.nalloc}", list(shape), dtype, offset=off)
        raise RuntimeError(f"SBUF region overflow allocating {name} {shape}")


class WStream:
    def __init__(self, P, slots):
        self.P = P
        self.slots = slots
        self.plan = []
        self.issued = 0
        self.taken = 0

    def add(self, pieces):
        self.plan.append(pieces)

    def _issue(self, j):
        t, b = self.slots[j % len(self.slots)]
        first = True
        for dst_fn, src in self.plan[j]:
            self.P.dma("pool", dst_fn(t), src, writes=(b,), add_write=not first)
            first = False

    def get(self, hold_from=None):
        j = self.taken
        self.taken += 1
        base = j if hold_from is None else hold_from
        lim = min(len(self.plan), base + len(self.slots))
        while self.issued < lim:
            self._issue(self.issued)
            self.issued += 1
        return self.slots[j % len(self.slots)]


def slot_view(t, kc, n, kc0=0, c0=0, width=512):
    return bass.AP(t, kc0 * width + c0, [[8192, 128], [width, kc], [1, n]])


class Ctx:
    pass


class LV:
    def __init__(self, t, idx=None, rows=None):
        self.t, self.idx, self.rows = t, idx, rows
        shp = list(t.shape)
        self.shape = shp[1:] if idx is not None else shp
        self.dtype = t.dtype

    def ap(self):
        a = self.t.ap()
        if self.idx is not None:
            a = a[self.idx]
        if self.rows is not None:
            a = a[self.rows[0]:self.rows[1]]
        return a


def build_fused(dbg=None, passes=("A", "B", "L1"), stop_after=None):
    dbg = dbg or set()
    nc = bass.Bass("TRN2", target_bir_lowering=False)
    stack = ExitStack()
    with stack:
        P = Prog(nc, stack)
        g = Ctx()
        g.nc, g.P, g.dbg = nc, P, dbg
        di = lambda name, shape, dt=F32: nc.dram_tensor(name, list(shape), dt, kind="ExternalInput")
        g.x3 = di("x3", [3 * TO, D])
        g.c_b = di("c_b", [128, KC])
        g.tmask2 = di("tmask", [2, 128, 4])
        g.consts = di("consts", [128, NCONST])
        g.vecs2 = di("vecs", [2, 128, NVEC])
        g.rows2 = di("rows", [2, 1, NROW])
        g.lbl = di("lbl", [128, 2, 8])
        g.rel_bias = di("rel_bias", [32, 12])
        g.onehot = di("onehot", [3, 32, 383])
        g.Wf = dict(
            w_ada=di("w_ada", [2, D, 6 * D]), w_in=di("w_in", [2, D, INW]), w_gate=di("w_gate", [2, D, 3 * D]),
            w_a_out=di("w_a_out", [2, 1024, D]), w_b_out=di("w_b_out", [2, 256, D]), w_c_out=di("w_c_out", [2, 768, D]),
            w_o=di("w_o", [2, D, D]), w_up=di("w_up", [2, D, DFF]), w_down=di("w_down", [2, DFF, D]))
        g.xout_t = nc.dram_tensor("xout", [TO, D], F32, kind="ExternalOutput")
        ds = lambda name, shape, dt: nc.dram_tensor(name, list(shape), dt)
        g.AQ = ds("AQ", [1024, TO], BF16)
        g.AF = ds("AFz", [1024, TC], F32)
        g.AG = ds("AG", [1024, TO], BF16)
        g.AI = ds("AI", [TC, 1024], BF16)
        g.BQ = ds("BQ", [768, TO], BF16)
        g.BK = ds("BK", [768, TC], BF16)
        g.BV = ds("BV", [3, 16, 128, 512], BF16)
        g.U = ds("U", [768, TO + 32], BF16)
        g.GATES = ds("GATES", [3 * D, TO], BF16)
        g.X1 = ds("X1", [TO, D], F32)
        g.FT = ds("FT", [12, 128 * 383], F32)
        g.GROWS = ds("GROWS", [2, D], F32)
        g.XL1 = ds("XL1", [TC, D], F32)
        g.taps = {}

        def tap(name, shape, dt=F32):
            return None
        g.tap = tap
        off = SB_BASE
        g.slots = []
        for i in range(NSLOT):
            t = nc.alloc_sbuf_tensor_at(f"wslot{i}", [128, 8192], BF16, offset=off)
            g.slots.append((t, Buf(f"slot{i}")))
            off += SLOT_BYTES
        g.RK = Region(off, CONST_BYTES); off += CONST_BYTES
        g.RA1 = Region(off, RA_BYTES // 2)
        g.RA2 = Region(off + RA_BYTES // 2, RA_BYTES // 2)
        g.RA = Region(off, RA_BYTES); off += RA_BYTES
        g.RB = Region(off, RB_BYTES); off += RB_BYTES
        g.RC = Region(off, SB_TOP - off)
        g.RBC = Region(g.RB.base, RB_BYTES + g.RC.size)
        g.banks = [(nc.alloc_psum_tensor(f"ps{i}", [128, 512], F32), Buf(f"bank{i}", excl=True)) for i in range(8)]
        g.bank_i = 0

        def bank():
            b = g.banks[g.bank_i]
            g.bank_i = (g.bank_i + 1) % 8
            return b
        g.bank = bank
        g.W = WStream(P, g.slots)
        g.cst = P.sb(g.RK, [128, NCONST], F32, "cst")
        g.identb = P.sb(g.RK, [128, 128], BF16, "identb")
        g.vec = P.sb(g.RK, [128, NVEC], F32, "vec")
        g.tm = P.sb(g.RK, [128, 4], F32, "tm")
        g.modc = P.sb(g.RK, [128, 4, 16], F32, "modc")
        g.AB = P.sb(g.RK, [128, 4, 16], F32, "AB")
        g.lbv = P.sb(g.RK, [128, 3, 8], F32, "lbv")
        g.kbuf = Buf("consts")
        P.dma("sp", g.cst[:], g.consts.ap(), writes=(g.kbuf,))
        P.op("dve", lambda e: e.tensor_copy(out=g.identb[:], in_=g.cst[:, C_IDENT:C_IDENT + 128]),
             reads=(g.kbuf,), writes=(g.kbuf,))
        g.ident = g.cst[:, C_IDENT:C_IDENT + 128]
        g.ones = g.cst[:, C_ONES:C_ONES + 128]

        for ps in passes:
            l = 1 if ps == "L1" else 0
            for k, t in g.Wf.items():
                setattr(g, k, LV(t, l))
            if ps in ("A", "L1"):
                plan_weights(g, 0, 8)
            plan_weights_p2(g)
            if ps in ("A", "L1"):
                plan_weights(g, 8, 24)
            plan_weights_p45(g)

        build_bias_tables(g)
        P.barrier()
        for ps in passes:
            l = 1 if ps == "L1" else 0
            g.l = l
            g.vecs = LV(g.vecs2, l)
            g.rows = LV(g.rows2, l)
            if ps == "A":
                g.xctx = LV(g.x3, None, (0, TC))
                g.xout = LV(g.XL1, None, (0, TO))
                tmi = 0
            elif ps == "B":
                g.xctx = LV(g.x3, None, (TO, TO + TC))
                g.xout = LV(g.XL1, None, (TO, TC))
                tmi = 1
            else:
                g.xctx = LV(g.XL1)
                g.xout = LV(g.xout_t)
                tmi = 1
            if "L1" not in passes and ps == passes[-1]:
                g.xout = LV(g.xout_t)
            P.dma("sp", g.vec[:], g.vecs.ap(), writes=(g.kbuf,))
            P.dma("sp", g.tm[:], g.tmask2.ap()[tmi], writes=(g.kbuf,), add_write=True)
            if ps in ("A", "L1"):
                phase0_setup(g)
                for _ in range(8):
                    phase0_tile(g)
                P.barrier()
            phase1(g)
            P.barrier()
            phase2(g)
            P.barrier()
            g.RA1.reset()
            g.YT = P.sb(g.RA1, [128, 16, TO], BF16, "YT")
            phase3_hgrn(g)
            while getattr(g, "p0_next", 24) < 24:
                phase0_tile(g)
            P.barrier()
            phase3_attn(g)
            P.barrier()
            phase3_conv(g)
            P.barrier()
            phase4a(g)
            P.barrier()
            phase4b(g)
            P.barrier()
            phase5(g)
            P.barrier()
        P.final_wait()
    return nc, g


def plan_weights(g, nt0, nt1):
    W = g.W
    wv = lambda w, c0, n: w.ap().rearrange("(kc p) n -> p kc n", p=128)[:, :, c0:c0 + n]
    for nt in range(nt0, nt1):
        W.add([(lambda t: slot_view(t, 16, 512), wv(g.w_ada, nt * 512, 512))])


def phase0_setup(g):
    P = g.P
    if not hasattr(g, "csb"):
        g.csb = P.sb(g.RK, [128, KC], F32, "csb")
        g.cact = P.sb(g.RK, [128, KC], BF16, "cact")
        g.stg0 = P.sb(g.RK, [1, 512], F32, "stg0")
        g.lb_in = P.sb(g.RK, [128, 2, 8], F32, "lb_in")
        g.b_c, g.b_stg0, g.b_lb, g.b_modc = Buf("c"), Buf("stg0"), Buf("lb"), Buf("modc")
    P.dma("sp", g.csb[:], g.c_b.ap(), writes=(g.b_c,))
    P.op("act", lambda e: e.activation(out=g.cact[:], in_=g.csb[:], func=AF.Silu), reads=(g.b_c,), writes=(g.b_c,))
    b_lb = g.b_lb
    P.dma("sp", g.lb_in[:], g.lbl.ap(), writes=(b_lb,))
    P.op("dve", lambda e: e.tensor_tensor(out=g.lbv[:, 2, :], in0=g.lb_in[:, 1, :], in1=g.lb_in[:, 0, :], op=ALU.subtract),
         reads=(b_lb,), writes=(b_lb,))
    P.op("act", lambda e: e.activation(out=g.lbv[:, 2, :], in_=g.lbv[:, 2, :], func=AF.Sigmoid), reads=(b_lb,), writes=(b_lb,))
    P.op("dve", lambda e: e.tensor_scalar(out=g.lbv[:, 0, :], in0=g.lbv[:, 2, :], scalar1=float(g.l), scalar2=None, op0=ALU.mult),
         reads=(b_lb,), writes=(b_lb,))
    P.op("dve", lambda e: e.tensor_scalar(out=g.lbv[:, 1, :], in0=g.lbv[:, 0, :], scalar1=-1.0, scalar2=1.0, op0=ALU.mult, op1=ALU.add),
         reads=(b_lb,), writes=(b_lb,))
    g.p0_next = 0


def phase0_tile(g):
    P = g.P
    nt = g.p0_next
    g.p0_next += 1
    stg, b_stg = g.stg0, g.b_stg0
    one11 = g.cst[0:1, C_ONES:C_ONES + 1]
    st, sb_ = g.W.get()
    pt, pb = g.bank()
    P.dma("sp", stg[:], g.rows.ap()[0:1, R_BADA + nt * 512:R_BADA + (nt + 1) * 512], writes=(b_stg,))
    P.mm([lambda e, kc=kc: e.matmul(pt[0:1, :], lhsT=g.cact[:, kc:kc + 1], rhs=slot_view(st, 16, 512)[:, kc, :],
                                    start=(kc == 0), stop=(kc == KC - 1)) for kc in range(KC)], reads=(g.b_c, sb_), writes=(pb,))
    P.op("dve", lambda e: e.tensor_tensor(out=stg[:], in0=pt[0:1, :], in1=stg[:], op=ALU.add), reads=(pb, b_stg), writes=(b_stg,))
    seg, j4 = nt // 4, nt % 4
    if seg in (2, 5):
        P.dma("sp", g.GROWS.ap()[(0 if seg == 2 else 1):(1 if seg == 2 else 2), j4 * 512:(j4 + 1) * 512], stg[:], reads=(b_stg,))
    else:
        si = {0: 0, 1: 1, 3: 2, 4: 3}[seg]
        pt2, pb2 = g.bank()
        P.mm([lambda e, q=q: e.matmul(pt2[:, q:q + 1], lhsT=stg[0:1, q * 128:(q + 1) * 128], rhs=one11, start=True, stop=True)
              for q in range(4)], reads=(b_stg, g.kbuf), writes=(pb2,))
        P.op("dve", lambda e: e.tensor_copy(out=g.modc[:, si, j4 * 4:(j4 + 1) * 4], in_=pt2[:, 0:4]), reads=(pb2,), writes=(g.b_modc,))
    if nt == 7:
        P.op("dve", lambda e: e.scalar_tensor_tensor(out=g.AB[:, 0, :], in0=g.modc[:, 1, :], scalar=1.0,
                                                     in1=g.vec[:, V_PRE:V_PRE + 16], op0=ALU.add, op1=ALU.mult),
             reads=(g.b_modc, g.kbuf), writes=(g.b_modc,))
        P.op("dve", lambda e: e.tensor_copy(out=g.AB[:, 1, :], in_=g.modc[:, 0, :]), reads=(g.b_modc,), writes=(g.b_modc,))
    if nt == 19:
        P.op("dve", lambda e: e.scalar_tensor_tensor(out=g.AB[:, 2, :], in0=g.modc[:, 3, :], scalar=1.0,
                                                     in1=g.vec[:, V_MLPPRE:V_MLPPRE + 16], op0=ALU.add, op1=ALU.mult),
             reads=(g.b_modc, g.kbuf), writes=(g.b_modc,))
        P.op("dve", lambda e: e.tensor_copy(out=g.AB[:, 3, :], in_=g.modc[:, 2, :]), reads=(g.b_modc,), writes=(g.b_modc,))


def phase0_hook(g):
    if getattr(g, "p0_next", 24) < 24:
        phase0_tile(g)


def norm_stats(g, xs, b_xs, ntile, junk, st, b_st):
    P = g.P
    for j in range(ntile):
        P.op("act", lambda e, j=j: e.activation(out=junk[:], in_=xs[:, j, :], func=AF.Square, accum_out=st[:, j:j + 1]),
             reads=(b_xs,), writes=(b_st,))
    P.op("dve", lambda e: e.tensor_scalar(out=st[:, ntile:2 * ntile], in0=st[:, 0:ntile], scalar1=1.0 / D, scalar2=EPS,
                                          op0=ALU.mult, op1=ALU.add), reads=(b_st,), writes=(b_st,))
    P.op("act", lambda e: e.activation(out=st[:, 2 * ntile:3 * ntile], in_=st[:, ntile:2 * ntile], func=AF.Sqrt),
         reads=(b_st,), writes=(b_st,))
    P.op("dve", lambda e: e.reciprocal(out=st[:, 3 * ntile:4 * ntile], in_=st[:, 2 * ntile:3 * ntile]),
         reads=(b_st,), writes=(b_st,))
    for j in range(ntile):
        eng = "act" if j % 2 == 0 else "dve"
        if eng == "act":
            P.op("act", lambda e, j=j: e.activation(out=xs[:, j, :], in_=xs[:, j, :], func=AF.Copy,
                                                    scale=st[:, 3 * ntile + j:3 * ntile + j + 1]),
                 reads=(b_st, b_xs), writes=(b_xs,))
        else:
            P.op("dve", lambda e, j=j: e.tensor_scalar(out=xs[:, j, :], in0=xs[:, j, :],
                                                       scalar1=st[:, 3 * ntile + j:3 * ntile + j + 1], scalar2=None, op0=ALU.mult),
                 reads=(b_st, b_xs), writes=(b_xs,))


def norm_xpose(g, xs, b_xs, ntile, AB_a, AB_b, dst_fn):
    P = g.P
    for fc in range(KC):
        pt, pb = g.bank()
        mms = [lambda e, j=j, fc=fc: e.transpose(pt[:, j * 128:(j + 1) * 128], xs[:, j, fc * 128:(fc + 1) * 128], g.ident)
               for j in range(ntile)]
        P.mm(mms, reads=(b_xs, g.kbuf), writes=(pb,))
        n = ntile * 128
        if fc % 2 == 0:
            P.op("act", lambda e, fc=fc: e.activation(out=dst_fn(fc), in_=pt[:, 0:n], func=AF.Identity,
                                                      scale=AB_a[:, fc:fc + 1], bias=AB_b[:, fc:fc + 1]),
                 reads=(pb, g.b_modc))
        else:
            P.op("dve", lambda e, fc=fc: e.tensor_scalar(out=dst_fn(fc), in0=pt[:, 0:n], scalar1=AB_a[:, fc:fc + 1],
                                                         scalar2=AB_b[:, fc:fc + 1], op0=ALU.mult, op1=ALU.add),
                 reads=(pb, g.b_modc))


def norm_transpose(g, xs, b_xs, ntile, AB_a, AB_b, dst_fn, junk, st, b_st):
    norm_stats(g, xs, b_xs, ntile, junk, st, b_st)
    norm_xpose(g, xs, b_xs, ntile, AB_a, AB_b, dst_fn)


def phase1(g):
    P = g.P
    g.RA.reset()
    g.hT = P.sb(g.RA, [128, KC, TC], BF16, "hT")
    R = g.RBC
    R.reset()
    NB = 3
    xsN = [P.sb(R, [128, 2, D], F32, f"xs{i}") for i in range(NB)]
    bx = [Buf(f"xs{i}") for i in range(NB)]
    stN = [P.sb(R, [128, 8], F32, f"st{i}") for i in range(NB)]
    bst = [Buf(f"st{i}") for i in range(NB)]
    junk = P.sb(R, [128, D], BF16, "junk")
    xv = g.xctx.ap().rearrange("(t p) d -> p t d", p=128)

    def stage_a(gi):
        xs, b_xs = xsN[gi % NB], bx[gi % NB]
        P.dma("sp", xs[:], xv[:, gi * 2:gi * 2 + 2, :], writes=(b_xs,))
        norm_stats(g, xs, b_xs, 2, junk, stN[gi % NB], bst[gi % NB])

    def stage_b(gi):
        norm_xpose(g, xsN[gi % NB], bx[gi % NB], 2, g.AB[:, 0, :], g.AB[:, 1, :],
                   lambda fc, gi=gi: g.hT[:, fc, gi * 256:(gi + 1) * 256])
    stage_a(0)
    for gi in range(8):
        if gi + 1 < 8:
            stage_a(gi + 1)
        stage_b(gi)


def plan_weights_p2(g):
    W = g.W
    wv = lambda w, c0, n: w.ap().rearrange("(kc p) n -> p kc n", p=128)[:, :, c0:c0 + n]
    full = lambda c0, n: [(lambda t, n=n: slot_view(t, 16, n), wv(g.w_in, c0, n))]
    for c0 in (0, 512):
        W.add(full(c0, 512))
    for c0 in (1024, 1536):
        W.add(full(c0, 512))
    for c0 in (3072, 3584):
        W.add(full(c0, 512))
    for c0 in (2048, 2560):
        W.add(full(c0, 512))
    W.add(full(4096, 512)); W.add(full(4608, 256))
    W.add(full(4864, 512)); W.add(full(5376, 256))
    for gi in range(3):
        W.add(full(5632 + gi * 256, 256))
    for i in range(3):
        W.add([(lambda t: slot_view(t, 16, 256, c0=0), wv(g.w_in, 6400 + i * 256, 256)),
               (lambda t: slot_view(t, 16, 256, c0=256), wv(g.w_in, 7168 + i * 256, 256))])
    for i in range(12):
        W.add([(lambda t: slot_view(t, 16, 512), wv(g.w_gate, i * 512, 512))])


def phase2(g):
    P = g.P
    R = g.RBC
    R.reset()
    hT = g.hT
    NSTG = 8
    stg = [(P.sb(R, [128, 512], F32, f"stg{i}"), Buf(f"stg{i}")) for i in range(NSTG)]
    tmp = [(P.sb(R, [128, 512], F32, f"tmp{i}"), Buf(f"tmp{i}")) for i in range(4)]
    onesv = P.sb(R, [128, 256], F32, "onesv")
    b_ones = Buf("onesv")
    P.op("dve", lambda e: e.memset(onesv[:], 1.0), writes=(b_ones,))
    st = {"i": 0, "t": 0, "e": 0}

    def nstg():
        s = stg[st["i"] % NSTG]
        st["i"] += 1
        return s

    def ntmp():
        s = tmp[st["t"] % 4]
        st["t"] += 1
        return s

    def eng2():
        st["e"] += 1
        return "act" if st["e"] % 2 == 0 else "dve"

    OWN = [(1024, 512), (1536, 512)]
    CTX = [(0, 512), (512, 512), (1024, 512), (1536, 512)]

    def fm(slot, c0, ncols, toks, evac):
        stt, sbuf = slot
        sv = slot_view(stt, 16, 512)
        for j in range(ncols // 128):
            for (t0, nt) in toks:
                pt, pb = g.bank()
                mms = [lambda e, kc=kc, j=j, t0=t0, nt=nt: e.matmul(
                    pt[:, 0:nt], lhsT=sv[:, kc, c0 + j * 128:c0 + (j + 1) * 128], rhs=hT[:, kc, t0:t0 + nt],
                    start=(kc == 0), stop=(kc == KC - 1)) for kc in range(KC)]
                P.mm(mms, reads=(sbuf,), writes=(pb,))
                evac(pt, pb, j, t0, nt)

    def spill(dram_ap, src_ap, b_src):
        P.dma("sp", dram_ap, src_ap, reads=(b_src,))

    def ev_aq(f0):
        def ev(pt, pb, j, t0, nt):
            s, sb_ = nstg()
            sv = s.bitcast(BF16)
            P.op("act", lambda e: e.activation(out=sv[:, 0:nt], in_=pt[:, 0:nt], func=AF.Silu), reads=(pb,), writes=(sb_,))
            spill(g.AQ.ap()[f0 + j * 128:f0 + (j + 1) * 128, t0 - 1024:t0 - 1024 + nt], sv[:, 0:nt], sb_)
        return ev
    for i in range(2):
        fm(g.W.get(), 0, 512, OWN, ev_aq(i * 512))

    def ev_af(f0):
        def ev(pt, pb, j, t0, nt):
            s, sb_ = nstg()
            en = eng2()
            if en == "act":
                P.op("act", lambda e: e.copy(out=s[:, 0:nt], in_=pt[:, 0:nt]), reads=(pb,), writes=(sb_,))
            else:
                P.op("dve", lambda e: e.tensor_copy(out=s[:, 0:nt], in_=pt[:, 0:nt]), reads=(pb,), writes=(sb_,))
            spill(g.AF.ap()[f0 + j * 128:f0 + (j + 1) * 128, t0:t0 + nt], s[:, 0:nt], sb_)
        return ev
    for i in range(2):
        fm(g.W.get(), 0, 512, CTX, ev_af(i * 512))

    def ev_ag(f0):
        def ev(pt, pb, j, t0, nt):
            t_, tb = ntmp()
            s, sb_ = nstg()
            sv = s.bitcast(BF16)
            P.op("act", lambda e: e.activation(out=t_[:, 0:nt], in_=pt[:, 0:nt], func=AF.Silu), reads=(pb,), writes=(tb,))
            P.op("dve", lambda e: e.tensor_scalar(out=sv[:, 0:nt], in0=t_[:, 0:nt], scalar1=g.vec[:, V_NORMW:V_NORMW + 1],
                                                  scalar2=None, op0=ALU.mult), reads=(tb, g.kbuf), writes=(sb_,))
            spill(g.AG.ap()[f0 + j * 128:f0 + (j + 1) * 128, t0 - 1024:t0 - 1024 + nt], sv[:, 0:nt], sb_)
        return ev
    for i in range(2):
        fm(g.W.get(), 0, 512, OWN, ev_ag(i * 512))

    for i in range(2):
        stt, sbuf = g.W.get()
        sv = slot_view(stt, 16, 512)
        for ti in range(16):
            pt, pb = g.bank()
            mms = [lambda e, kc=kc, ti=ti: e.matmul(pt[:, :], lhsT=hT[:, kc, ti * 128:(ti + 1) * 128], rhs=sv[:, kc, :],
                                                   start=(kc == 0), stop=(kc == KC - 1)) for kc in range(KC)]
            P.mm(mms, reads=(sbuf,), writes=(pb,))
            s, sb_ = nstg()
            sv2 = s.bitcast(BF16)
            mc = 0 if ti < 8 else 1
            en = eng2()
            if en == "act":
                P.op("act", lambda e: e.activation(out=sv2[:, 0:512], in_=pt[:, :], func=AF.Copy, scale=g.tm[:, mc:mc + 1]),
                     reads=(pb, g.kbuf), writes=(sb_,))
            else:
                P.op("dve", lambda e: e.tensor_scalar(out=sv2[:, 0:512], in0=pt[:, :], scalar1=g.tm[:, mc:mc + 1], scalar2=None,
                                                      op0=ALU.mult), reads=(pb, g.kbuf), writes=(sb_,))
            spill(g.AI.ap()[ti * 128:(ti + 1) * 128, i * 512:(i + 1) * 512], sv2[:, 0:512], sb_)

    def ev_b(dst, f0, scale, own):
        def ev(pt, pb, j, t0, nt):
            s, sb_ = nstg()
            sv = s.bitcast(BF16)
            en = eng2()
            if en == "act":
                P.op("act", lambda e: e.activation(out=sv[:, 0:nt], in_=pt[:, 0:nt], func=AF.Copy, scale=scale), reads=(pb,), writes=(sb_,))
            else:
                P.op("dve", lambda e: e.tensor_scalar(out=sv[:, 0:nt], in0=pt[:, 0:nt], scalar1=scale, scalar2=None, op0=ALU.mult),
                     reads=(pb,), writes=(sb_,))
            tt = t0 - 1024 if own else t0
            spill(dst.ap()[f0 + j * 128:f0 + (j + 1) * 128, tt:tt + nt], sv[:, 0:nt], sb_)
        return ev
    fm(g.W.get(), 0, 512, OWN, ev_b(g.BQ, 0, 0.125, True))
    fm(g.W.get(), 0, 256, OWN, ev_b(g.BQ, 512, 0.125, True))
    fm(g.W.get(), 0, 512, CTX, ev_b(g.BK, 0, 1.0, False))
    fm(g.W.get(), 0, 256, CTX, ev_b(g.BK, 512, 1.0, False))

    for gi, dil in enumerate((1, 4, 16)):
        stt, sbuf = g.W.get()
        sv = slot_view(stt, 16, 256)
        for bi in range(16):
            if gi == 0:
                start, mc = bi * 128, (0 if bi < 8 else 1)
            elif gi == 1:
                r, n = bi // 4, bi % 4
                start, mc = n * 512 + r, (0 if n < 2 else 1)
            else:
                start, mc = bi, 2
            pt, pb = g.bank()
            mms = [lambda e, kc=kc, start=start, dil=dil: e.matmul(
                pt[:, 0:256], lhsT=hT[:, kc, start:start + 127 * dil + 1:dil], rhs=sv[:, kc, :],
                start=(kc == 0), stop=(kc == KC - 1)) for kc in range(KC)]
            P.mm(mms, reads=(sbuf,), writes=(pb,))
            s, sb_ = nstg()
            sv2 = s.bitcast(BF16)
            vdst = bass.AP(sv2, 0, [[1024, 128], [256, 2], [192, 2], [1, 64]])
            mdst = bass.AP(sv2, 64, [[1024, 128], [256, 2], [64, 2], [1, 64]])
            vsrc = bass.AP(pt, 0, [[512, 128], [128, 2], [64, 2], [1, 64]])
            osrc = bass.AP(onesv, 0, [[256, 128], [128, 2], [64, 2], [1, 64]])
            P.op("dve", lambda e: e.tensor_scalar(out=vdst, in0=vsrc, scalar1=g.tm[:, mc:mc + 1], scalar2=None, op0=ALU.mult),
                 reads=(pb, g.kbuf), writes=(sb_,))
            P.op("dve", lambda e: e.tensor_scalar(out=mdst, in0=osrc, scalar1=g.tm[:, mc:mc + 1], scalar2=None, op0=ALU.mult),
                 reads=(b_ones, g.kbuf, sb_), writes=(sb_,))
            spill(g.BV.ap()[gi, bi], sv2[:, 0:512], sb_)

    CT = [(994, 30), (1024, 512), (1536, 512)]
    for i in range(3):
        slot = g.W.get()
        stt, sbuf = slot
        sv = slot_view(stt, 16, 512)
        for j in range(2):
            for (t0, nt) in CT:
                pg, pgb = g.bank()
                P.mm([lambda e, kc=kc: e.matmul(pg[:, 0:nt], lhsT=sv[:, kc, 256 + j * 128:256 + (j + 1) * 128], rhs=hT[:, kc, t0:t0 + nt],
                                               start=(kc == 0), stop=(kc == KC - 1)) for kc in range(KC)], reads=(sbuf,), writes=(pgb,))
                pa, pab = g.bank()
                P.mm([lambda e, kc=kc: e.matmul(pa[:, 0:nt], lhsT=sv[:, kc, j * 128:(j + 1) * 128], rhs=hT[:, kc, t0:t0 + nt],
                                               start=(kc == 0), stop=(kc == KC - 1)) for kc in range(KC)], reads=(sbuf,), writes=(pab,))
                t_, tb = ntmp()
                s, sb_ = nstg()
                sv2 = s.bitcast(BF16)
                P.op("act", lambda e: e.activation(out=t_[:, 0:nt], in_=pg[:, 0:nt], func=AF.Sigmoid), reads=(pgb,), writes=(tb,))
                if nt == 30:
                    P.op("dve", lambda e: e.scalar_tensor_tensor(out=sv2[:, 0:nt], in0=pa[:, 0:nt], scalar=g.tm[:, 3:4], in1=t_[:, 0:nt],
                                                                 op0=ALU.mult, op1=ALU.mult), reads=(pab, tb, g.kbuf), writes=(sb_,))
                    c0 = 2
                else:
                    P.op("dve", lambda e: e.tensor_tensor(out=sv2[:, 0:nt], in0=pa[:, 0:nt], in1=t_[:, 0:nt], op=ALU.mult),
                         reads=(pab, tb), writes=(sb_,))
                    c0 = 32 + t0 - 1024
                f0 = i * 256 + j * 128
                spill(g.U.ap()[f0:f0 + 128, c0:c0 + nt], sv2[:, 0:nt], sb_)

    for i in range(12):
        def ev(pt, pb, j, t0, nt, i=i):
            s, sb_ = nstg()
            sv = s.bitcast(BF16)
            fcol = V_BGATE + i * 4 + j
            P.op("act", lambda e: e.activation(out=sv[:, 0:nt], in_=pt[:, 0:nt], func=AF.Sigmoid, bias=g.vec[:, fcol:fcol + 1]),
                 reads=(pb, g.kbuf), writes=(sb_,))
            f0 = i * 512 + j * 128
            spill(g.GATES.ap()[f0:f0 + 128, t0 - 1024:t0 - 1024 + nt], sv[:, 0:nt], sb_)
        fm(g.W.get(), 0, 512, OWN, ev)

    for name, src in (("AQ", g.AQ), ("AF", g.AF), ("AG", g.AG), ("AI", g.AI), ("BQ", g.BQ), ("BK", g.BK), ("BV", g.BV),
                      ("U", g.U), ("GATES", g.GATES)):
        t = g.tap(name, src.shape, src.dtype)
        if t is not None:
            P.barrier()
            P.dma("sp", t.ap(), src.ap())


def phase3_hgrn(g):
    P = g.P
    g.RBC.reset(); g.RA2.reset()
    RR = [g.RBC, g.RA2]
    sb = lambda shape, dt, name: P.sb(RR, shape, dt, name)
    YT = g.YT
    cst = g.cst
    S = sb([128, 8, 128], F32, "S"); Sbf = sb([128, 8, 128], BF16, "Sbf")
    vseg = sb([64, 8, 1024], BF16, "vseg")
    QpT = sb([128, 8, 512], BF16, "QpT"); KpT = sb([128, 8, 512], BF16, "KpT")
    Q2T = sb([128, 8, 512], BF16, "Q2T"); bQ2 = Buf("Q2")
    Ktok = sb([64, 8, 8, 128], BF16, "Ktok")
    attnT = sb([64, 8, 8, 64], BF16, "attnT")
    gseg = sb([128, 8, 512], BF16, "gseg")
    dec = sb([128, 8, 8], F32, "dec")
    onesb = sb([128, 128], BF16, "onesb")
    zt = [sb([128, 512], F32, f"zt{i}") for i in range(2)]
    qh = [sb([128, 512], BF16, f"qh{i}") for i in range(2)]
    tf = sb([128, 512], F32, "tf"); tl = sb([128, 512], F32, "tl"); tk = sb([128, 512], F32, "tk")
    tA = sb([128, 512], F32, "tA"); tB = sb([128, 512], F32, "tB"); tC = sb([128, 512], F32, "tC")
    KppT = sb([128, 512], BF16, "KppT")
    RX = Region(g.RA1.base + 16384, 16384)
    T1t = [P.sb(RX, [128, 512], F32, f"t1_{i}") for i in range(6)] + [P.sb(RX, [128, 512], BF16, "KppT1")]
    rr, to = tk, tf
    osq2 = [KppT, sb([128, 512], BF16, "osqB")]
    bS, bSbf, bV, bQ, bK, bKt, bAt, bG, bDec, bOnes = (Buf(n) for n in "S Sbf V Q K Kt At G Dec Ones".split())
    bz = [Buf("z0"), Buf("z1")]; bq = [Buf("q0"), Buf("q1")]
    btf, btl, btk, btA, btB, btC, bKpp = (Buf(n) for n in "tf tl tk tA tB tC Kpp".split())
    brr, bto = btk, btf
    TS = [dict(t=(tf, tl, tk, tA, tB, tC, KppT), b=(btf, btl, btk, btA, btB, btC, bKpp)),
          dict(t=tuple(T1t), b=tuple(Buf(f"t1b{i}") for i in range(7)))]
    bosq2 = [bKpp, Buf("osqB")]
    resetm = cst[:, C_RESET:C_RESET + 512]
    caus = bass.AP(cst, C_CAUS, [[NCONST, 64], [0, 8], [1, 64]])
    P.op("dve", lambda e: e.memset(S[:], 0.0), writes=(bS,))
    P.op("dve", lambda e: e.memset(Sbf[:], 0.0), writes=(bSbf,))
    P.op("dve", lambda e: e.memset(onesb[:], 1.0), writes=(bOnes,))
    v3 = lambda t: t[:].rearrange("p (c s) -> p c s", s=64)
    hcount = 0
    for sg in range(4):
        own = sg >= 2
        o0 = (sg - 2) * 512
        P.dma("sp", vseg[:], g.AI.ap()[sg * 512:(sg + 1) * 512, :].rearrange("(c p) n -> p c n", p=64), writes=(bV,))
        if own:
            P.dma("sp", gseg[:], g.AG.ap().rearrange("(h v) t -> v h t", v=128)[:, :, o0:o0 + 512], writes=(bG,))
        def head_stages(h, T, z, b_z, q, b_q):
            tf, tl, tk, tA, tB, tC, KppT = T["t"]
            btf, btl, btk, btA, btB, btC, bKpp = T["b"]
            lb = g.lbv[:, 0, h:h + 1]; oml = g.lbv[:, 1, h:h + 1]
            Bl = v3(tl)[:, :, 63:64]
            Bm = v3(tl)[:, :, 31:32]
            L = []
            L.append(lambda: P.op("act", lambda e: e.activation(out=tf[:], in_=z[:], func=AF.Exp, scale=-1.0), reads=(b_z,), writes=(btf,)))
            L.append(lambda: P.op("dve", lambda e: e.tensor_scalar(out=tf[:], in0=tf[:], scalar1=1.0, scalar2=None, op0=ALU.add), reads=(btf,), writes=(btf,)))
            L.append(lambda: P.op("dve", lambda e: e.reciprocal(out=tf[:], in_=tf[:]), reads=(btf,), writes=(btf,)))
            L.append(lambda: P.op("dve", lambda e: e.tensor_scalar(out=tf[:], in0=tf[:], scalar1=oml, scalar2=lb, op0=ALU.mult, op1=ALU.add),
                                  reads=(btf, g.b_lb), writes=(btf,)))
            L.append(lambda: P.op("act", lambda e: e.activation(out=tl[:], in_=tf[:], func=AF.Ln), reads=(btf,), writes=(btl,)))
            L.append(lambda: P.op("dve", lambda e: e.tensor_scalar(out=tk[:], in0=tf[:], scalar1=-1.0, scalar2=1.0, op0=ALU.mult, op1=ALU.add),
                                  reads=(btf,), writes=(btk,)))
            L.append(lambda: P.op("dve", lambda e: e.tensor_tensor_scan(out=tl[:], data0=resetm, data1=tl[:], initial=0.0, op0=ALU.mult, op1=ALU.add),
                                  reads=(btl, g.kbuf), writes=(btl,)))
            L.append(lambda: P.op("dve", lambda e: e.tensor_tensor(out=v3(tA), in0=Bl.to_broadcast([128, 8, 64]), in1=v3(tl), op=ALU.subtract),
                                  reads=(btl,), writes=(btA,)))
            L.append(lambda: P.op("act", lambda e: e.activation(out=tA[:], in_=tA[:], func=AF.Exp), reads=(btA,), writes=(btA,)))
            L.append(lambda: P.op("dve", lambda e: e.tensor_tensor(out=KppT[:], in0=tk[:], in1=tA[:], op=ALU.mult), reads=(btk, btA), writes=(bKpp,)))
            L.append(lambda: P.op("act", lambda e: e.activation(out=dec[:, :, h], in_=v3(tl)[:, :, 63], func=AF.Exp), reads=(btl,), writes=(bDec,)))
            if own:
                L.append(lambda: P.op("dve", lambda e: e.tensor_tensor(out=v3(tB), in0=v3(tl), in1=Bm.to_broadcast([128, 8, 64]), op=ALU.subtract),
                                      reads=(btl,), writes=(btB,)))
                L.append(lambda: P.op("act", lambda e: e.activation(out=tC[:], in_=tB[:], func=AF.Exp), reads=(btB,), writes=(btC,)))
                L.append(lambda: P.op("act", lambda e: e.activation(out=tB[:], in_=tB[:], func=AF.Exp, scale=-1.0), reads=(btB,), writes=(btB,)))
                L.append(lambda: P.op("dve", lambda e: e.tensor_tensor(out=QpT[:, h, :], in0=q[:], in1=tC[:], op=ALU.mult), reads=(b_q, btC), writes=(bQ,)))
                L.append(lambda: P.op("dve", lambda e: e.tensor_tensor(out=KpT[:, h, :], in0=tk[:], in1=tB[:], op=ALU.mult), reads=(btk, btB), writes=(bK,)))
                L.append(lambda: P.op("act", lambda e: e.activation(out=tA[:], in_=tl[:], func=AF.Exp), reads=(btl,), writes=(btA,)))
                L.append(lambda: P.op("dve", lambda e: e.tensor_tensor(out=Q2T[:, h, :], in0=q[:], in1=tA[:], op=ALU.mult), reads=(b_q, btA), writes=(bQ2,)))

            def pe_part():
                pt, pb = g.bank()
                ptb = pt.bitcast(BF16)
                P.mm([lambda e, c=c: e.transpose(ptb[0:64, c * 128:(c + 1) * 128], KppT[:, c * 64:(c + 1) * 64], g.identb[:])
                      for c in range(8)], reads=(bKpp, g.kbuf), writes=(pb,))
                P.op("act", lambda e: e.copy(out=Ktok[:, h, :, :].rearrange("p c k -> p (c k)"), in_=ptb[0:64, 0:1024]),
                     reads=(pb,), writes=(bKt,))
                if own:
                    pt2, pb2 = g.bank()
                    P.mm([lambda e, c=c: e.matmul(pt2[0:64, c * 64:(c + 1) * 64], lhsT=KpT[:, h, c * 64:(c + 1) * 64],
                                                  rhs=QpT[:, h, c * 64:(c + 1) * 64], start=True, stop=True) for c in range(8)],
                         reads=(bK, bQ), writes=(pb2,))
                    P.op("dve", lambda e: e.tensor_tensor(out=attnT[:, h, :, :], in0=pt2[0:64, :].rearrange("p (c t) -> p c t", t=64),
                                                          in1=caus, op=ALU.mult), reads=(pb2, g.kbuf), writes=(bAt,))
            L.append(pe_part)
            return L

        for hp in range(4):
            LL = []
            for i in range(2):
                h = 2 * hp + i
                P.dma("sp", zt[i][:], g.AF.ap()[h * 128:(h + 1) * 128, sg * 512:(sg + 1) * 512], writes=(bz[i],))
                if own:
                    P.dma("sp", qh[i][:], g.AQ.ap()[h * 128:(h + 1) * 128, o0:o0 + 512], writes=(bq[i],))
                LL.append(head_stages(h, TS[i], zt[i], bz[i], qh[i], bq[i]))
            for a, b in zip(*LL):
                a()
                b()
            phase0_hook(g)
        pending = None
        for c in range(8):
            last = (sg == 3 and c == 7)
            if not last:
                pdA, pdAb = g.bank()
                pdB, pdBb = g.bank()
                mms = []
                for h in range(8):
                    pd = pdA if h < 4 else pdB
                    hh = h % 4
                    mms.append(lambda e, h=h, pd=pd, hh=hh: e.matmul(pd[:, hh * 128:(hh + 1) * 128], lhsT=Ktok[0:64, h, c, :],
                                                                       rhs=vseg[0:64, c, h * 128:(h + 1) * 128], start=True, stop=True))
                P.mm(mms, reads=(bKt, bV), writes=(pdAb, pdBb))
            if own:
                po, pob = g.bank()
                mms = []
                for h in range(8):
                    mms.append(lambda e, h=h: e.matmul(po[:, h * 64:(h + 1) * 64], lhsT=vseg[0:64, c, h * 128:(h + 1) * 128],
                                                       rhs=attnT[0:64, h, c, :], start=True, stop=False))
                    mms.append(lambda e, h=h: e.matmul(po[:, h * 64:(h + 1) * 64], lhsT=Sbf[:, h, :],
                                                       rhs=Q2T[:, h, c * 64:(c + 1) * 64], start=False, stop=True))
                P.mm(mms, reads=(bV, bAt, bSbf, bQ2), writes=(pob,))
                oq, boq = osq2[c % 2], bosq2[c % 2]
                P.op("act", lambda e: e.activation(out=oq[:], in_=po[:, :], func=AF.Square), reads=(pob,), writes=(boq,))
            if not last:
                P.op("dve", lambda e: e.tensor_tensor(out=S[:], in0=S[:], in1=dec[:, c, :].unsqueeze(2).to_broadcast([128, 8, 128]),
                                                      op=ALU.mult), reads=(bS, bDec), writes=(bS,))
                P.op("dve", lambda e: e.tensor_tensor(out=S[:, 0:4, :], in0=S[:, 0:4, :], in1=pdA[:, :].rearrange("p (h v) -> p h v", v=128),
                                                      op=ALU.add), reads=(bS, pdAb), writes=(bS,))
                P.op("dve", lambda e: e.tensor_tensor(out=S[:, 4:8, :], in0=S[:, 4:8, :], in1=pdB[:, :].rearrange("p (h v) -> p h v", v=128),
                                                      op=ALU.add), reads=(bS, pdBb), writes=(bS,))
                P.op("act", lambda e: e.copy(out=Sbf[:], in_=S[:]), reads=(bS,), writes=(bSbf,))

            def tail(po, pob, oq, boq, c):
                ps, psb = g.bank()
                P.mm([lambda e: e.matmul(ps[:, :], lhsT=onesb[:], rhs=oq[:], start=True, stop=True)], reads=(boq, bOnes), writes=(psb,))
                P.op("act", lambda e: e.activation(out=rr[:], in_=ps[:, :], func=AF.Sqrt, scale=1.0 / 128, bias=EPS), reads=(psb,), writes=(brr,))
                P.op("dve", lambda e: e.reciprocal(out=rr[:], in_=rr[:]), reads=(brr,), writes=(brr,))
                P.op("dve", lambda e: e.tensor_tensor(out=to[:], in0=po[:, :], in1=rr[:], op=ALU.mult), reads=(pob, brr), writes=(bto,))
                tok0 = o0 + c * 64
                P.op("dve", lambda e: e.tensor_tensor(out=YT[:, 0:8, tok0:tok0 + 64], in0=to[:].rearrange("p (h t) -> p h t", t=64),
                                                      in1=gseg[:, :, c * 64:(c + 1) * 64], op=ALU.mult), reads=(bto, bG))
            if pending is not None:
                tail(*pending)
                pending = None
            if own:
                pending = (po, pob, oq, boq, c)
        if pending is not None:
            tail(*pending)
            pending = None
    t = g.tap("YA", [1024, TO], BF16)
    if t is not None:
        P.barrier()
        P.dma("sp", t.ap().rearrange("(h v) t -> v h t", v=128), YT[:, 0:8, :])


def build_bias_tables(g):
    P = g.P
    R = g.RBC
    R.reset()
    relb = P.sb(R, [32, 12], F32, "relb")
    relrep = P.sb(R, [32, 12, 128], F32, "relrep")
    oh = P.sb(R, [32, 3, 383], F32, "oh")
    fr = [P.sb(R, [128, 383], F32, f"fr{i}") for i in range(2)]
    bfr = [Buf("fr0"), Buf("fr1")]
    b_in = Buf("relb")
    P.dma("sp", relb[:], g.rel_bias.ap(), writes=(b_in,))
    P.dma("sp", oh[:], g.onehot.ap().rearrange("g b j -> b g j"), writes=(b_in,), add_write=True)
    P.op("dve", lambda e: e.tensor_copy(out=relrep[:], in_=relb[:].unsqueeze(2).to_broadcast([32, 12, 128])), reads=(b_in,), writes=(b_in,))
    for h in range(12):
        gi = h // 4
        pt, pb = g.bank()
        P.mm([lambda e: e.matmul(pt[:, 0:383], lhsT=relrep[:, h, :], rhs=oh[:, gi, :], start=True, stop=True)], reads=(b_in,), writes=(pb,))
        f, bf_ = fr[h % 2], bfr[h % 2]
        P.op("act", lambda e: e.activation(out=f[:], in_=pt[:, 0:383], func=AF.Exp), reads=(pb,), writes=(bf_,))
        P.op("dve", lambda e: e.tensor_tensor(out=f[:], in0=f[:], in1=g.cst[:, C_FVALID:C_FVALID + 383], op=ALU.mult),
             reads=(bf_, g.kbuf), writes=(bf_,))
        P.dma("sp", g.FT.ap()[h].rearrange("(p j) -> p j", j=383), f[:], reads=(bf_,))


def phase3_attn(g):
    P = g.P
    g.RBC.reset(); g.RA2.reset()
    RR = [g.RBC, g.RA2]
    sb = lambda shape, dt, name: P.sb(RR, shape, dt, name)
    YT = g.YT
    EB = sb([128, 12, 2, 128], F32, "EB")
    QT = sb([64, 4, TO], BF16, "QT"); KT = sb([64, 4, TC], BF16, "KT")
    Vg = sb([128, 16, 512], BF16, "Vg")
    UZ = sb([128, 4, TO], F32, "UZ")
    pe = [sb([128, 512], F32, f"pe{i}") for i in range(2)]
    pT = [sb([128, 512], BF16, f"pT{i}") for i in range(2)]
    rz = sb([128, 512], F32, "rz")
    bEB, bQ, bK, bV, bUZ, brz = (Buf(n) for n in "EB Q K V UZ rz".split())
    bpe = [Buf("pe0"), Buf("pe1")]; bpT = [Buf("pT0"), Buf("pT1")]
    first = True
    for h in range(12):
        for ty in range(2):
            src = bass.AP(g.FT, h * 128 * 383 + (127 if ty == 1 else 255), [[382, 128], [1, 128]])
            P.dma("sp", EB[:, h, ty, :], src, writes=(bEB,), add_write=not first)
            first = False
    cnt = 0
    pendingB = None
    for gi, dil in enumerate((1, 4, 16)):
        P.dma("sp", QT[:], g.BQ.ap()[gi * 256:(gi + 1) * 256, :].rearrange("(h d) t -> d h t", d=64), writes=(bQ,))
        P.dma("sp", KT[:], g.BK.ap()[gi * 256:(gi + 1) * 256, :].rearrange("(h d) t -> d h t", d=64), writes=(bK,))
        P.dma("sp", Vg[:], g.BV.ap()[gi].rearrange("b p n -> p b n"), writes=(bV,))
        for hh in range(4):
            h = gi * 4 + hh
            if gi < 2:
                nbatch = 4
            else:
                nbatch = 2
            for bt in range(nbatch):
                pt, pb = g.bank()
                po, pob = g.bank()
                e_, be_ = pe[cnt % 2], bpe[cnt % 2]
                p_, bp_ = pT[cnt % 2], bpT[cnt % 2]
                cnt += 1
                mm1, mm2 = [], []
                if gi == 0:
                    for j in range(2):
                        qb = bt * 2 + j
                        qs = QT[:, hh, qb * 128:(qb + 1) * 128]
                        for ty in range(2):
                            k0 = 896 + qb * 128 + ty * 128
                            reg = (j * 2 + ty) * 128
                            mm1.append(lambda e, k0=k0, reg=reg, qs=qs: e.matmul(pt[:, reg:reg + 128], lhsT=KT[:, hh, k0:k0 + 128], rhs=qs,
                                                                                 start=True, stop=True))
                            blk = 7 + qb + ty
                            mm2.append(lambda e, blk=blk, reg=reg, j=j, ty=ty, po=po, p_=p_, hh=hh: e.matmul(
                                po[:, j * 128:(j + 1) * 128], lhsT=Vg[:, blk, hh * 128:(hh + 1) * 128], rhs=p_[:, reg:reg + 128],
                                start=(ty == 0), stop=(ty == 1)))
                    eb_in = bass.AP(EB, h * 256, [[12 * 256, 128], [0, 2], [1, 256]])
                    uz_view = UZ[:, hh, bt * 256:(bt + 1) * 256]
                    ncol = 512
                elif gi == 1:
                    r = bt
                    for j in range(2):
                        n = 2 + j
                        qs = QT[:, hh, j * 512 + r:j * 512 + r + 509:4]
                        for ty in range(2):
                            m = n - 1 + ty
                            k0 = m * 512 + r
                            reg = (j * 2 + ty) * 128
                            mm1.append(lambda e, k0=k0, reg=reg, qs=qs: e.matmul(pt[:, reg:reg + 128], lhsT=KT[:, hh, k0:k0 + 509:4], rhs=qs,
                                                                                 start=True, stop=True))
                            blk = r * 4 + m
                            mm2.append(lambda e, blk=blk, reg=reg, j=j, ty=ty, po=po, p_=p_, hh=hh: e.matmul(
                                po[:, j * 128:(j + 1) * 128], lhsT=Vg[:, blk, hh * 128:(hh + 1) * 128], rhs=p_[:, reg:reg + 128],
                                start=(ty == 0), stop=(ty == 1)))
                    eb_in = bass.AP(EB, h * 256, [[12 * 256, 128], [0, 2], [1, 256]])
                    uz_view = UZ[:, hh, r:TO:4]
                    ncol = 512
                else:
                    for rr_ in range(8):
                        r = bt * 8 + rr_
                        reg = rr_ * 64
                        mm1.append(lambda e, r=r, reg=reg: e.matmul(pt[:, reg:reg + 64], lhsT=KT[:, hh, r:TC:16], rhs=QT[:, hh, r:TO:16],
                                                                    start=True, stop=True))
                        mm2.append(lambda e, r=r, reg=reg, po=po, p_=p_, hh=hh: e.matmul(po[:, reg:reg + 64], lhsT=Vg[:, r, hh * 128:(hh + 1) * 128],
                                                                    rhs=p_[:, reg:reg + 64], start=True, stop=True))
                    eb_in = bass.AP(EB, h * 256 + 128 + 64, [[12 * 256, 128], [0, 8], [1, 64]])
                    uz_view = bass.AP(UZ, hh * TO + bt * 8, [[4 * TO, 128], [1, 8], [16, 64]])
                    ncol = 512
                P.mm(mm1, reads=(bQ, bK), writes=(pb,))
                P.op("act", lambda e: e.activation(out=e_[:, 0:ncol], in_=pt[:, 0:ncol], func=AF.Exp), reads=(pb,), writes=(be_,))
                if gi < 2:
                    e_v = e_[:, 0:512].rearrange("p (j c) -> p j c", j=2)
                    p_v = p_[:, 0:512].rearrange("p (j c) -> p j c", j=2)
                else:
                    e_v = e_[:, 0:512].rearrange("p (j c) -> p j c", j=8)
                    p_v = p_[:, 0:512].rearrange("p (j c) -> p j c", j=8)
                P.op("dve", lambda e: e.tensor_tensor(out=p_v, in0=e_v, in1=eb_in, op=ALU.mult), reads=(be_, bEB), writes=(bp_,))
                nout = 256 if gi < 2 else 512

                def make_stB(mm2, po, pob, bp_, uz_view, gi, nout):
                    def stB():
                        P.mm(mm2, reads=(bV, bp_), writes=(pob,))
                        if gi == 0:
                            P.op("act", lambda e: e.copy(out=uz_view, in_=po[:, 0:nout]), reads=(pob,), writes=(bUZ,))
                        elif gi == 1:
                            P.op("dve", lambda e: e.tensor_tensor(out=uz_view, in0=po[:, 0:nout], in1=uz_view, op=ALU.add),
                                 reads=(pob, bUZ), writes=(bUZ,))
                        else:
                            P.op("dve", lambda e: e.tensor_tensor(out=uz_view, in0=po[:, 0:nout].rearrange("p (r l) -> p r l", r=8),
                                                                  in1=uz_view, op=ALU.add), reads=(pob, bUZ), writes=(bUZ,))
                    return stB
                if pendingB is not None:
                    pendingB()
                pendingB = make_stB(mm2, po, pob, bp_, uz_view, gi, nout)
        if pendingB is not None:
            pendingB()
            pendingB = None
    for s in range(4):
        for tb in range(2):
            pz, pzb = g.bank()
            P.mm([lambda e: e.matmul(pz[:, :], lhsT=g.cst[:, C_SWAP:C_SWAP + 128], rhs=UZ[:, s, tb * 512:(tb + 1) * 512], start=True, stop=True)],
                 reads=(bUZ, g.kbuf), writes=(pzb,))
            lo = 0 if s % 2 == 0 else 64
            P.op("dve", lambda e: e.reciprocal(out=rz[lo:lo + 64, :], in_=pz[lo:lo + 64, :]), reads=(pzb,), writes=(brz,))
            P.op("dve", lambda e: e.tensor_tensor(out=YT[lo:lo + 64, 8 + s // 2, tb * 512:(tb + 1) * 512], in0=UZ[lo:lo + 64, s, tb * 512:(tb + 1) * 512],
                                                  in1=rz[lo:lo + 64, :], op=ALU.mult), reads=(brz, bUZ))
    t = g.tap("YB", [256, TO], BF16)
    if t is not None:
        P.barrier()
        P.dma("sp", t.ap().rearrange("(c p) t -> p c t", p=128), YT[:, 8:10, :])


def phase3_conv(g):
    P = g.P
    g.RBC.reset(); g.RA2.reset()
    RR = [g.RBC, g.RA2]
    sb = lambda shape, dt, name: P.sb(RR, shape, dt, name)
    YT = g.YT
    UT = sb([128, 6, TO + 32], BF16, "UT")
    yT = sb([128, 6, TO], F32, "yT")
    ybf = sb([128, 6, 512], BF16, "ybf"); ysq = sb([128, 6, 512], BF16, "ysq")
    m_ = sb([128, 512], F32, "m"); v_ = sb([128, 512], F32, "v"); rs = sb([128, 512], F32, "rs")
    tt = [sb([128, 512], F32, f"tt{i}") for i in range(2)]
    onesb = sb([128, 128], BF16, "onesb")
    dg = [sb([128, 128], BF16, f"dg{i}") for i in range(4)]
    bdg = [Buf(f"dg{i}") for i in range(4)]
    bU, by, bybf, bysq, bm, bv, brs, bOnes = (Buf(n) for n in "U y ybf ysq m v rs ones".split())
    btt = [Buf("tt0"), Buf("tt1")]
    P.op("dve", lambda e: e.memset(onesb[:], 1.0), writes=(bOnes,))
    P.dma("sp", UT[:, :, 2:TO + 32], g.U.ap().rearrange("(t p) n -> p t n", p=128)[:, :, 2:TO + 32], writes=(bU,))
    k = 0
    for ct in range(6):
        pA, pAb = g.bank()
        pB, pBb = g.bank()
        for j in range(31):
            d, bd = dg[k % 4], bdg[k % 4]
            k += 1
            col = V_CONVW + ct * 31 + j
            P.op("dve", lambda e: e.tensor_scalar(out=d[:], in0=g.identb[:], scalar1=g.vec[:, col:col + 1], scalar2=None, op0=ALU.mult),
                 reads=(g.kbuf,), writes=(bd,))
            P.mm([lambda e: e.matmul(pA[:, :], lhsT=d[:], rhs=UT[:, ct, 2 + j:2 + j + 512], start=(j == 0), stop=(j == 30)),
                  lambda e: e.matmul(pB[:, :], lhsT=d[:], rhs=UT[:, ct, 512 + 2 + j:512 + 2 + j + 512], start=(j == 0), stop=(j == 30))],
                 reads=(bd, bU), writes=(pAb, pBb))
        cb = g.vec[:, V_CONVB + ct:V_CONVB + ct + 1]
        P.op("act", lambda e: e.activation(out=yT[:, ct, 0:512], in_=pA[:, :], func=AF.Identity, bias=cb), reads=(pAb, g.kbuf), writes=(by,))
        P.op("act", lambda e: e.activation(out=yT[:, ct, 512:1024], in_=pB[:, :], func=AF.Identity, bias=cb), reads=(pBb, g.kbuf, by), writes=(by,))
    for tb in range(2):
        ts = slice(tb * 512, (tb + 1) * 512)
        P.op("act", lambda e: e.copy(out=ybf[:], in_=yT[:, :, ts]), reads=(by,), writes=(bybf,))
        P.op("act", lambda e: e.activation(out=ysq[:], in_=yT[:, :, ts], func=AF.Square), reads=(by,), writes=(bysq,))
        p1, p1b = g.bank()
        p2, p2b = g.bank()
        P.mm([lambda e, ct=ct: e.matmul(p1[:, :], lhsT=onesb[:], rhs=ybf[:, ct, :], start=(ct == 0), stop=(ct == 5)) for ct in range(6)],
             reads=(bybf, bOnes), writes=(p1b,))
        P.mm([lambda e, ct=ct: e.matmul(p2[:, :], lhsT=onesb[:], rhs=ysq[:, ct, :], start=(ct == 0), stop=(ct == 5)) for ct in range(6)],
             reads=(bysq, bOnes), writes=(p2b,))
        P.op("dve", lambda e: e.tensor_scalar(out=m_[:], in0=p1[:, :], scalar1=1.0 / 768, scalar2=None, op0=ALU.mult), reads=(p1b,), writes=(bm,))
        P.op("dve", lambda e: e.tensor_tensor(out=v_[:], in0=m_[:], in1=m_[:], op=ALU.mult), reads=(bm,), writes=(bv,))
        P.op("dve", lambda e: e.scalar_tensor_tensor(out=v_[:], in0=p2[:, :], scalar=1.0 / 768, in1=v_[:], op0=ALU.mult, op1=ALU.subtract),
             reads=(p2b, bv), writes=(bv,))
        P.op("act", lambda e: e.activation(out=rs[:], in_=v_[:], func=AF.Sqrt, bias=EPS), reads=(bv,), writes=(brs,))
        P.op("dve", lambda e: e.reciprocal(out=rs[:], in_=rs[:]), reads=(brs,), writes=(brs,))
        for ct in range(6):
            t_, bt_ = tt[ct % 2], btt[ct % 2]
            P.op("dve", lambda e: e.tensor_tensor(out=t_[:], in0=yT[:, ct, ts], in1=m_[:], op=ALU.subtract), reads=(by, bm), writes=(bt_,))
            P.op("dve", lambda e: e.tensor_tensor(out=t_[:], in0=t_[:], in1=rs[:], op=ALU.mult), reads=(bt_, brs), writes=(bt_,))
            P.op("act", lambda e: e.activation(out=YT[:, 10 + ct, ts], in_=t_[:], func=AF.Silu, scale=g.vec[:, V_LNG + ct:V_LNG + ct + 1],
                                               bias=g.vec[:, V_LNB + ct:V_LNB + ct + 1]), reads=(bt_, g.kbuf))
    t = g.tap("YC", [768, TO], BF16)
    if t is not None:
        P.barrier()
        P.dma("sp", t.ap().rearrange("(c p) t -> p c t", p=128), YT[:, 10:16, :])


def plan_weights_p45(g):
    W = g.W
    wv = lambda w, c0, n: w.ap().rearrange("(kc p) n -> p kc n", p=128)[:, :, c0:c0 + n]
    for cb in range(4):
        W.add([(lambda t: slot_view(t, 8, 512, kc0=0), wv(g.w_a_out, cb * 512, 512)),
               (lambda t: slot_view(t, 2, 512, kc0=8), wv(g.w_b_out, cb * 512, 512)),
               (lambda t: slot_view(t, 6, 512, kc0=10), wv(g.w_c_out, cb * 512, 512))])
    for cb in range(4):
        W.add([(lambda t: slot_view(t, 16, 512), wv(g.w_o, cb * 512, 512))])
    for gq in range(8):
        for i in range(2):
            W.add([(lambda t: slot_view(t, 16, 512), wv(g.w_up, gq * 1024 + i * 512, 512))])
        for half in range(2):
            src = g.w_down.ap()[gq * 1024:(gq + 1) * 1024, half * 1024:(half + 1) * 1024].rearrange("(kc p) n -> p kc n", p=128)
            W.add([(lambda t: slot_view(t, 8, 1024, width=1024), src)])


def phase4a(g):
    P = g.P
    g.RBC.reset(); g.RA2.reset()
    YT = g.YT
    g.mT = P.sb(g.RA2, [128, 16, TO], BF16, "mT")
    R = g.RBC
    gt = [P.sb(R, [128, 3, TO], BF16, f"gt{i}") for i in range(2)]
    bgt = [Buf("gt0"), Buf("gt1")]
    t1 = [P.sb(R, [128, 512], F32, f"t1{i}") for i in range(2)]
    t2 = [P.sb(R, [128, 512], F32, f"t2{i}") for i in range(2)]
    bt1 = [Buf("t10"), Buf("t11")]; bt2 = [Buf("t20"), Buf("t21")]
    gv = g.GATES.ap().rearrange("(i f) t -> f i t", i=3)
    k = 0
    for cb in range(4):
        stt, sbuf = g.W.get()
        sv = slot_view(stt, 16, 512)
        for j in range(4):
            fo = cb * 4 + j
            gg, bg = gt[fo % 2], bgt[fo % 2]
            P.dma("sp", gg[:], gv[fo * 128:(fo + 1) * 128], writes=(bg,))
            for tb in range(2):
                ts = slice(tb * 512, (tb + 1) * 512)
                banks = []
                for (k0, k1) in ((0, 8), (8, 10), (10, 16)):
                    pt, pb = g.bank()
                    P.mm([lambda e, kc=kc, pt=pt, k0=k0, k1=k1: e.matmul(pt[:, :], lhsT=sv[:, kc, j * 128:(j + 1) * 128], rhs=YT[:, kc, ts],
                                                                         start=(kc == k0), stop=(kc == k1 - 1)) for kc in range(k0, k1)],
                         reads=(sbuf,), writes=(pb,))
                    banks.append((pt, pb))
                a, ba = t1[k % 2], bt1[k % 2]
                b, bb = t2[k % 2], bt2[k % 2]
                k += 1
                (pA, pAb), (pB, pBb), (pC, pCb) = banks
                P.op("dve", lambda e: e.tensor_tensor(out=a[:], in0=pA[:, :], in1=gg[:, 0, ts], op=ALU.mult), reads=(pAb, bg), writes=(ba,))
                P.op("dve", lambda e: e.tensor_tensor(out=b[:], in0=pB[:, :], in1=gg[:, 1, ts], op=ALU.mult), reads=(pBb, bg), writes=(bb,))
                P.op("dve", lambda e: e.tensor_tensor(out=a[:], in0=a[:], in1=b[:], op=ALU.add), reads=(ba, bb), writes=(ba,))
                P.op("dve", lambda e: e.tensor_tensor(out=b[:], in0=pC[:, :], in1=gg[:, 2, ts], op=ALU.mult), reads=(pCb, bg, bb), writes=(bb,))
                P.op("dve", lambda e: e.tensor_tensor(out=g.mT[:, fo, ts], in0=a[:], in1=b[:], op=ALU.add), reads=(ba, bb))
    t = g.tap("MT", [D, TO], BF16)
    if t is not None:
        P.barrier()
        P.dma("sp", t.ap().rearrange("(c p) t -> p c t", p=128), g.mT[:])


def phase4b(g):
    P = g.P
    j0 = g.W.taken
    slots = [g.W.get(hold_from=j0) for _ in range(4)]
    g.RA1.reset(); g.RB.reset(); g.RC.reset()
    g.h2T = P.sb(g.RB, [128, 16, TO], BF16, "h2T")
    xt = [P.sb(g.RA1, [128, D], F32, f"xt{i}") for i in range(2)]
    bxt = [Buf("xt0"), Buf("xt1")]
    Grow = P.sb(g.RA1, [128, D], F32, "Grow")
    tq = P.sb(g.RA1, [128, D], F32, "tq")
    x1t = [P.sb(g.RC, [128, 1, D], F32, f"x1t{i}") for i in range(2)]
    bx1 = [Buf("x1t0"), Buf("x1t1")]
    junk = P.sb(g.RC, [128, D], BF16, "junk")
    st = P.sb(g.RC, [128, 8], F32, "st"); st2 = P.sb(g.RC, [128, 8], F32, "st2")
    bst, bst2, bG, btq = Buf("st"), Buf("st2"), Buf("Grow"), Buf("tq")
    postbc = P.sb(g.RC, [128, D], F32, "postbc")
    P.dma("sp", Grow[:], bass.AP(g.GROWS, 0, [[0, 128], [1, D]]), writes=(bG,))
    P.dma("sp", postbc[:], bass.AP(g.rows2, g.l * NROW + R_POST, [[0, 128], [1, D]]), writes=(bG,), add_write=True)
    P.op("dve", lambda e: e.tensor_tensor(out=Grow[:], in0=Grow[:], in1=postbc[:], op=ALU.mult), reads=(bG,), writes=(bG,))
    xv = g.xctx.ap().rearrange("(t p) d -> p t d", p=128)
    def stage_a(ti):
        x_, bx_ = xt[ti % 2], bxt[ti % 2]
        x1, b1 = x1t[ti % 2], bx1[ti % 2]
        P.dma("sp", x_[:], xv[:, 8 + ti, :], writes=(bx_,))
        bks = []
        for cb in range(4):
            pt, pb = g.bank()
            stt, sbuf = slots[cb]
            sv = slot_view(stt, 16, 512)
            P.mm([lambda e, kc=kc, pt=pt, sv=sv: e.matmul(pt[:, :], lhsT=g.mT[:, kc, ti * 128:(ti + 1) * 128], rhs=sv[:, kc, :],
                                                          start=(kc == 0), stop=(kc == KC - 1)) for kc in range(KC)], reads=(sbuf,), writes=(pb,))
            bks.append((pt, pb))
        for cb, (pt, pb) in enumerate(bks):
            P.op("act", lambda e, pt=pt, cb=cb: e.activation(out=junk[:, 0:512], in_=pt[:, :], func=AF.Square, accum_out=st2[:, cb:cb + 1]),
                 reads=(pb,), writes=(bst2,))
            P.op("dve", lambda e, pt=pt, cb=cb: e.tensor_tensor(out=tq[:, cb * 512:(cb + 1) * 512], in0=pt[:, :], in1=Grow[:, cb * 512:(cb + 1) * 512],
                                                               op=ALU.mult), reads=(pb, bG), writes=(btq,))
        P.op("dve", lambda e: e.reduce_sum(out=st2[:, 4:5], in_=st2[:, 0:4], axis=AX.X), reads=(bst2,), writes=(bst2,))
        P.op("dve", lambda e: e.tensor_scalar(out=st2[:, 5:6], in0=st2[:, 4:5], scalar1=1.0 / D, scalar2=EPS, op0=ALU.mult, op1=ALU.add),
             reads=(bst2,), writes=(bst2,))
        P.op("act", lambda e: e.activation(out=st2[:, 6:7], in_=st2[:, 5:6], func=AF.Sqrt), reads=(bst2,), writes=(bst2,))
        P.op("dve", lambda e: e.reciprocal(out=st2[:, 7:8], in_=st2[:, 6:7]), reads=(bst2,), writes=(bst2,))
        P.op("dve", lambda e: e.scalar_tensor_tensor(out=x1[:, 0, :], in0=tq[:], scalar=st2[:, 7:8], in1=x_[:], op0=ALU.mult, op1=ALU.add),
             reads=(btq, bst2, bx_), writes=(b1,))
        P.dma("sp", g.X1.ap()[ti * 128:(ti + 1) * 128, :], x1[:, 0, :], reads=(b1,))
        norm_stats(g, x1, b1, 1, junk, st, bst)

    def stage_b(ti):
        norm_xpose(g, x1t[ti % 2], bx1[ti % 2], 1, g.AB[:, 2, :], g.AB[:, 3, :], lambda fc, ti=ti: g.h2T[:, fc, ti * 128:(ti + 1) * 128])
    stage_a(0)
    for ti in range(8):
        if ti + 1 < 8:
            stage_a(ti + 1)
        stage_b(ti)
    t = g.tap("X1", [TO, D], F32)
    if t is not None:
        P.barrier()
        P.dma("sp", t.ap(), g.X1.ap())
    t = g.tap("H2T", [D, TO], BF16)
    if t is not None:
        P.barrier()
        P.dma("sp", t.ap().rearrange("(c p) t -> p c t", p=128), g.h2T[:])


def phase5(g):
    P = g.P
    g.RA.reset(); g.RC.reset()
    Y = P.sb(g.RA, [128, 8, D], F32, "Y")
    aT = P.sb(g.RC, [128, 8, TO], BF16, "aT")
    rst = [P.sb(g.RC, [128, 512], F32, f"rst{i}") for i in range(2)]
    brst = [Buf("rst0"), Buf("rst1")]
    baT = [Buf(f"aT{i}") for i in range(8)]
    bY = [[Buf(f"Y{ti}_{cb}") for cb in range(4)] for ti in range(8)]
    h2T = g.h2T
    k = 0
    for gq in range(8):
        for i in range(2):
            stt, sbuf = g.W.get()
            sv = slot_view(stt, 16, 512)
            for j in range(4):
                ffc = i * 4 + j
                for tb in range(2):
                    ts = slice(tb * 512, (tb + 1) * 512)
                    pt, pb = g.bank()
                    P.mm([lambda e, kc=kc, pt=pt: e.matmul(pt[:, :], lhsT=sv[:, kc, j * 128:(j + 1) * 128], rhs=h2T[:, kc, ts],
                                                           start=(kc == 0), stop=(kc == KC - 1)) for kc in range(KC)], reads=(sbuf,), writes=(pb,))
                    r_, br_ = rst[k % 2], brst[k % 2]
                    k += 1
                    P.op("act", lambda e, pt=pt, r_=r_: e.activation(out=r_[:], in_=pt[:, :], func=AF.Relu), reads=(pb,), writes=(br_,))
                    P.op("dve", lambda e, r_=r_, ffc=ffc, ts=ts: e.tensor_tensor(out=aT[:, ffc, ts], in0=r_[:], in1=r_[:], op=ALU.mult),
                         reads=(br_,), writes=(baT[ffc],))
        for half in range(2):
            stt, sbuf = g.W.get()
            sv8 = slot_view(stt, 8, 1024, width=1024)
            for cbh in range(2):
                cb = half * 2 + cbh
                for ti in range(8):
                    pt, pb = g.bank()
                    P.mm([lambda e, kc=kc, pt=pt: e.matmul(pt[:, :], lhsT=aT[:, kc, ti * 128:(ti + 1) * 128], rhs=sv8[:, kc, cbh * 512:(cbh + 1) * 512],
                                                           start=(kc == 0), stop=(kc == 7)) for kc in range(8)],
                         reads=(sbuf,) + tuple(baT), writes=(pb,))
                    yv = Y[:, ti, cb * 512:(cb + 1) * 512]
                    if gq == 0:
                        P.op("act", lambda e, pt=pt, yv=yv: e.copy(out=yv, in_=pt[:, :]), reads=(pb,), writes=(bY[ti][cb],))
                    else:
                        P.op("dve", lambda e, pt=pt, yv=yv: e.tensor_tensor(out=yv, in0=pt[:, :], in1=yv, op=ALU.add),
                             reads=(pb, bY[ti][cb]), writes=(bY[ti][cb],))
    P.barrier()
    g.RB.reset(); g.RC.reset()
    Grow = P.sb(g.RB, [128, D], F32, "Grow2")
    x1t = [P.sb(g.RB, [128, D], F32, f"x1f{i}") for i in range(2)]
    bx1 = [Buf("x1f0"), Buf("x1f1")]
    ot = [P.sb(g.RC, [128, D], F32, f"ot{i}") for i in range(2)]
    bot = [Buf("ot0"), Buf("ot1")]
    junk = P.sb(g.RC, [128, D], BF16, "junkf")
    tq = P.sb(g.RC, [128, D], F32, "tqf")
    postbc = P.sb(g.RC, [128, D], F32, "postbc2")
    st = P.sb(g.RB, [128, 8], F32, "stf")
    bG, bst, btq = Buf("G2"), Buf("stf"), Buf("tqf")
    P.dma("sp", Grow[:], bass.AP(g.GROWS, D, [[0, 128], [1, D]]), writes=(bG,))
    P.dma("sp", postbc[:], bass.AP(g.rows2, g.l * NROW + R_MLPPOST, [[0, 128], [1, D]]), writes=(bG,), add_write=True)
    P.op("dve", lambda e: e.tensor_tensor(out=Grow[:], in0=Grow[:], in1=postbc[:], op=ALU.mult), reads=(bG,), writes=(bG,))
    for ti in range(8):
        x1, b1 = x1t[ti % 2], bx1[ti % 2]
        o_, bo_ = ot[ti % 2], bot[ti % 2]
        P.dma("sp", x1[:], g.X1.ap()[ti * 128:(ti + 1) * 128, :], writes=(b1,))
        P.op("act", lambda e: e.activation(out=junk[:], in_=Y[:, ti, :], func=AF.Square, accum_out=st[:, 0:1]), writes=(bst,))
        P.op("dve", lambda e: e.tensor_scalar(out=st[:, 1:2], in0=st[:, 0:1], scalar1=1.0 / D, scalar2=EPS, op0=ALU.mult, op1=ALU.add),
             reads=(bst,), writes=(bst,))
        P.op("act", lambda e: e.activation(out=st[:, 2:3], in_=st[:, 1:2], func=AF.Sqrt), reads=(bst,), writes=(bst,))
        P.op("dve", lambda e: e.reciprocal(out=st[:, 3:4], in_=st[:, 2:3]), reads=(bst,), writes=(bst,))
        P.op("dve", lambda e: e.tensor_tensor(out=tq[:], in0=Y[:, ti, :], in1=Grow[:], op=ALU.mult), reads=(bG,), writes=(btq,))
        P.op("dve", lambda e: e.scalar_tensor_tensor(out=o_[:], in0=tq[:], scalar=st[:, 3:4], in1=x1[:], op0=ALU.mult, op1=ALU.add),
             reads=(btq, bst, b1), writes=(bo_,))
        P.dma("sp", g.xout.ap()[ti * 128:(ti + 1) * 128, :], o_[:], reads=(bo_,))


def make_consts():
    c = np.zeros((128, NCONST), np.float32)
    c[:, C_IDENT:C_IDENT + 128] = np.eye(128, dtype=np.float32)
    sw = np.zeros((128, 128), np.float32)
    for m in range(128):
        sw[(m + 64) % 128, m] = 1.0
    c[:, C_SWAP:C_SWAP + 128] = sw
    fv = np.zeros(383, np.float32); fv[127:127 + 129] = 1.0
    c[:, C_FVALID:C_FVALID + 383] = fv[None, :]
    s = np.arange(64)[:, None]; t = np.arange(64)[None, :]
    c[:64, C_CAUS:C_CAUS + 64] = (s <= t).astype(np.float32)
    rm = np.ones(512, np.float32); rm[0::64] = 0.0
    c[:, C_RESET:C_RESET + 512] = rm[None, :]
    c[:, C_ONES:C_ONES + 128] = 1.0
    return c

def t5_bucket_np(dist):
    import math
    exact = 16
    d = np.maximum(dist, 1).astype(np.float32)
    large = exact + (np.log(d / np.float32(exact)) / np.float32(math.log(2048 / exact)) * np.float32(32 - exact)).astype(np.int32)
    return np.where(dist < exact, dist, np.clip(large, exact, 31))

def make_onehot():
    oh = np.zeros((3, 32, 383), np.float32)
    for gi, dil in enumerate((1, 4, 16)):
        for s in range(129):
            b = int(t5_bucket_np(np.array([s * dil]))[0])
            oh[gi, b, 127 + s] = 1.0
    return oh

def col(v):
    return np.ascontiguousarray(np.asarray(v, np.float32).reshape(-1, 128).T)

def layer_small(inputs, l):
    vec = np.zeros((128, NVEC), np.float32)
    vec[:, V_PRE:V_PRE + 16] = col(inputs["mix_norm_pre"][l])
    vec[:, V_MLPPRE:V_MLPPRE + 16] = col(inputs["mlp_norm_pre"][l])
    vec[:, V_BGATE:V_BGATE + 48] = col(inputs["b_gate"][l])
    vec[:, V_NORMW:V_NORMW + 1] = col(inputs["hgrn_norm_w"][l])
    vec[:, V_CONVB:V_CONVB + 6] = col(inputs["conv_b"][l])
    vec[:, V_LNG:V_LNG + 6] = col(inputs["conv_ln_g"][l])
    vec[:, V_LNB:V_LNB + 6] = col(inputs["conv_ln_b"][l])
    cw = np.asarray(inputs["conv_w"][l], np.float32)
    vec[:, V_CONVW:V_CONVW + 186] = cw.reshape(31, 6, 128).transpose(2, 1, 0).reshape(128, 186)
    rows = np.zeros((1, NROW), np.float32)
    rows[0, R_BADA:R_BADA + 12288] = inputs["b_ada"][l]
    rows[0, R_POST:R_POST + 2048] = inputs["mix_norm_post"][l]
    rows[0, R_MLPPOST:R_MLPPOST + 2048] = inputs["mlp_norm_post"][l]
    return vec, rows

def core_inputs(inputs, l, b, half, xfull):
    xb = np.asarray(xfull[b], np.float32)
    if half == 0:
        xctx = np.concatenate([np.zeros((1024, D), np.float32), xb[:1024]], axis=0)
    else:
        xctx = xb
    flag = float(half)
    tm = np.zeros((128, 4), np.float32)
    tm[:, 0] = flag; tm[:, 1] = 1.0; tm[:64, 2] = flag; tm[64:, 2] = 1.0; tm[:, 3] = flag
    vec, rows = layer_small(inputs, l)
    lbl = np.asarray(inputs["hgrn_lb_logits"], np.float32).reshape(2, 8, 128).transpose(2, 0, 1)
    return {
        "xctx": np.ascontiguousarray(xctx), "c_b": col(inputs["c"][b]), "tmask": tm,
        "consts": make_consts(), "vecs": vec, "rows": rows, "lbl": np.ascontiguousarray(lbl),
        "rel_bias": np.asarray(inputs["rel_bias"], np.float32), "onehot": make_onehot(),
        "w_ada": np.asarray(inputs["w_ada"][l]), "w_in": np.asarray(inputs["w_in"][l]),
        "w_gate": np.asarray(inputs["w_gate"][l]), "w_a_out": np.asarray(inputs["w_a_out"][l]),
        "w_b_out": np.asarray(inputs["w_b_out"][l]), "w_c_out": np.asarray(inputs["w_c_out"][l]),
        "w_o": np.asarray(inputs["w_o"][l]), "w_up": np.asarray(inputs["w_up"][l]),
        "w_down": np.asarray(inputs["w_down"][l]),
        "layer_is1": np.full((128, 1), float(l), np.float32),
    }


def fused_core_inputs(inputs, b, half, shared):
    xb = np.asarray(inputs["x"][b], np.float32)
    z = np.zeros((1024, D), np.float32)
    if half == 1:
        x3 = np.concatenate([z, xb[:1024], xb[1024:]], axis=0)
    else:
        x3 = np.concatenate([z, z, xb[:1024]], axis=0)
    tm = np.zeros((2, 128, 4), np.float32)
    for i, flag in enumerate((0.0, float(half))):
        tm[i, :, 0] = flag; tm[i, :, 1] = 1.0; tm[i, :64, 2] = flag; tm[i, 64:, 2] = 1.0; tm[i, :, 3] = flag
    m = dict(shared)
    m["x3"] = np.ascontiguousarray(x3)
    m["c_b"] = col(inputs["c"][b])
    m["tmask"] = tm
    return m


def shared_inputs(inputs):
    vl = [layer_small(inputs, l) for l in range(2)]
    lbl = np.asarray(inputs["hgrn_lb_logits"], np.float32).reshape(2, 8, 128).transpose(2, 0, 1)
    sh = {
        "consts": make_consts(), "vecs": np.stack([v[0] for v in vl]), "rows": np.stack([v[1] for v in vl]),
        "lbl": np.ascontiguousarray(lbl), "rel_bias": np.asarray(inputs["rel_bias"], np.float32), "onehot": make_onehot(),
    }
    for k in ("w_ada", "w_in", "w_gate", "w_a_out", "w_b_out", "w_c_out", "w_o", "w_up", "w_down"):
        sh[k] = np.asarray(inputs[k], np.float32)
    return sh


_NC_CACHE = {}


def kernel(**inputs):
    inputs = {k: np.asarray(v) for k, v in inputs.items()}
    if "nc" not in _NC_CACHE:
        _NC_CACHE["nc"] = build_fused()[0]
    nc = _NC_CACHE["nc"]
    sh = shared_inputs(inputs)
    maps = [fused_core_inputs(inputs, b, half, sh) for b in range(4) for half in range(2)]
    res = run_bass_kernel_spmd(nc, maps, core_ids=list(range(8)))
    out = np.empty((4, 2048, D), np.float32)
    for b in range(4):
        for half in range(2):
            out[b, half * 1024:(half + 1) * 1024] = res.results[b * 2 + half]["xout"]
    return out
```

```python
import numpy as np
from contextlib import ExitStack
import concourse.bass as bass
import concourse.mybir as mybir
from concourse.bass_utils import run_bass_kernel_spmd

F32 = mybir.dt.float32
BF16 = mybir.dt.bfloat16
AF = mybir.ActivationFunctionType
ALU = mybir.AluOpType
AX = mybir.AxisListType

D = 2048
KC = 16
TC = 2048
TO = 1024
DFF = 8192
INW = 7936
EPS = 1e-6
SB_BASE = 16512
SB_TOP = 229344
NSLOT = 4
SLOT_BYTES = 16384
CONST_BYTES = 10240
RA_BYTES = 65536
RB_BYTES = 32768

C_IDENT = 0
C_SWAP = 128
C_FVALID = 256
C_CAUS = 256 + 383
C_RESET = C_CAUS + 64
C_ONES = C_RESET + 512
NCONST = C_ONES + 128
V_PRE = 0
V_MLPPRE = 16
V_BGATE = 32
V_NORMW = 80
V_CONVB = 81
V_LNG = 87
V_LNB = 93
V_CONVW = 99
NVEC = V_CONVW + 6 * 31
R_BADA = 0
R_POST = 12288
R_MLPPOST = 12288 + 2048
NROW = 12288 + 4096


class Buf:
    __slots__ = ("name", "w", "r", "excl")

    def __init__(self, name="", excl=False):
        self.name = name
        self.w = []
        self.r = {}
        self.excl = excl


class Region:
    def __init__(self, base, size):
        self.base = base
        self.size = size
        self.off = 0

    def reset(self):
        self.off = 0


class Prog:
    def __init__(self, nc, stack):
        self.nc = nc
        self.E = dict(pe=nc.tensor, act=nc.scalar, dve=nc.vector, pool=nc.gpsimd, sp=nc.sync)
        self.sem = {}
        self.cnt = {}
        for k in ("pe", "act", "dve", "pool"):
            self.sem[k] = stack.enter_context(nc.semaphore("s_" + k))
            self.cnt[k] = 0
        self.dsem = {}
        self.dcnt = {}
        self.dnext = {}
        for q, n in (("sp", 16), ("pool", 8)):
            self.dsem[q] = [stack.enter_context(nc.semaphore(f"d_{q}{i}")) for i in range(n)]
            self.dcnt[q] = [0] * n
            self.dnext[q] = 0
        self.waited = {e: {} for e in self.E}
        self.nalloc = 0
        self.n_ins = 0

    def _semof(self, key):
        if isinstance(key, tuple):
            return self.dsem[key[1]][key[2]]
        return self.sem[key]

    def _wait(self, e, key, val):
        if e == "pe" and key == "pe":
            return
        if self.waited[e].get(key, 0) >= val:
            return
        self.E[e].wait_ge(self._semof(key), val)
        self.waited[e][key] = val

    def _deps(self, e, reads, writes):
        best = {}
        for b in reads:
            for k, v in b.w:
                if best.get(k, 0) < v:
                    best[k] = v
            if b.excl:
                for k, v in b.r.items():
                    if k != e and best.get(k, 0) < v:
                        best[k] = v
        for b in writes:
            for k, v in b.w:
                if best.get(k, 0) < v:
                    best[k] = v
            for k, v in b.r.items():
                if best.get(k, 0) < v:
                    best[k] = v
        for k, v in best.items():
            self._wait(e, k, v)

    def _commit(self, tok, reads, writes):
        for b in writes:
            b.w = [tok]
            b.r = {}
        for b in reads:
            if b.r.get(tok[0], 0) < tok[1]:
                b.r[tok[0]] = tok[1]

    def op(self, e, fn, reads=(), writes=()):
        self._deps(e, reads, writes)
        ins = fn(self.E[e])
        self.cnt[e] += 1
        ins.then_inc(self.sem[e], 1)
        self._commit((e, self.cnt[e]), reads, writes)
        self.n_ins += 1
        return ins

    def mm(self, mms, reads=(), writes=()):
        self._deps("pe", reads, writes)
        ins = None
        for f in mms:
            ins = f(self.nc.tensor)
        self.cnt["pe"] += 1
        ins.then_inc(self.sem["pe"], 1)
        self._commit(("pe", self.cnt["pe"]), reads, writes)
        self.n_ins += len(mms)

    def dma(self, q, out, in_, reads=(), writes=(), add_write=False):
        i = self.dnext[q]
        self.dnext[q] = (i + 1) % len(self.dsem[q])
        key = ("d", q, i)
        self._wait(q, key, self.dcnt[q][i])
        self._deps(q, reads, writes)
        self.E[q].dma_start(out=out, in_=in_).then_inc(self.dsem[q][i], 16)
        self.dcnt[q][i] += 16
        tok = (key, self.dcnt[q][i])
        for b in writes:
            if add_write:
                b.w = b.w + [tok]
            else:
                b.w = [tok]
                b.r = {}
        for b in reads:
            b.r[key] = tok[1]
        self.n_ins += 1
        return tok

    def barrier(self, engines=("pe", "act", "dve", "sp", "pool")):
        toks = [(k, self.cnt[k]) for k in ("pe", "act", "dve", "pool") if self.cnt[k] > 0]
        for i, v in enumerate(self.dcnt["sp"]):
            if v > 0:
                toks.append((("d", "sp", i), v))
        for e in engines:
            for k, v in toks:
                self._wait(e, k, v)

    def final_wait(self):
        toks = [(k, self.cnt[k]) for k in ("pe", "act", "dve", "pool") if self.cnt[k] > 0]
        for q in ("sp", "pool"):
            for i, v in enumerate(self.dcnt[q]):
                if v > 0:
                    toks.append((("d", q, i), v))
        for e in ("sp", "pool", "act", "dve", "pe"):
            for k, v in toks:
                self._wait(e, k, v)

    def sb(self, region, shape, dtype, name="t"):
        esz = 4 if dtype == F32 else 2
        n = 1
        for s in shape[1:]:
            n *= s
        nbytes = (n * esz + 31) // 32 * 32
        regs = region if isinstance(region, (list, tuple)) else [region]
        for r in regs:
            if r.off + nbytes <= r.size:
                off = r.base + r.off
                r.off += nbytes
                self.nalloc += 1
                return self.nc.alloc_sbuf_tensor_at(f"{name}_{self.nalloc}", list(shape), dtype, offset=off)
        raise RuntimeError(f"SBUF region overflow allocating {name} {shape}")


class WStream:
    def __init__(self, P, slots):
        self.P = P
        self.slots = slots
        self.plan = []
        self.issued = 0
        self.taken = 0

    def add(self, pieces):
        self.plan.append(pieces)

    def _issue(self, j):
        t, b = self.slots[j % len(self.slots)]
        first = True
        for dst_fn, src in self.plan[j]:
            self.P.dma("pool", dst_fn(t), src, writes=(b,), add_write=not first)
            first = False

    def get(self, hold_from=None):
        j = self.taken
        self.taken += 1
        base = j if hold_from is None else hold_from
        lim = min(len(self.plan), base + len(self.slots))
        while self.issued < lim:
            self._issue(self.issued)
            self.issued += 1
        return self.slots[j % len(self.slots)]


def slot_view(t, kc, n, kc0=0, c0=0, width=512):
    return bass.AP(t, kc0 * width + c0, [[8192, 128], [width, kc], [1, n]])


class Ctx:
    pass


class LV:
    def __init__(self, t, idx=None, rows=None):
        self.t, self.idx, self.rows = t, idx, rows
        shp = list(t.shape)
        self.shape = shp[1:] if idx is not None else shp
        self.dtype = t.dtype

    def ap(self):
        a = self.t.ap()
        if self.idx is not None:
            a = a[self.idx]
        if self.rows is not None:
            a = a[self.rows[0]:self.rows[1]]
        return a


def build_fused(dbg=None, passes=("A", "B", "L1"), stop_after=None):
    dbg = dbg or set()
    nc = bass.Bass("TRN2", target_bir_lowering=False)
    stack = ExitStack()
    with stack:
        P = Prog(nc, stack)
        g = Ctx()
        g.nc, g.P, g.dbg = nc, P, dbg
        di = lambda name, shape, dt=F32: nc.dram_tensor(name, list(shape), dt, kind="ExternalInput")
        g.x3 = di("x3", [3 * TO, D])
        g.c_b = di("c_b", [128, KC])
        g.tmask2 = di("tmask", [2, 128, 4])
        g.consts = di("consts", [128, NCONST])
        g.vecs2 = di("vecs", [2, 128, NVEC])
        g.rows2 = di("rows", [2, 1, NROW])
        g.lbl = di("lbl", [128, 2, 8])
        g.rel_bias = di("rel_bias", [32, 12])
        g.onehot = di("onehot", [3, 32, 383])
        g.Wf = dict(
            w_ada=di("w_ada", [2, D, 6 * D]), w_in=di("w_in", [2, D, INW]), w_gate=di("w_gate", [2, D, 3 * D]),
            w_a_out=di("w_a_out", [2, 1024, D]), w_b_out=di("w_b_out", [2, 256, D]), w_c_out=di("w_c_out", [2, 768, D]),
            w_o=di("w_o", [2, D, D]), w_up=di("w_up", [2, D, DFF]), w_down=di("w_down", [2, DFF, D]))
        g.xout_t = nc.dram_tensor("xout", [TO, D], F32, kind="ExternalOutput")
        ds = lambda name, shape, dt: nc.dram_tensor(name, list(shape), dt)
        g.AQ = ds("AQ", [1024, TO], BF16)
        g.AF = ds("AFz", [1024, TC], F32)
        g.AG = ds("AG", [1024, TO], BF16)
        g.AI = ds("AI", [TC, 1024], BF16)
        g.BQ = ds("BQ", [768, TO], BF16)
        g.BK = ds("BK", [768, TC], BF16)
        g.BV = ds("BV", [3, 16, 128, 512], BF16)
        g.U = ds("U", [768, TO + 32], BF16)
        g.GATES = ds("GATES", [3 * D, TO], BF16)
        g.X1 = ds("X1", [TO, D], F32)
        g.FT = ds("FT", [12, 128 * 383], F32)
        g.GROWS = ds("GROWS", [2, D], F32)
        g.XL1 = ds("XL1", [TC, D], F32)
        g.taps = {}

        def tap(name, shape, dt=F32):
            return None
        g.tap = tap
        off = SB_BASE
        g.slots = []
        for i in range(NSLOT):
            t = nc.alloc_sbuf_tensor_at(f"wslot{i}", [128, 8192], BF16, offset=off)
            g.slots.append((t, Buf(f"slot{i}")))
            off += SLOT_BYTES
        g.RK = Region(off, CONST_BYTES); off += CONST_BYTES
        g.RA1 = Region(off, RA_BYTES // 2)
        g.RA2 = Region(off + RA_BYTES // 2, RA_BYTES // 2)
        g.RA = Region(off, RA_BYTES); off += RA_BYTES
        g.RB = Region(off, RB_BYTES); off += RB_BYTES
        g.RC = Region(off, SB_TOP - off)
        g.RBC = Region(g.RB.base, RB_BYTES + g.RC.size)
        g.banks = [(nc.alloc_psum_tensor(f"ps{i}", [128, 512], F32), Buf(f"bank{i}", excl=True)) for i in range(8)]
        g.bank_i = 0

        def bank():
            b = g.banks[g.bank_i]
            g.bank_i = (g.bank_i + 1) % 8
            return b
        g.bank = bank
        g.W = WStream(P, g.slots)
        g.cst = P.sb(g.RK, [128, NCONST], F32, "cst")
        g.identb = P.sb(g.RK, [128, 128], BF16, "identb")
        g.vec = P.sb(g.RK, [128, NVEC], F32, "vec")
        g.tm = P.sb(g.RK, [128, 4], F32, "tm")
        g.modc = P.sb(g.RK, [128, 4, 16], F32, "modc")
        g.AB = P.sb(g.RK, [128, 4, 16], F32, "AB")
        g.lbv = P.sb(g.RK, [128, 3, 8], F32, "lbv")
        g.kbuf = Buf("consts")
        P.dma("sp", g.cst[:], g.consts.ap(), writes=(g.kbuf,))
        P.op("dve", lambda e: e.tensor_copy(out=g.identb[:], in_=g.cst[:, C_IDENT:C_IDENT + 128]),
             reads=(g.kbuf,), writes=(g.kbuf,))
        g.ident = g.cst[:, C_IDENT:C_IDENT + 128]
        g.ones = g.cst[:, C_ONES:C_ONES + 128]

        for ps in passes:
            l = 1 if ps == "L1" else 0
            for k, t in g.Wf.items():
                setattr(g, k, LV(t, l))
            if ps in ("A", "L1"):
                plan_weights(g, 0, 8)
            plan_weights_p2(g)
            if ps in ("A", "L1"):
                plan_weights(g, 8, 24)
            plan_weights_p45(g)

        build_bias_tables(g)
        P.barrier()
        for ps in passes:
            l = 1 if ps == "L1" else 0
            g.l = l
            g.vecs = LV(g.vecs2, l)
            g.rows = LV(g.rows2, l)
            g.noprev = (ps == "A")
            if ps == "A":
                g.xctx = LV(g.x3, None, (0, TC))
                g.xout = LV(g.XL1, None, (0, TO))
                tmi = 0
            elif ps == "B":
                g.xctx = LV(g.x3, None, (TO, TO + TC))
                g.xout = LV(g.XL1, None, (TO, TC))
                tmi = 1
            else:
                g.xctx = LV(g.XL1)
                g.xout = LV(g.xout_t)
                tmi = 1
            if "L1" not in passes and ps == passes[-1]:
                g.xout = LV(g.xout_t)
            P.dma("sp", g.vec[:], g.vecs.ap(), writes=(g.kbuf,))
            P.dma("sp", g.tm[:], g.tmask2.ap()[tmi], writes=(g.kbuf,), add_write=True)
            if ps in ("A", "L1"):
                phase0_setup(g)
                for _ in range(8):
                    phase0_tile(g)
                P.barrier()
            phase1(g)
            P.barrier()
            phase2(g)
            P.barrier()
            g.RA1.reset()
            g.YT = P.sb(g.RA1, [128, 16, TO], BF16, "YT")
            phase3_hgrn(g)
            while getattr(g, "p0_next", 24) < 24:
                phase0_tile(g)
            P.barrier()
            phase3_attn(g)
            P.barrier()
            phase3_conv(g)
            P.barrier()
            phase4a(g)
            P.barrier()
            phase4b(g)
            P.barrier()
            phase5(g)
            P.barrier()
        P.final_wait()
    return nc, g


def plan_weights(g, nt0, nt1):
    W = g.W
    wv = lambda w, c0, n: w.ap().rearrange("(kc p) n -> p kc n", p=128)[:, :, c0:c0 + n]
    for nt in range(nt0, nt1):
        W.add([(lambda t: slot_view(t, 16, 512), wv(g.w_ada, nt * 512, 512))])


def phase0_setup(g):
    P = g.P
    if not hasattr(g, "csb"):
        g.csb = P.sb(g.RK, [128, KC], F32, "csb")
        g.cact = P.sb(g.RK, [128, KC], BF16, "cact")
        g.stg0 = P.sb(g.RK, [1, 512], F32, "stg0")
        g.lb_in = P.sb(g.RK, [128, 2, 8], F32, "lb_in")
        g.b_c, g.b_stg0, g.b_lb, g.b_modc = Buf("c"), Buf("stg0"), Buf("lb"), Buf("modc")
    P.dma("sp", g.csb[:], g.c_b.ap(), writes=(g.b_c,))
    P.op("act", lambda e: e.activation(out=g.cact[:], in_=g.csb[:], func=AF.Silu), reads=(g.b_c,), writes=(g.b_c,))
    b_lb = g.b_lb
    P.dma("sp", g.lb_in[:], g.lbl.ap(), writes=(b_lb,))
    P.op("dve", lambda e: e.tensor_tensor(out=g.lbv[:, 2, :], in0=g.lb_in[:, 1, :], in1=g.lb_in[:, 0, :], op=ALU.subtract),
         reads=(b_lb,), writes=(b_lb,))
    P.op("act", lambda e: e.activation(out=g.lbv[:, 2, :], in_=g.lbv[:, 2, :], func=AF.Sigmoid), reads=(b_lb,), writes=(b_lb,))
    P.op("dve", lambda e: e.tensor_scalar(out=g.lbv[:, 0, :], in0=g.lbv[:, 2, :], scalar1=float(g.l), scalar2=None, op0=ALU.mult),
         reads=(b_lb,), writes=(b_lb,))
    P.op("dve", lambda e: e.tensor_scalar(out=g.lbv[:, 1, :], in0=g.lbv[:, 0, :], scalar1=-1.0, scalar2=1.0, op0=ALU.mult, op1=ALU.add),
         reads=(b_lb,), writes=(b_lb,))
    g.p0_next = 0


def phase0_tile(g):
    P = g.P
    nt = g.p0_next
    g.p0_next += 1
    stg, b_stg = g.stg0, g.b_stg0
    one11 = g.cst[0:1, C_ONES:C_ONES + 1]
    st, sb_ = g.W.get()
    pt, pb = g.bank()
    P.dma("sp", stg[:], g.rows.ap()[0:1, R_BADA + nt * 512:R_BADA + (nt + 1) * 512], writes=(b_stg,))
    P.mm([lambda e, kc=kc: e.matmul(pt[0:1, :], lhsT=g.cact[:, kc:kc + 1], rhs=slot_view(st, 16, 512)[:, kc, :],
                                    start=(kc == 0), stop=(kc == KC - 1)) for kc in range(KC)], reads=(g.b_c, sb_), writes=(pb,))
    P.op("dve", lambda e: e.tensor_tensor(out=stg[:], in0=pt[0:1, :], in1=stg[:], op=ALU.add), reads=(pb, b_stg), writes=(b_stg,))
    seg, j4 = nt // 4, nt % 4
    if seg in (2, 5):
        P.dma("sp", g.GROWS.ap()[(0 if seg == 2 else 1):(1 if seg == 2 else 2), j4 * 512:(j4 + 1) * 512], stg[:], reads=(b_stg,))
    else:
        si = {0: 0, 1: 1, 3: 2, 4: 3}[seg]
        pt2, pb2 = g.bank()
        P.mm([lambda e, q=q: e.matmul(pt2[:, q:q + 1], lhsT=stg[0:1, q * 128:(q + 1) * 128], rhs=one11, start=True, stop=True)
              for q in range(4)], reads=(b_stg, g.kbuf), writes=(pb2,))
        P.op("dve", lambda e: e.tensor_copy(out=g.modc[:, si, j4 * 4:(j4 + 1) * 4], in_=pt2[:, 0:4]), reads=(pb2,), writes=(g.b_modc,))
    if nt == 7:
        P.op("dve", lambda e: e.scalar_tensor_tensor(out=g.AB[:, 0, :], in0=g.modc[:, 1, :], scalar=1.0,
                                                     in1=g.vec[:, V_PRE:V_PRE + 16], op0=ALU.add, op1=ALU.mult),
             reads=(g.b_modc, g.kbuf), writes=(g.b_modc,))
        P.op("dve", lambda e: e.tensor_copy(out=g.AB[:, 1, :], in_=g.modc[:, 0, :]), reads=(g.b_modc,), writes=(g.b_modc,))
    if nt == 19:
        P.op("dve", lambda e: e.scalar_tensor_tensor(out=g.AB[:, 2, :], in0=g.modc[:, 3, :], scalar=1.0,
                                                     in1=g.vec[:, V_MLPPRE:V_MLPPRE + 16], op0=ALU.add, op1=ALU.mult),
             reads=(g.b_modc, g.kbuf), writes=(g.b_modc,))
        P.op("dve", lambda e: e.tensor_copy(out=g.AB[:, 3, :], in_=g.modc[:, 2, :]), reads=(g.b_modc,), writes=(g.b_modc,))


def phase0_hook(g):
    if getattr(g, "p0_next", 24) < 24:
        phase0_tile(g)


def norm_stats(g, xs, b_xs, ntile, junk, st, b_st):
    P = g.P
    for j in range(ntile):
        P.op("act", lambda e, j=j: e.activation(out=junk[:], in_=xs[:, j, :], func=AF.Square, accum_out=st[:, j:j + 1]),
             reads=(b_xs,), writes=(b_st,))
    P.op("dve", lambda e: e.tensor_scalar(out=st[:, ntile:2 * ntile], in0=st[:, 0:ntile], scalar1=1.0 / D, scalar2=EPS,
                                          op0=ALU.mult, op1=ALU.add), reads=(b_st,), writes=(b_st,))
    P.op("act", lambda e: e.activation(out=st[:, 2 * ntile:3 * ntile], in_=st[:, ntile:2 * ntile], func=AF.Sqrt),
         reads=(b_st,), writes=(b_st,))
    P.op("dve", lambda e: e.reciprocal(out=st[:, 3 * ntile:4 * ntile], in_=st[:, 2 * ntile:3 * ntile]),
         reads=(b_st,), writes=(b_st,))
    for j in range(ntile):
        eng = "act" if j % 2 == 0 else "dve"
        if eng == "act":
            P.op("act", lambda e, j=j: e.activation(out=xs[:, j, :], in_=xs[:, j, :], func=AF.Copy,
                                                    scale=st[:, 3 * ntile + j:3 * ntile + j + 1]),
                 reads=(b_st, b_xs), writes=(b_xs,))
        else:
            P.op("dve", lambda e, j=j: e.tensor_scalar(out=xs[:, j, :], in0=xs[:, j, :],
                                                       scalar1=st[:, 3 * ntile + j:3 * ntile + j + 1], scalar2=None, op0=ALU.mult),
                 reads=(b_st, b_xs), writes=(b_xs,))


def norm_xpose(g, xs, b_xs, ntile, AB_a, AB_b, dst_fn):
    P = g.P
    for fc in range(KC):
        pt, pb = g.bank()
        mms = [lambda e, j=j, fc=fc: e.transpose(pt[:, j * 128:(j + 1) * 128], xs[:, j, fc * 128:(fc + 1) * 128], g.ident)
               for j in range(ntile)]
        P.mm(mms, reads=(b_xs, g.kbuf), writes=(pb,))
        n = ntile * 128
        if fc % 2 == 0:
            P.op("act", lambda e, fc=fc: e.activation(out=dst_fn(fc), in_=pt[:, 0:n], func=AF.Identity,
                                                      scale=AB_a[:, fc:fc + 1], bias=AB_b[:, fc:fc + 1]),
                 reads=(pb, g.b_modc))
        else:
            P.op("dve", lambda e, fc=fc: e.tensor_scalar(out=dst_fn(fc), in0=pt[:, 0:n], scalar1=AB_a[:, fc:fc + 1],
                                                         scalar2=AB_b[:, fc:fc + 1], op0=ALU.mult, op1=ALU.add),
                 reads=(pb, g.b_modc))


def norm_transpose(g, xs, b_xs, ntile, AB_a, AB_b, dst_fn, junk, st, b_st):
    norm_stats(g, xs, b_xs, ntile, junk, st, b_st)
    norm_xpose(g, xs, b_xs, ntile, AB_a, AB_b, dst_fn)


def phase1(g):
    P = g.P
    g.RA.reset()
    g.hT = P.sb(g.RA, [128, KC, TC], BF16, "hT")
    R = g.RBC
    R.reset()
    NB = 3
    xsN = [P.sb(R, [128, 2, D], F32, f"xs{i}") for i in range(NB)]
    bx = [Buf(f"xs{i}") for i in range(NB)]
    stN = [P.sb(R, [128, 8], F32, f"st{i}") for i in range(NB)]
    bst = [Buf(f"st{i}") for i in range(NB)]
    junk = P.sb(R, [128, D], BF16, "junk")
    xv = g.xctx.ap().rearrange("(t p) d -> p t d", p=128)

    def stage_a(gi):
        xs, b_xs = xsN[gi % NB], bx[gi % NB]
        P.dma("sp", xs[:], xv[:, gi * 2:gi * 2 + 2, :], writes=(b_xs,))
        norm_stats(g, xs, b_xs, 2, junk, stN[gi % NB], bst[gi % NB])

    def stage_b(gi):
        norm_xpose(g, xsN[gi % NB], bx[gi % NB], 2, g.AB[:, 0, :], g.AB[:, 1, :],
                   lambda fc, gi=gi: g.hT[:, fc, gi * 256:(gi + 1) * 256])
    stage_a(0)
    for gi in range(8):
        if gi + 1 < 8:
            stage_a(gi + 1)
        stage_b(gi)


def plan_weights_p2(g):
    W = g.W
    wv = lambda w, c0, n: w.ap().rearrange("(kc p) n -> p kc n", p=128)[:, :, c0:c0 + n]
    full = lambda c0, n: [(lambda t, n=n: slot_view(t, 16, n), wv(g.w_in, c0, n))]
    for c0 in (0, 512):
        W.add(full(c0, 512))
    for c0 in (1024, 1536):
        W.add(full(c0, 512))
    for c0 in (3072, 3584):
        W.add(full(c0, 512))
    for c0 in (2048, 2560):
        W.add(full(c0, 512))
    W.add(full(4096, 512)); W.add(full(4608, 256))
    W.add(full(4864, 512)); W.add(full(5376, 256))
    for gi in range(3):
        W.add(full(5632 + gi * 256, 256))
    for i in range(3):
        W.add([(lambda t: slot_view(t, 16, 256, c0=0), wv(g.w_in, 6400 + i * 256, 256)),
               (lambda t: slot_view(t, 16, 256, c0=256), wv(g.w_in, 7168 + i * 256, 256))])
    for i in range(12):
        W.add([(lambda t: slot_view(t, 16, 512), wv(g.w_gate, i * 512, 512))])


def phase2(g):
    P = g.P
    R = g.RBC
    R.reset()
    hT = g.hT
    NSTG = 8
    stg = [(P.sb(R, [128, 512], F32, f"stg{i}"), Buf(f"stg{i}")) for i in range(NSTG)]
    tmp = [(P.sb(R, [128, 512], F32, f"tmp{i}"), Buf(f"tmp{i}")) for i in range(4)]
    onesv = P.sb(R, [128, 256], F32, "onesv")
    b_ones = Buf("onesv")
    P.op("dve", lambda e: e.memset(onesv[:], 1.0), writes=(b_ones,))
    st = {"i": 0, "t": 0, "e": 0}

    def nstg():
        s = stg[st["i"] % NSTG]
        st["i"] += 1
        return s

    def ntmp():
        s = tmp[st["t"] % 4]
        st["t"] += 1
        return s

    def eng2():
        st["e"] += 1
        return "act" if st["e"] % 2 == 0 else "dve"

    OWN = [(1024, 512), (1536, 512)]
    CTX = [(0, 512), (512, 512), (1024, 512), (1536, 512)]

    def fm(slot, c0, ncols, toks, evac):
        stt, sbuf = slot
        sv = slot_view(stt, 16, 512)
        for j in range(ncols // 128):
            for (t0, nt) in toks:
                pt, pb = g.bank()
                mms = [lambda e, kc=kc, j=j, t0=t0, nt=nt: e.matmul(
                    pt[:, 0:nt], lhsT=sv[:, kc, c0 + j * 128:c0 + (j + 1) * 128], rhs=hT[:, kc, t0:t0 + nt],
                    start=(kc == 0), stop=(kc == KC - 1)) for kc in range(KC)]
                P.mm(mms, reads=(sbuf,), writes=(pb,))
                evac(pt, pb, j, t0, nt)

    def spill(dram_ap, src_ap, b_src):
        P.dma("sp", dram_ap, src_ap, reads=(b_src,))

    def ev_aq(f0):
        def ev(pt, pb, j, t0, nt):
            s, sb_ = nstg()
            sv = s.bitcast(BF16)
            P.op("act", lambda e: e.activation(out=sv[:, 0:nt], in_=pt[:, 0:nt], func=AF.Silu), reads=(pb,), writes=(sb_,))
            spill(g.AQ.ap()[f0 + j * 128:f0 + (j + 1) * 128, t0 - 1024:t0 - 1024 + nt], sv[:, 0:nt], sb_)
        return ev
    for i in range(2):
        fm(g.W.get(), 0, 512, OWN, ev_aq(i * 512))

    def ev_af(f0):
        def ev(pt, pb, j, t0, nt):
            s, sb_ = nstg()
            en = eng2()
            if en == "act":
                P.op("act", lambda e: e.copy(out=s[:, 0:nt], in_=pt[:, 0:nt]), reads=(pb,), writes=(sb_,))
            else:
                P.op("dve", lambda e: e.tensor_copy(out=s[:, 0:nt], in_=pt[:, 0:nt]), reads=(pb,), writes=(sb_,))
            spill(g.AF.ap()[f0 + j * 128:f0 + (j + 1) * 128, t0:t0 + nt], s[:, 0:nt], sb_)
        return ev
    for i in range(2):
        fm(g.W.get(), 0, 512, OWN if g.noprev else CTX, ev_af(i * 512))

    def ev_ag(f0):
        def ev(pt, pb, j, t0, nt):
            t_, tb = ntmp()
            s, sb_ = nstg()
            sv = s.bitcast(BF16)
            P.op("act", lambda e: e.activation(out=t_[:, 0:nt], in_=pt[:, 0:nt], func=AF.Silu), reads=(pb,), writes=(tb,))
            P.op("dve", lambda e: e.tensor_scalar(out=sv[:, 0:nt], in0=t_[:, 0:nt], scalar1=g.vec[:, V_NORMW:V_NORMW + 1],
                                                  scalar2=None, op0=ALU.mult), reads=(tb, g.kbuf), writes=(sb_,))
            spill(g.AG.ap()[f0 + j * 128:f0 + (j + 1) * 128, t0 - 1024:t0 - 1024 + nt], sv[:, 0:nt], sb_)
        return ev
    for i in range(2):
        fm(g.W.get(), 0, 512, OWN, ev_ag(i * 512))

    for i in range(2):
        stt, sbuf = g.W.get()
        sv = slot_view(stt, 16, 512)
        for ti in range(8 if g.noprev else 0, 16):
            pt, pb = g.bank()
            mms = [lambda e, kc=kc, ti=ti: e.matmul(pt[:, :], lhsT=hT[:, kc, ti * 128:(ti + 1) * 128], rhs=sv[:, kc, :],
                                                   start=(kc == 0), stop=(kc == KC - 1)) for kc in range(KC)]
            P.mm(mms, reads=(sbuf,), writes=(pb,))
            s, sb_ = nstg()
            sv2 = s.bitcast(BF16)
            mc = 0 if ti < 8 else 1
            en = eng2()
            if en == "act":
                P.op("act", lambda e: e.activation(out=sv2[:, 0:512], in_=pt[:, :], func=AF.Copy, scale=g.tm[:, mc:mc + 1]),
                     reads=(pb, g.kbuf), writes=(sb_,))
            else:
                P.op("dve", lambda e: e.tensor_scalar(out=sv2[:, 0:512], in0=pt[:, :], scalar1=g.tm[:, mc:mc + 1], scalar2=None,
                                                      op0=ALU.mult), reads=(pb, g.kbuf), writes=(sb_,))
            spill(g.AI.ap()[ti * 128:(ti + 1) * 128, i * 512:(i + 1) * 512], sv2[:, 0:512], sb_)

    def ev_b(dst, f0, scale, own):
        def ev(pt, pb, j, t0, nt):
            s, sb_ = nstg()
            sv = s.bitcast(BF16)
            en = eng2()
            if en == "act":
                P.op("act", lambda e: e.activation(out=sv[:, 0:nt], in_=pt[:, 0:nt], func=AF.Copy, scale=scale), reads=(pb,), writes=(sb_,))
            else:
                P.op("dve", lambda e: e.tensor_scalar(out=sv[:, 0:nt], in0=pt[:, 0:nt], scalar1=scale, scalar2=None, op0=ALU.mult),
                     reads=(pb,), writes=(sb_,))
            tt = t0 - 1024 if own else t0
            spill(dst.ap()[f0 + j * 128:f0 + (j + 1) * 128, tt:tt + nt], sv[:, 0:nt], sb_)
        return ev
    fm(g.W.get(), 0, 512, OWN, ev_b(g.BQ, 0, 0.125, True))
    fm(g.W.get(), 0, 256, OWN, ev_b(g.BQ, 512, 0.125, True))
    fm(g.W.get(), 0, 512, CTX, ev_b(g.BK, 0, 1.0, False))
    fm(g.W.get(), 0, 256, CTX, ev_b(g.BK, 512, 1.0, False))

    for gi, dil in enumerate((1, 4, 16)):
        stt, sbuf = g.W.get()
        sv = slot_view(stt, 16, 256)
        for bi in range(16):
            if gi == 0:
                start, mc = bi * 128, (0 if bi < 8 else 1)
            elif gi == 1:
                r, n = bi // 4, bi % 4
                start, mc = n * 512 + r, (0 if n < 2 else 1)
            else:
                start, mc = bi, 2
            pt, pb = g.bank()
            mms = [lambda e, kc=kc, start=start, dil=dil: e.matmul(
                pt[:, 0:256], lhsT=hT[:, kc, start:start + 127 * dil + 1:dil], rhs=sv[:, kc, :],
                start=(kc == 0), stop=(kc == KC - 1)) for kc in range(KC)]
            P.mm(mms, reads=(sbuf,), writes=(pb,))
            s, sb_ = nstg()
            sv2 = s.bitcast(BF16)
            vdst = bass.AP(sv2, 0, [[1024, 128], [256, 2], [192, 2], [1, 64]])
            mdst = bass.AP(sv2, 64, [[1024, 128], [256, 2], [64, 2], [1, 64]])
            vsrc = bass.AP(pt, 0, [[512, 128], [128, 2], [64, 2], [1, 64]])
            osrc = bass.AP(onesv, 0, [[256, 128], [128, 2], [64, 2], [1, 64]])
            P.op("dve", lambda e: e.tensor_scalar(out=vdst, in0=vsrc, scalar1=g.tm[:, mc:mc + 1], scalar2=None, op0=ALU.mult),
                 reads=(pb, g.kbuf), writes=(sb_,))
            P.op("dve", lambda e: e.tensor_scalar(out=mdst, in0=osrc, scalar1=g.tm[:, mc:mc + 1], scalar2=None, op0=ALU.mult),
                 reads=(b_ones, g.kbuf, sb_), writes=(sb_,))
            spill(g.BV.ap()[gi, bi], sv2[:, 0:512], sb_)

    CT = [(994, 30), (1024, 512), (1536, 512)]
    for i in range(3):
        slot = g.W.get()
        stt, sbuf = slot
        sv = slot_view(stt, 16, 512)
        for j in range(2):
            for (t0, nt) in CT:
                pg, pgb = g.bank()
                P.mm([lambda e, kc=kc: e.matmul(pg[:, 0:nt], lhsT=sv[:, kc, 256 + j * 128:256 + (j + 1) * 128], rhs=hT[:, kc, t0:t0 + nt],
                                               start=(kc == 0), stop=(kc == KC - 1)) for kc in range(KC)], reads=(sbuf,), writes=(pgb,))
                pa, pab = g.bank()
                P.mm([lambda e, kc=kc: e.matmul(pa[:, 0:nt], lhsT=sv[:, kc, j * 128:(j + 1) * 128], rhs=hT[:, kc, t0:t0 + nt],
                                               start=(kc == 0), stop=(kc == KC - 1)) for kc in range(KC)], reads=(sbuf,), writes=(pab,))
                t_, tb = ntmp()
                s, sb_ = nstg()
                sv2 = s.bitcast(BF16)
                P.op("act", lambda e: e.activation(out=t_[:, 0:nt], in_=pg[:, 0:nt], func=AF.Sigmoid), reads=(pgb,), writes=(tb,))
                if nt == 30:
                    P.op("dve", lambda e: e.scalar_tensor_tensor(out=sv2[:, 0:nt], in0=pa[:, 0:nt], scalar=g.tm[:, 3:4], in1=t_[:, 0:nt],
                                                                 op0=ALU.mult, op1=ALU.mult), reads=(pab, tb, g.kbuf), writes=(sb_,))
                    c0 = 2
                else:
                    P.op("dve", lambda e: e.tensor_tensor(out=sv2[:, 0:nt], in0=pa[:, 0:nt], in1=t_[:, 0:nt], op=ALU.mult),
                         reads=(pab, tb), writes=(sb_,))
                    c0 = 32 + t0 - 1024
                f0 = i * 256 + j * 128
                spill(g.U.ap()[f0:f0 + 128, c0:c0 + nt], sv2[:, 0:nt], sb_)

    for i in range(12):
        def ev(pt, pb, j, t0, nt, i=i):
            s, sb_ = nstg()
            sv = s.bitcast(BF16)
            fcol = V_BGATE + i * 4 + j
            P.op("act", lambda e: e.activation(out=sv[:, 0:nt], in_=pt[:, 0:nt], func=AF.Sigmoid, bias=g.vec[:, fcol:fcol + 1]),
                 reads=(pb, g.kbuf), writes=(sb_,))
            f0 = i * 512 + j * 128
            spill(g.GATES.ap()[f0:f0 + 128, t0 - 1024:t0 - 1024 + nt], sv[:, 0:nt], sb_)
        fm(g.W.get(), 0, 512, OWN, ev)

    for name, src in (("AQ", g.AQ), ("AF", g.AF), ("AG", g.AG), ("AI", g.AI), ("BQ", g.BQ), ("BK", g.BK), ("BV", g.BV),
                      ("U", g.U), ("GATES", g.GATES)):
        t = g.tap(name, src.shape, src.dtype)
        if t is not None:
            P.barrier()
            P.dma("sp", t.ap(), src.ap())


def phase3_hgrn(g):
    P = g.P
    g.RBC.reset(); g.RA2.reset()
    RR = [g.RBC, g.RA2]
    sb = lambda shape, dt, name: P.sb(RR, shape, dt, name)
    YT = g.YT
    cst = g.cst
    S = sb([128, 8, 128], F32, "S"); Sbf = sb([128, 8, 128], BF16, "Sbf")
    vseg = sb([64, 8, 1024], BF16, "vseg")
    QpT = sb([128, 8, 512], BF16, "QpT"); KpT = sb([128, 8, 512], BF16, "KpT")
    Q2T = sb([128, 8, 512], BF16, "Q2T"); bQ2 = Buf("Q2")
    Ktok = sb([64, 8, 8, 128], BF16, "Ktok")
    attnT = sb([64, 8, 8, 64], BF16, "attnT")
    gseg = sb([128, 8, 512], BF16, "gseg")
    dec = sb([128, 8, 8], F32, "dec")
    onesb = sb([128, 128], BF16, "onesb")
    zt = [sb([128, 512], F32, f"zt{i}") for i in range(2)]
    qh = [sb([128, 512], BF16, f"qh{i}") for i in range(2)]
    tf = sb([128, 512], F32, "tf"); tl = sb([128, 512], F32, "tl"); tk = sb([128, 512], F32, "tk")
    tA = sb([128, 512], F32, "tA"); tB = sb([128, 512], F32, "tB"); tC = sb([128, 512], F32, "tC")
    KppT = sb([128, 512], BF16, "KppT")
    RX = Region(g.RA1.base + 16384, 16384)
    T1t = [P.sb(RX, [128, 512], F32, f"t1_{i}") for i in range(6)] + [P.sb(RX, [128, 512], BF16, "KppT1")]
    rr, to = tk, tf
    osq2 = [KppT, sb([128, 512], BF16, "osqB")]
    bS, bSbf, bV, bQ, bK, bKt, bAt, bG, bDec, bOnes = (Buf(n) for n in "S Sbf V Q K Kt At G Dec Ones".split())
    bz = [Buf("z0"), Buf("z1")]; bq = [Buf("q0"), Buf("q1")]
    btf, btl, btk, btA, btB, btC, bKpp = (Buf(n) for n in "tf tl tk tA tB tC Kpp".split())
    brr, bto = btk, btf
    TS = [dict(t=(tf, tl, tk, tA, tB, tC, KppT), b=(btf, btl, btk, btA, btB, btC, bKpp)),
          dict(t=tuple(T1t), b=tuple(Buf(f"t1b{i}") for i in range(7)))]
    bosq2 = [bKpp, Buf("osqB")]
    resetm = cst[:, C_RESET:C_RESET + 512]
    caus = bass.AP(cst, C_CAUS, [[NCONST, 64], [0, 8], [1, 64]])
    P.op("dve", lambda e: e.memset(S[:], 0.0), writes=(bS,))
    P.op("dve", lambda e: e.memset(Sbf[:], 0.0), writes=(bSbf,))
    P.op("dve", lambda e: e.memset(onesb[:], 1.0), writes=(bOnes,))
    v3 = lambda t: t[:].rearrange("p (c s) -> p c s", s=64)
    hcount = 0
    for sg in range(2 if g.noprev else 0, 4):
        own = sg >= 2
        o0 = (sg - 2) * 512
        P.dma("sp", vseg[:], g.AI.ap()[sg * 512:(sg + 1) * 512, :].rearrange("(c p) n -> p c n", p=64), writes=(bV,))
        if own:
            P.dma("sp", gseg[:], g.AG.ap().rearrange("(h v) t -> v h t", v=128)[:, :, o0:o0 + 512], writes=(bG,))
        def head_stages(h, T, z, b_z, q, b_q):
            tf, tl, tk, tA, tB, tC, KppT = T["t"]
            btf, btl, btk, btA, btB, btC, bKpp = T["b"]
            lb = g.lbv[:, 0, h:h + 1]; oml = g.lbv[:, 1, h:h + 1]
            Bl = v3(tl)[:, :, 63:64]
            Bm = v3(tl)[:, :, 31:32]
            L = []
            L.append(lambda: P.op("act", lambda e: e.activation(out=tf[:], in_=z[:], func=AF.Exp, scale=-1.0), reads=(b_z,), writes=(btf,)))
            L.append(lambda: P.op("dve", lambda e: e.tensor_scalar(out=tf[:], in0=tf[:], scalar1=1.0, scalar2=None, op0=ALU.add), reads=(btf,), writes=(btf,)))
            L.append(lambda: P.op("dve", lambda e: e.reciprocal(out=tf[:], in_=tf[:]), reads=(btf,), writes=(btf,)))
            L.append(lambda: P.op("dve", lambda e: e.tensor_scalar(out=tf[:], in0=tf[:], scalar1=oml, scalar2=lb, op0=ALU.mult, op1=ALU.add),
                                  reads=(btf, g.b_lb), writes=(btf,)))
            L.append(lambda: P.op("act", lambda e: e.activation(out=tl[:], in_=tf[:], func=AF.Ln), reads=(btf,), writes=(btl,)))
            L.append(lambda: P.op("dve", lambda e: e.tensor_scalar(out=tk[:], in0=tf[:], scalar1=-1.0, scalar2=1.0, op0=ALU.mult, op1=ALU.add),
                                  reads=(btf,), writes=(btk,)))
            L.append(lambda: P.op("dve", lambda e: e.tensor_tensor_scan(out=tl[:], data0=resetm, data1=tl[:], initial=0.0, op0=ALU.mult, op1=ALU.add),
                                  reads=(btl, g.kbuf), writes=(btl,)))
            L.append(lambda: P.op("dve", lambda e: e.tensor_tensor(out=v3(tA), in0=Bl.to_broadcast([128, 8, 64]), in1=v3(tl), op=ALU.subtract),
                                  reads=(btl,), writes=(btA,)))
            L.append(lambda: P.op("act", lambda e: e.activation(out=tA[:], in_=tA[:], func=AF.Exp), reads=(btA,), writes=(btA,)))
            L.append(lambda: P.op("dve", lambda e: e.tensor_tensor(out=KppT[:], in0=tk[:], in1=tA[:], op=ALU.mult), reads=(btk, btA), writes=(bKpp,)))
            L.append(lambda: P.op("act", lambda e: e.activation(out=dec[:, :, h], in_=v3(tl)[:, :, 63], func=AF.Exp), reads=(btl,), writes=(bDec,)))
            if own:
                L.append(lambda: P.op("dve", lambda e: e.tensor_tensor(out=v3(tB), in0=v3(tl), in1=Bm.to_broadcast([128, 8, 64]), op=ALU.subtract),
                                      reads=(btl,), writes=(btB,)))
                L.append(lambda: P.op("act", lambda e: e.activation(out=tC[:], in_=tB[:], func=AF.Exp), reads=(btB,), writes=(btC,)))
                L.append(lambda: P.op("act", lambda e: e.activation(out=tB[:], in_=tB[:], func=AF.Exp, scale=-1.0), reads=(btB,), writes=(btB,)))
                L.append(lambda: P.op("dve", lambda e: e.tensor_tensor(out=QpT[:, h, :], in0=q[:], in1=tC[:], op=ALU.mult), reads=(b_q, btC), writes=(bQ,)))
                L.append(lambda: P.op("dve", lambda e: e.tensor_tensor(out=KpT[:, h, :], in0=tk[:], in1=tB[:], op=ALU.mult), reads=(btk, btB), writes=(bK,)))
                L.append(lambda: P.op("act", lambda e: e.activation(out=tA[:], in_=tl[:], func=AF.Exp), reads=(btl,), writes=(btA,)))
                L.append(lambda: P.op("dve", lambda e: e.tensor_tensor(out=Q2T[:, h, :], in0=q[:], in1=tA[:], op=ALU.mult), reads=(b_q, btA), writes=(bQ2,)))

            def pe_part():
                pt, pb = g.bank()
                ptb = pt.bitcast(BF16)
                P.mm([lambda e, c=c: e.transpose(ptb[0:64, c * 128:(c + 1) * 128], KppT[:, c * 64:(c + 1) * 64], g.identb[:])
                      for c in range(8)], reads=(bKpp, g.kbuf), writes=(pb,))
                P.op("act", lambda e: e.copy(out=Ktok[:, h, :, :].rearrange("p c k -> p (c k)"), in_=ptb[0:64, 0:1024]),
                     reads=(pb,), writes=(bKt,))
                if own:
                    pt2, pb2 = g.bank()
                    P.mm([lambda e, c=c: e.matmul(pt2[0:64, c * 64:(c + 1) * 64], lhsT=KpT[:, h, c * 64:(c + 1) * 64],
                                                  rhs=QpT[:, h, c * 64:(c + 1) * 64], start=True, stop=True) for c in range(8)],
                         reads=(bK, bQ), writes=(pb2,))
                    P.op("dve", lambda e: e.tensor_tensor(out=attnT[:, h, :, :], in0=pt2[0:64, :].rearrange("p (c t) -> p c t", t=64),
                                                          in1=caus, op=ALU.mult), reads=(pb2, g.kbuf), writes=(bAt,))
            L.append(pe_part)
            return L

        for hp in range(4):
            LL = []
            for i in range(2):
                h = 2 * hp + i
                P.dma("sp", zt[i][:], g.AF.ap()[h * 128:(h + 1) * 128, sg * 512:(sg + 1) * 512], writes=(bz[i],))
                if own:
                    P.dma("sp", qh[i][:], g.AQ.ap()[h * 128:(h + 1) * 128, o0:o0 + 512], writes=(bq[i],))
                LL.append(head_stages(h, TS[i], zt[i], bz[i], qh[i], bq[i]))
            for a, b in zip(*LL):
                a()
                b()
            phase0_hook(g)
        pending = None
        for c in range(8):
            last = (sg == 3 and c == 7)
            if not last:
                pdA, pdAb = g.bank()
                pdB, pdBb = g.bank()
                mms = []
                for h in range(8):
                    pd = pdA if h < 4 else pdB
                    hh = h % 4
                    mms.append(lambda e, h=h, pd=pd, hh=hh: e.matmul(pd[:, hh * 128:(hh + 1) * 128], lhsT=Ktok[0:64, h, c, :],
                                                                       rhs=vseg[0:64, c, h * 128:(h + 1) * 128], start=True, stop=True))
                P.mm(mms, reads=(bKt, bV), writes=(pdAb, pdBb))
            if own:
                po, pob = g.bank()
                mms = []
                for h in range(8):
                    mms.append(lambda e, h=h: e.matmul(po[:, h * 64:(h + 1) * 64], lhsT=vseg[0:64, c, h * 128:(h + 1) * 128],
                                                       rhs=attnT[0:64, h, c, :], start=True, stop=False))
                    mms.append(lambda e, h=h: e.matmul(po[:, h * 64:(h + 1) * 64], lhsT=Sbf[:, h, :],
                                                       rhs=Q2T[:, h, c * 64:(c + 1) * 64], start=False, stop=True))
                P.mm(mms, reads=(bV, bAt, bSbf, bQ2), writes=(pob,))
                oq, boq = osq2[c % 2], bosq2[c % 2]
                P.op("act", lambda e: e.activation(out=oq[:], in_=po[:, :], func=AF.Square), reads=(pob,), writes=(boq,))
            if not last:
                P.op("dve", lambda e: e.tensor_tensor(out=S[:], in0=S[:], in1=dec[:, c, :].unsqueeze(2).to_broadcast([128, 8, 128]),
                                                      op=ALU.mult), reads=(bS, bDec), writes=(bS,))
                P.op("dve", lambda e: e.tensor_tensor(out=S[:, 0:4, :], in0=S[:, 0:4, :], in1=pdA[:, :].rearrange("p (h v) -> p h v", v=128),
                                                      op=ALU.add), reads=(bS, pdAb), writes=(bS,))
                P.op("dve", lambda e: e.tensor_tensor(out=S[:, 4:8, :], in0=S[:, 4:8, :], in1=pdB[:, :].rearrange("p (h v) -> p h v", v=128),
                                                      op=ALU.add), reads=(bS, pdBb), writes=(bS,))
                P.op("act", lambda e: e.copy(out=Sbf[:], in_=S[:]), reads=(bS,), writes=(bSbf,))

            def tail(po, pob, oq, boq, c):
                ps, psb = g.bank()
                P.mm([lambda e: e.matmul(ps[:, :], lhsT=onesb[:], rhs=oq[:], start=True, stop=True)], reads=(boq, bOnes), writes=(psb,))
                P.op("act", lambda e: e.activation(out=rr[:], in_=ps[:, :], func=AF.Sqrt, scale=1.0 / 128, bias=EPS), reads=(psb,), writes=(brr,))
                P.op("dve", lambda e: e.reciprocal(out=rr[:], in_=rr[:]), reads=(brr,), writes=(brr,))
                P.op("dve", lambda e: e.tensor_tensor(out=to[:], in0=po[:, :], in1=rr[:], op=ALU.mult), reads=(pob, brr), writes=(bto,))
                tok0 = o0 + c * 64
                P.op("dve", lambda e: e.tensor_tensor(out=YT[:, 0:8, tok0:tok0 + 64], in0=to[:].rearrange("p (h t) -> p h t", t=64),
                                                      in1=gseg[:, :, c * 64:(c + 1) * 64], op=ALU.mult), reads=(bto, bG))
            if pending is not None:
                tail(*pending)
                pending = None
            if own:
                pending = (po, pob, oq, boq, c)
        if pending is not None:
            tail(*pending)
            pending = None
    t = g.tap("YA", [1024, TO], BF16)
    if t is not None:
        P.barrier()
        P.dma("sp", t.ap().rearrange("(h v) t -> v h t", v=128), YT[:, 0:8, :])


def build_bias_tables(g):
    P = g.P
    R = g.RBC
    R.reset()
    relb = P.sb(R, [32, 12], F32, "relb")
    relrep = P.sb(R, [32, 12, 128], F32, "relrep")
    oh = P.sb(R, [32, 3, 383], F32, "oh")
    fr = [P.sb(R, [128, 383], F32, f"fr{i}") for i in range(2)]
    bfr = [Buf("fr0"), Buf("fr1")]
    b_in = Buf("relb")
    P.dma("sp", relb[:], g.rel_bias.ap(), writes=(b_in,))
    P.dma("sp", oh[:], g.onehot.ap().rearrange("g b j -> b g j"), writes=(b_in,), add_write=True)
    P.op("dve", lambda e: e.tensor_copy(out=relrep[:], in_=relb[:].unsqueeze(2).to_broadcast([32, 12, 128])), reads=(b_in,), writes=(b_in,))
    for h in range(12):
        gi = h // 4
        pt, pb = g.bank()
        P.mm([lambda e: e.matmul(pt[:, 0:383], lhsT=relrep[:, h, :], rhs=oh[:, gi, :], start=True, stop=True)], reads=(b_in,), writes=(pb,))
        f, bf_ = fr[h % 2], bfr[h % 2]
        P.op("act", lambda e: e.activation(out=f[:], in_=pt[:, 0:383], func=AF.Exp), reads=(pb,), writes=(bf_,))
        P.op("dve", lambda e: e.tensor_tensor(out=f[:], in0=f[:], in1=g.cst[:, C_FVALID:C_FVALID + 383], op=ALU.mult),
             reads=(bf_, g.kbuf), writes=(bf_,))
        P.dma("sp", g.FT.ap()[h].rearrange("(p j) -> p j", j=383), f[:], reads=(bf_,))


def phase3_attn(g):
    P = g.P
    g.RBC.reset(); g.RA2.reset()
    RR = [g.RBC, g.RA2]
    sb = lambda shape, dt, name: P.sb(RR, shape, dt, name)
    YT = g.YT
    EB = sb([128, 12, 2, 128], F32, "EB")
    QT = sb([64, 4, TO], BF16, "QT"); KT = sb([64, 4, TC], BF16, "KT")
    Vg = sb([128, 16, 512], BF16, "Vg")
    UZ = sb([128, 4, TO], F32, "UZ")
    pe = [sb([128, 512], F32, f"pe{i}") for i in range(2)]
    pT = [sb([128, 512], BF16, f"pT{i}") for i in range(2)]
    rz = sb([128, 512], F32, "rz")
    bEB, bQ, bK, bV, bUZ, brz = (Buf(n) for n in "EB Q K V UZ rz".split())
    bpe = [Buf("pe0"), Buf("pe1")]; bpT = [Buf("pT0"), Buf("pT1")]
    first = True
    for h in range(12):
        for ty in range(2):
            src = bass.AP(g.FT, h * 128 * 383 + (127 if ty == 1 else 255), [[382, 128], [1, 128]])
            P.dma("sp", EB[:, h, ty, :], src, writes=(bEB,), add_write=not first)
            first = False
    cnt = 0
    pendingB = None
    for gi, dil in enumerate((1, 4, 16)):
        P.dma("sp", QT[:], g.BQ.ap()[gi * 256:(gi + 1) * 256, :].rearrange("(h d) t -> d h t", d=64), writes=(bQ,))
        P.dma("sp", KT[:], g.BK.ap()[gi * 256:(gi + 1) * 256, :].rearrange("(h d) t -> d h t", d=64), writes=(bK,))
        P.dma("sp", Vg[:], g.BV.ap()[gi].rearrange("b p n -> p b n"), writes=(bV,))
        for hh in range(4):
            h = gi * 4 + hh
            if gi < 2:
                nbatch = 4
            else:
                nbatch = 2
            for bt in range(nbatch):
                pt, pb = g.bank()
                po, pob = g.bank()
                e_, be_ = pe[cnt % 2], bpe[cnt % 2]
                p_, bp_ = pT[cnt % 2], bpT[cnt % 2]
                cnt += 1
                mm1, mm2 = [], []
                if gi == 0:
                    for j in range(2):
                        qb = bt * 2 + j
                        qs = QT[:, hh, qb * 128:(qb + 1) * 128]
                        for ty in range(2):
                            k0 = 896 + qb * 128 + ty * 128
                            reg = (j * 2 + ty) * 128
                            mm1.append(lambda e, k0=k0, reg=reg, qs=qs: e.matmul(pt[:, reg:reg + 128], lhsT=KT[:, hh, k0:k0 + 128], rhs=qs,
                                                                                 start=True, stop=True))
                            blk = 7 + qb + ty
                            mm2.append(lambda e, blk=blk, reg=reg, j=j, ty=ty, po=po, p_=p_, hh=hh: e.matmul(
                                po[:, j * 128:(j + 1) * 128], lhsT=Vg[:, blk, hh * 128:(hh + 1) * 128], rhs=p_[:, reg:reg + 128],
                                start=(ty == 0), stop=(ty == 1)))
                    eb_in = bass.AP(EB, h * 256, [[12 * 256, 128], [0, 2], [1, 256]])
                    uz_view = UZ[:, hh, bt * 256:(bt + 1) * 256]
                    ncol = 512
                elif gi == 1:
                    r = bt
                    for j in range(2):
                        n = 2 + j
                        qs = QT[:, hh, j * 512 + r:j * 512 + r + 509:4]
                        for ty in range(2):
                            m = n - 1 + ty
                            k0 = m * 512 + r
                            reg = (j * 2 + ty) * 128
                            mm1.append(lambda e, k0=k0, reg=reg, qs=qs: e.matmul(pt[:, reg:reg + 128], lhsT=KT[:, hh, k0:k0 + 509:4], rhs=qs,
                                                                                 start=True, stop=True))
                            blk = r * 4 + m
                            mm2.append(lambda e, blk=blk, reg=reg, j=j, ty=ty, po=po, p_=p_, hh=hh: e.matmul(
                                po[:, j * 128:(j + 1) * 128], lhsT=Vg[:, blk, hh * 128:(hh + 1) * 128], rhs=p_[:, reg:reg + 128],
                                start=(ty == 0), stop=(ty == 1)))
                    eb_in = bass.AP(EB, h * 256, [[12 * 256, 128], [0, 2], [1, 256]])
                    uz_view = UZ[:, hh, r:TO:4]
                    ncol = 512
                else:
                    for rr_ in range(8):
                        r = bt * 8 + rr_
                        reg = rr_ * 64
                        mm1.append(lambda e, r=r, reg=reg: e.matmul(pt[:, reg:reg + 64], lhsT=KT[:, hh, r:TC:16], rhs=QT[:, hh, r:TO:16],
                                                                    start=True, stop=True))
                        mm2.append(lambda e, r=r, reg=reg, po=po, p_=p_, hh=hh: e.matmul(po[:, reg:reg + 64], lhsT=Vg[:, r, hh * 128:(hh + 1) * 128],
                                                                    rhs=p_[:, reg:reg + 64], start=True, stop=True))
                    eb_in = bass.AP(EB, h * 256 + 128 + 64, [[12 * 256, 128], [0, 8], [1, 64]])
                    uz_view = bass.AP(UZ, hh * TO + bt * 8, [[4 * TO, 128], [1, 8], [16, 64]])
                    ncol = 512
                P.mm(mm1, reads=(bQ, bK), writes=(pb,))
                P.op("act", lambda e: e.activation(out=e_[:, 0:ncol], in_=pt[:, 0:ncol], func=AF.Exp), reads=(pb,), writes=(be_,))
                if gi < 2:
                    e_v = e_[:, 0:512].rearrange("p (j c) -> p j c", j=2)
                    p_v = p_[:, 0:512].rearrange("p (j c) -> p j c", j=2)
                else:
                    e_v = e_[:, 0:512].rearrange("p (j c) -> p j c", j=8)
                    p_v = p_[:, 0:512].rearrange("p (j c) -> p j c", j=8)
                P.op("dve", lambda e: e.tensor_tensor(out=p_v, in0=e_v, in1=eb_in, op=ALU.mult), reads=(be_, bEB), writes=(bp_,))
                nout = 256 if gi < 2 else 512

                def make_stB(mm2, po, pob, bp_, uz_view, gi, nout):
                    def stB():
                        P.mm(mm2, reads=(bV, bp_), writes=(pob,))
                        if gi == 0:
                            P.op("act", lambda e: e.copy(out=uz_view, in_=po[:, 0:nout]), reads=(pob,), writes=(bUZ,))
                        elif gi == 1:
                            P.op("dve", lambda e: e.tensor_tensor(out=uz_view, in0=po[:, 0:nout], in1=uz_view, op=ALU.add),
                                 reads=(pob, bUZ), writes=(bUZ,))
                        else:
                            P.op("dve", lambda e: e.tensor_tensor(out=uz_view, in0=po[:, 0:nout].rearrange("p (r l) -> p r l", r=8),
                                                                  in1=uz_view, op=ALU.add), reads=(pob, bUZ), writes=(bUZ,))
                    return stB
                if pendingB is not None:
                    pendingB()
                pendingB = make_stB(mm2, po, pob, bp_, uz_view, gi, nout)
        if pendingB is not None:
            pendingB()
            pendingB = None
    for s in range(4):
        for tb in range(2):
            pz, pzb = g.bank()
            P.mm([lambda e: e.matmul(pz[:, :], lhsT=g.cst[:, C_SWAP:C_SWAP + 128], rhs=UZ[:, s, tb * 512:(tb + 1) * 512], start=True, stop=True)],
                 reads=(bUZ, g.kbuf), writes=(pzb,))
            lo = 0 if s % 2 == 0 else 64
            P.op("dve", lambda e: e.reciprocal(out=rz[lo:lo + 64, :], in_=pz[lo:lo + 64, :]), reads=(pzb,), writes=(brz,))
            P.op("dve", lambda e: e.tensor_tensor(out=YT[lo:lo + 64, 8 + s // 2, tb * 512:(tb + 1) * 512], in0=UZ[lo:lo + 64, s, tb * 512:(tb + 1) * 512],
                                                  in1=rz[lo:lo + 64, :], op=ALU.mult), reads=(brz, bUZ))
    t = g.tap("YB", [256, TO], BF16)
    if t is not None:
        P.barrier()
        P.dma("sp", t.ap().rearrange("(c p) t -> p c t", p=128), YT[:, 8:10, :])


def phase3_conv(g):
    P = g.P
    g.RBC.reset(); g.RA2.reset()
    RR = [g.RBC, g.RA2]
    sb = lambda shape, dt, name: P.sb(RR, shape, dt, name)
    YT = g.YT
    UT = sb([128, 6, TO + 32], BF16, "UT")
    yT = sb([128, 6, TO], F32, "yT")
    ybf = sb([128, 6, 512], BF16, "ybf"); ysq = sb([128, 6, 512], BF16, "ysq")
    m_ = sb([128, 512], F32, "m"); v_ = sb([128, 512], F32, "v"); rs = sb([128, 512], F32, "rs")
    tt = [sb([128, 512], F32, f"tt{i}") for i in range(2)]
    onesb = sb([128, 128], BF16, "onesb")
    dg = [sb([128, 128], BF16, f"dg{i}") for i in range(4)]
    bdg = [Buf(f"dg{i}") for i in range(4)]
    bU, by, bybf, bysq, bm, bv, brs, bOnes = (Buf(n) for n in "U y ybf ysq m v rs ones".split())
    btt = [Buf("tt0"), Buf("tt1")]
    P.op("dve", lambda e: e.memset(onesb[:], 1.0), writes=(bOnes,))
    P.dma("sp", UT[:, :, 2:TO + 32], g.U.ap().rearrange("(t p) n -> p t n", p=128)[:, :, 2:TO + 32], writes=(bU,))
    k = 0
    for ct in range(6):
        pA, pAb = g.bank()
        pB, pBb = g.bank()
        for j in range(31):
            d, bd = dg[k % 4], bdg[k % 4]
            k += 1
            col = V_CONVW + ct * 31 + j
            P.op("dve", lambda e: e.tensor_scalar(out=d[:], in0=g.identb[:], scalar1=g.vec[:, col:col + 1], scalar2=None, op0=ALU.mult),
                 reads=(g.kbuf,), writes=(bd,))
            P.mm([lambda e: e.matmul(pA[:, :], lhsT=d[:], rhs=UT[:, ct, 2 + j:2 + j + 512], start=(j == 0), stop=(j == 30)),
                  lambda e: e.matmul(pB[:, :], lhsT=d[:], rhs=UT[:, ct, 512 + 2 + j:512 + 2 + j + 512], start=(j == 0), stop=(j == 30))],
                 reads=(bd, bU), writes=(pAb, pBb))
        cb = g.vec[:, V_CONVB + ct:V_CONVB + ct + 1]
        P.op("act", lambda e: e.activation(out=yT[:, ct, 0:512], in_=pA[:, :], func=AF.Identity, bias=cb), reads=(pAb, g.kbuf), writes=(by,))
        P.op("act", lambda e: e.activation(out=yT[:, ct, 512:1024], in_=pB[:, :], func=AF.Identity, bias=cb), reads=(pBb, g.kbuf, by), writes=(by,))
    for tb in range(2):
        ts = slice(tb * 512, (tb + 1) * 512)
        P.op("act", lambda e: e.copy(out=ybf[:], in_=yT[:, :, ts]), reads=(by,), writes=(bybf,))
        P.op("act", lambda e: e.activation(out=ysq[:], in_=yT[:, :, ts], func=AF.Square), reads=(by,), writes=(bysq,))
        p1, p1b = g.bank()
        p2, p2b = g.bank()
        P.mm([lambda e, ct=ct: e.matmul(p1[:, :], lhsT=onesb[:], rhs=ybf[:, ct, :], start=(ct == 0), stop=(ct == 5)) for ct in range(6)],
             reads=(bybf, bOnes), writes=(p1b,))
        P.mm([lambda e, ct=ct: e.matmul(p2[:, :], lhsT=onesb[:], rhs=ysq[:, ct, :], start=(ct == 0), stop=(ct == 5)) for ct in range(6)],
             reads=(bysq, bOnes), writes=(p2b,))
        P.op("dve", lambda e: e.tensor_scalar(out=m_[:], in0=p1[:, :], scalar1=1.0 / 768, scalar2=None, op0=ALU.mult), reads=(p1b,), writes=(bm,))
        P.op("dve", lambda e: e.tensor_tensor(out=v_[:], in0=m_[:], in1=m_[:], op=ALU.mult), reads=(bm,), writes=(bv,))
        P.op("dve", lambda e: e.scalar_tensor_tensor(out=v_[:], in0=p2[:, :], scalar=1.0 / 768, in1=v_[:], op0=ALU.mult, op1=ALU.subtract),
             reads=(p2b, bv), writes=(bv,))
        P.op("act", lambda e: e.activation(out=rs[:], in_=v_[:], func=AF.Sqrt, bias=EPS), reads=(bv,), writes=(brs,))
        P.op("dve", lambda e: e.reciprocal(out=rs[:], in_=rs[:]), reads=(brs,), writes=(brs,))
        for ct in range(6):
            t_, bt_ = tt[ct % 2], btt[ct % 2]
            P.op("dve", lambda e: e.tensor_tensor(out=t_[:], in0=yT[:, ct, ts], in1=m_[:], op=ALU.subtract), reads=(by, bm), writes=(bt_,))
            P.op("dve", lambda e: e.tensor_tensor(out=t_[:], in0=t_[:], in1=rs[:], op=ALU.mult), reads=(bt_, brs), writes=(bt_,))
            P.op("act", lambda e: e.activation(out=YT[:, 10 + ct, ts], in_=t_[:], func=AF.Silu, scale=g.vec[:, V_LNG + ct:V_LNG + ct + 1],
                                               bias=g.vec[:, V_LNB + ct:V_LNB + ct + 1]), reads=(bt_, g.kbuf))
    t = g.tap("YC", [768, TO], BF16)
    if t is not None:
        P.barrier()
        P.dma("sp", t.ap().rearrange("(c p) t -> p c t", p=128), YT[:, 10:16, :])


def plan_weights_p45(g):
    W = g.W
    wv = lambda w, c0, n: w.ap().rearrange("(kc p) n -> p kc n", p=128)[:, :, c0:c0 + n]
    for cb in range(4):
        W.add([(lambda t: slot_view(t, 8, 512, kc0=0), wv(g.w_a_out, cb * 512, 512)),
               (lambda t: slot_view(t, 2, 512, kc0=8), wv(g.w_b_out, cb * 512, 512)),
               (lambda t: slot_view(t, 6, 512, kc0=10), wv(g.w_c_out, cb * 512, 512))])
    for cb in range(4):
        W.add([(lambda t: slot_view(t, 16, 512), wv(g.w_o, cb * 512, 512))])
    for gq in range(8):
        for i in range(2):
            W.add([(lambda t: slot_view(t, 16, 512), wv(g.w_up, gq * 1024 + i * 512, 512))])
        for half in range(2):
            src = g.w_down.ap()[gq * 1024:(gq + 1) * 1024, half * 1024:(half + 1) * 1024].rearrange("(kc p) n -> p kc n", p=128)
            W.add([(lambda t: slot_view(t, 8, 1024, width=1024), src)])


def phase4a(g):
    P = g.P
    g.RBC.reset(); g.RA2.reset()
    YT = g.YT
    g.mT = P.sb(g.RA2, [128, 16, TO], BF16, "mT")
    R = g.RBC
    gt = [P.sb(R, [128, 3, TO], BF16, f"gt{i}") for i in range(2)]
    bgt = [Buf("gt0"), Buf("gt1")]
    t1 = [P.sb(R, [128, 512], F32, f"t1{i}") for i in range(2)]
    t2 = [P.sb(R, [128, 512], F32, f"t2{i}") for i in range(2)]
    bt1 = [Buf("t10"), Buf("t11")]; bt2 = [Buf("t20"), Buf("t21")]
    gv = g.GATES.ap().rearrange("(i f) t -> f i t", i=3)
    k = 0
    for cb in range(4):
        stt, sbuf = g.W.get()
        sv = slot_view(stt, 16, 512)
        for j in range(4):
            fo = cb * 4 + j
            gg, bg = gt[fo % 2], bgt[fo % 2]
            P.dma("sp", gg[:], gv[fo * 128:(fo + 1) * 128], writes=(bg,))
            for tb in range(2):
                ts = slice(tb * 512, (tb + 1) * 512)
                banks = []
                for (k0, k1) in ((0, 8), (8, 10), (10, 16)):
                    pt, pb = g.bank()
                    P.mm([lambda e, kc=kc, pt=pt, k0=k0, k1=k1: e.matmul(pt[:, :], lhsT=sv[:, kc, j * 128:(j + 1) * 128], rhs=YT[:, kc, ts],
                                                                         start=(kc == k0), stop=(kc == k1 - 1)) for kc in range(k0, k1)],
                         reads=(sbuf,), writes=(pb,))
                    banks.append((pt, pb))
                a, ba = t1[k % 2], bt1[k % 2]
                b, bb = t2[k % 2], bt2[k % 2]
                k += 1
                (pA, pAb), (pB, pBb), (pC, pCb) = banks
                P.op("dve", lambda e: e.tensor_tensor(out=a[:], in0=pA[:, :], in1=gg[:, 0, ts], op=ALU.mult), reads=(pAb, bg), writes=(ba,))
                P.op("dve", lambda e: e.tensor_tensor(out=b[:], in0=pB[:, :], in1=gg[:, 1, ts], op=ALU.mult), reads=(pBb, bg), writes=(bb,))
                P.op("dve", lambda e: e.tensor_tensor(out=a[:], in0=a[:], in1=b[:], op=ALU.add), reads=(ba, bb), writes=(ba,))
                P.op("dve", lambda e: e.tensor_tensor(out=b[:], in0=pC[:, :], in1=gg[:, 2, ts], op=ALU.mult), reads=(pCb, bg, bb), writes=(bb,))
                P.op("dve", lambda e: e.tensor_tensor(out=g.mT[:, fo, ts], in0=a[:], in1=b[:], op=ALU.add), reads=(ba, bb))
    t = g.tap("MT", [D, TO], BF16)
    if t is not None:
        P.barrier()
        P.dma("sp", t.ap().rearrange("(c p) t -> p c t", p=128), g.mT[:])


def phase4b(g):
    P = g.P
    j0 = g.W.taken
    slots = [g.W.get(hold_from=j0) for _ in range(4)]
    g.RA1.reset(); g.RB.reset(); g.RC.reset()
    g.h2T = P.sb(g.RB, [128, 16, TO], BF16, "h2T")
    xt = [P.sb(g.RA1, [128, D], F32, f"xt{i}") for i in range(2)]
    bxt = [Buf("xt0"), Buf("xt1")]
    Grow = P.sb(g.RA1, [128, D], F32, "Grow")
    tq = P.sb(g.RA1, [128, D], F32, "tq")
    x1t = [P.sb(g.RC, [128, 1, D], F32, f"x1t{i}") for i in range(2)]
    bx1 = [Buf("x1t0"), Buf("x1t1")]
    junk = P.sb(g.RC, [128, D], BF16, "junk")
    st = P.sb(g.RC, [128, 8], F32, "st"); st2 = P.sb(g.RC, [128, 8], F32, "st2")
    bst, bst2, bG, btq = Buf("st"), Buf("st2"), Buf("Grow"), Buf("tq")
    postbc = P.sb(g.RC, [128, D], F32, "postbc")
    P.dma("sp", Grow[:], bass.AP(g.GROWS, 0, [[0, 128], [1, D]]), writes=(bG,))
    P.dma("sp", postbc[:], bass.AP(g.rows2, g.l * NROW + R_POST, [[0, 128], [1, D]]), writes=(bG,), add_write=True)
    P.op("dve", lambda e: e.tensor_tensor(out=Grow[:], in0=Grow[:], in1=postbc[:], op=ALU.mult), reads=(bG,), writes=(bG,))
    xv = g.xctx.ap().rearrange("(t p) d -> p t d", p=128)
    def stage_a(ti):
        x_, bx_ = xt[ti % 2], bxt[ti % 2]
        x1, b1 = x1t[ti % 2], bx1[ti % 2]
        P.dma("sp", x_[:], xv[:, 8 + ti, :], writes=(bx_,))
        bks = []
        for cb in range(4):
            pt, pb = g.bank()
            stt, sbuf = slots[cb]
            sv = slot_view(stt, 16, 512)
            P.mm([lambda e, kc=kc, pt=pt, sv=sv: e.matmul(pt[:, :], lhsT=g.mT[:, kc, ti * 128:(ti + 1) * 128], rhs=sv[:, kc, :],
                                                          start=(kc == 0), stop=(kc == KC - 1)) for kc in range(KC)], reads=(sbuf,), writes=(pb,))
            bks.append((pt, pb))
        for cb, (pt, pb) in enumerate(bks):
            P.op("act", lambda e, pt=pt, cb=cb: e.activation(out=junk[:, 0:512], in_=pt[:, :], func=AF.Square, accum_out=st2[:, cb:cb + 1]),
                 reads=(pb,), writes=(bst2,))
            P.op("dve", lambda e, pt=pt, cb=cb: e.tensor_tensor(out=tq[:, cb * 512:(cb + 1) * 512], in0=pt[:, :], in1=Grow[:, cb * 512:(cb + 1) * 512],
                                                               op=ALU.mult), reads=(pb, bG), writes=(btq,))
        P.op("dve", lambda e: e.reduce_sum(out=st2[:, 4:5], in_=st2[:, 0:4], axis=AX.X), reads=(bst2,), writes=(bst2,))
        P.op("dve", lambda e: e.tensor_scalar(out=st2[:, 5:6], in0=st2[:, 4:5], scalar1=1.0 / D, scalar2=EPS, op0=ALU.mult, op1=ALU.add),
             reads=(bst2,), writes=(bst2,))
        P.op("act", lambda e: e.activation(out=st2[:, 6:7], in_=st2[:, 5:6], func=AF.Sqrt), reads=(bst2,), writes=(bst2,))
        P.op("dve", lambda e: e.reciprocal(out=st2[:, 7:8], in_=st2[:, 6:7]), reads=(bst2,), writes=(bst2,))
        P.op("dve", lambda e: e.scalar_tensor_tensor(out=x1[:, 0, :], in0=tq[:], scalar=st2[:, 7:8], in1=x_[:], op0=ALU.mult, op1=ALU.add),
             reads=(btq, bst2, bx_), writes=(b1,))
        P.dma("sp", g.X1.ap()[ti * 128:(ti + 1) * 128, :], x1[:, 0, :], reads=(b1,))
        norm_stats(g, x1, b1, 1, junk, st, bst)

    def stage_b(ti):
        norm_xpose(g, x1t[ti % 2], bx1[ti % 2], 1, g.AB[:, 2, :], g.AB[:, 3, :], lambda fc, ti=ti: g.h2T[:, fc, ti * 128:(ti + 1) * 128])
    stage_a(0)
    for ti in range(8):
        if ti + 1 < 8:
            stage_a(ti + 1)
        stage_b(ti)
    t = g.tap("X1", [TO, D], F32)
    if t is not None:
        P.barrier()
        P.dma("sp", t.ap(), g.X1.ap())
    t = g.tap("H2T", [D, TO], BF16)
    if t is not None:
        P.barrier()
        P.dma("sp", t.ap().rearrange("(c p) t -> p c t", p=128), g.h2T[:])


def phase5(g):
    P = g.P
    g.RA.reset(); g.RC.reset()
    Y = P.sb(g.RA, [128, 8, D], F32, "Y")
    aT = P.sb(g.RC, [128, 8, TO], BF16, "aT")
    rst = [P.sb(g.RC, [128, 512], F32, f"rst{i}") for i in range(2)]
    brst = [Buf("rst0"), Buf("rst1")]
    baT = [Buf(f"aT{i}") for i in range(8)]
    bY = [[Buf(f"Y{ti}_{cb}") for cb in range(4)] for ti in range(8)]
    h2T = g.h2T
    k = 0
    for gq in range(8):
        for i in range(2):
            stt, sbuf = g.W.get()
            sv = slot_view(stt, 16, 512)
            for j in range(4):
                ffc = i * 4 + j
                for tb in range(2):
                    ts = slice(tb * 512, (tb + 1) * 512)
                    pt, pb = g.bank()
                    P.mm([lambda e, kc=kc, pt=pt: e.matmul(pt[:, :], lhsT=sv[:, kc, j * 128:(j + 1) * 128], rhs=h2T[:, kc, ts],
                                                           start=(kc == 0), stop=(kc == KC - 1)) for kc in range(KC)], reads=(sbuf,), writes=(pb,))
                    r_, br_ = rst[k % 2], brst[k % 2]
                    k += 1
                    P.op("act", lambda e, pt=pt, r_=r_: e.activation(out=r_[:], in_=pt[:, :], func=AF.Relu), reads=(pb,), writes=(br_,))
                    P.op("dve", lambda e, r_=r_, ffc=ffc, ts=ts: e.tensor_tensor(out=aT[:, ffc, ts], in0=r_[:], in1=r_[:], op=ALU.mult),
                         reads=(br_,), writes=(baT[ffc],))
        for half in range(2):
            stt, sbuf = g.W.get()
            sv8 = slot_view(stt, 8, 1024, width=1024)
            for cbh in range(2):
                cb = half * 2 + cbh
                for ti in range(8):
                    pt, pb = g.bank()
                    P.mm([lambda e, kc=kc, pt=pt: e.matmul(pt[:, :], lhsT=aT[:, kc, ti * 128:(ti + 1) * 128], rhs=sv8[:, kc, cbh * 512:(cbh + 1) * 512],
                                                           start=(kc == 0), stop=(kc == 7)) for kc in range(8)],
                         reads=(sbuf,) + tuple(baT), writes=(pb,))
                    yv = Y[:, ti, cb * 512:(cb + 1) * 512]
                    if gq == 0:
                        P.op("act", lambda e, pt=pt, yv=yv: e.copy(out=yv, in_=pt[:, :]), reads=(pb,), writes=(bY[ti][cb],))
                    else:
                        P.op("dve", lambda e, pt=pt, yv=yv: e.tensor_tensor(out=yv, in0=pt[:, :], in1=yv, op=ALU.add),
                             reads=(pb, bY[ti][cb]), writes=(bY[ti][cb],))
    P.barrier()
    g.RB.reset(); g.RC.reset()
    Grow = P.sb(g.RB, [128, D], F32, "Grow2")
    x1t = [P.sb(g.RB, [128, D], F32, f"x1f{i}") for i in range(2)]
    bx1 = [Buf("x1f0"), Buf("x1f1")]
    ot = [P.sb(g.RC, [128, D], F32, f"ot{i}") for i in range(2)]
    bot = [Buf("ot0"), Buf("ot1")]
    junk = P.sb(g.RC, [128, D], BF16, "junkf")
    tq = P.sb(g.RC, [128, D], F32, "tqf")
    postbc = P.sb(g.RC, [128, D], F32, "postbc2")
    st = P.sb(g.RB, [128, 8], F32, "stf")
    bG, bst, btq = Buf("G2"), Buf("stf"), Buf("tqf")
    P.dma("sp", Grow[:], bass.AP(g.GROWS, D, [[0, 128], [1, D]]), writes=(bG,))
    P.dma("sp", postbc[:], bass.AP(g.rows2, g.l * NROW + R_MLPPOST, [[0, 128], [1, D]]), writes=(bG,), add_write=True)
    P.op("dve", lambda e: e.tensor_tensor(out=Grow[:], in0=Grow[:], in1=postbc[:], op=ALU.mult), reads=(bG,), writes=(bG,))
    for ti in range(8):
        x1, b1 = x1t[ti % 2], bx1[ti % 2]
        o_, bo_ = ot[ti % 2], bot[ti % 2]
        P.dma("sp", x1[:], g.X1.ap()[ti * 128:(ti + 1) * 128, :], writes=(b1,))
        P.op("act", lambda e: e.activation(out=junk[:], in_=Y[:, ti, :], func=AF.Square, accum_out=st[:, 0:1]), writes=(bst,))
        P.op("dve", lambda e: e.tensor_scalar(out=st[:, 1:2], in0=st[:, 0:1], scalar1=1.0 / D, scalar2=EPS, op0=ALU.mult, op1=ALU.add),
             reads=(bst,), writes=(bst,))
        P.op("act", lambda e: e.activation(out=st[:, 2:3], in_=st[:, 1:2], func=AF.Sqrt), reads=(bst,), writes=(bst,))
        P.op("dve", lambda e: e.reciprocal(out=st[:, 3:4], in_=st[:, 2:3]), reads=(bst,), writes=(bst,))
        P.op("dve", lambda e: e.tensor_tensor(out=tq[:], in0=Y[:, ti, :], in1=Grow[:], op=ALU.mult), reads=(bG,), writes=(btq,))
        P.op("dve", lambda e: e.scalar_tensor_tensor(out=o_[:], in0=tq[:], scalar=st[:, 3:4], in1=x1[:], op0=ALU.mult, op1=ALU.add),
             reads=(btq, bst, b1), writes=(bo_,))
        P.dma("sp", g.xout.ap()[ti * 128:(ti + 1) * 128, :], o_[:], reads=(bo_,))


def make_consts():
    c = np.zeros((128, NCONST), np.float32)
    c[:, C_IDENT:C_IDENT + 128] = np.eye(128, dtype=np.float32)
    sw = np.zeros((128, 128), np.float32)
    for m in range(128):
        sw[(m + 64) % 128, m] = 1.0
    c[:, C_SWAP:C_SWAP + 128] = sw
    fv = np.zeros(383, np.float32); fv[127:127 + 129] = 1.0
    c[:, C_FVALID:C_FVALID + 383] = fv[None, :]
    s = np.arange(64)[:, None]; t = np.arange(64)[None, :]
    c[:64, C_CAUS:C_CAUS + 64] = (s <= t).astype(np.float32)
    rm = np.ones(512, np.float32); rm[0::64] = 0.0
    c[:, C_RESET:C_RESET + 512] = rm[None, :]
    c[:, C_ONES:C_ONES + 128] = 1.0
    return c

def t5_bucket_np(dist):
    import math
    exact = 16
    d = np.maximum(dist, 1).astype(np.float32)
    large = exact + (np.log(d / np.float32(exact)) / np.float32(math.log(2048 / exact)) * np.float32(32 - exact)).astype(np.int32)
    return np.where(dist < exact, dist, np.clip(large, exact, 31))

def make_onehot():
    oh = np.zeros((3, 32, 383), np.float32)
    for gi, dil in enumerate((1, 4, 16)):
        for s in range(129):
            b = int(t5_bucket_np(np.array([s * dil]))[0])
            oh[gi, b, 127 + s] = 1.0
    return oh

def col(v):
    return np.ascontiguousarray(np.asarray(v, np.float32).reshape(-1, 128).T)

def layer_small(inputs, l):
    vec = np.zeros((128, NVEC), np.float32)
    vec[:, V_PRE:V_PRE + 16] = col(inputs["mix_norm_pre"][l])
    vec[:, V_MLPPRE:V_MLPPRE + 16] = col(inputs["mlp_norm_pre"][l])
    vec[:, V_BGATE:V_BGATE + 48] = col(inputs["b_gate"][l])
    vec[:, V_NORMW:V_NORMW + 1] = col(inputs["hgrn_norm_w"][l])
    vec[:, V_CONVB:V_CONVB + 6] = col(inputs["conv_b"][l])
    vec[:, V_LNG:V_LNG + 6] = col(inputs["conv_ln_g"][l])
    vec[:, V_LNB:V_LNB + 6] = col(inputs["conv_ln_b"][l])
    cw = np.asarray(inputs["conv_w"][l], np.float32)
    vec[:, V_CONVW:V_CONVW + 186] = cw.reshape(31, 6, 128).transpose(2, 1, 0).reshape(128, 186)
    rows = np.zeros((1, NROW), np.float32)
    rows[0, R_BADA:R_BADA + 12288] = inputs["b_ada"][l]
    rows[0, R_POST:R_POST + 2048] = inputs["mix_norm_post"][l]
    rows[0, R_MLPPOST:R_MLPPOST + 2048] = inputs["mlp_norm_post"][l]
    return vec, rows

def core_inputs(inputs, l, b, half, xfull):
    xb = np.asarray(xfull[b], np.float32)
    if half == 0:
        xctx = np.concatenate([np.zeros((1024, D), np.float32), xb[:1024]], axis=0)
    else:
        xctx = xb
    flag = float(half)
    tm = np.zeros((128, 4), np.float32)
    tm[:, 0] = flag; tm[:, 1] = 1.0; tm[:64, 2] = flag; tm[64:, 2] = 1.0; tm[:, 3] = flag
    vec, rows = layer_small(inputs, l)
    lbl = np.asarray(inputs["hgrn_lb_logits"], np.float32).reshape(2, 8, 128).transpose(2, 0, 1)
    return {
        "xctx": np.ascontiguousarray(xctx), "c_b": col(inputs["c"][b]), "tmask": tm,
        "consts": make_consts(), "vecs": vec, "rows": rows, "lbl": np.ascontiguousarray(lbl),
        "rel_bias": np.asarray(inputs["rel_bias"], np.float32), "onehot": make_onehot(),
        "w_ada": np.asarray(inputs["w_ada"][l]), "w_in": np.asarray(inputs["w_in"][l]),
        "w_gate": np.asarray(inputs["w_gate"][l]), "w_a_out": np.asarray(inputs["w_a_out"][l]),
        "w_b_out": np.asarray(inputs["w_b_out"][l]), "w_c_out": np.asarray(inputs["w_c_out"][l]),
        "w_o": np.asarray(inputs["w_o"][l]), "w_up": np.asarray(inputs["w_up"][l]),
        "w_down": np.asarray(inputs["w_down"][l]),
        "layer_is1": np.full((128, 1), float(l), np.float32),
    }


def fused_core_inputs(inputs, b, half, shared):
    xb = np.asarray(inputs["x"][b], np.float32)
    z = np.zeros((1024, D), np.float32)
    if half == 1:
        x3 = np.concatenate([z, xb[:1024], xb[1024:]], axis=0)
    else:
        x3 = np.concatenate([z, z, xb[:1024]], axis=0)
    tm = np.zeros((2, 128, 4), np.float32)
    for i, flag in enumerate((0.0, float(half))):
        tm[i, :, 0] = flag; tm[i, :, 1] = 1.0; tm[i, :64, 2] = flag; tm[i, 64:, 2] = 1.0; tm[i, :, 3] = flag
    m = dict(shared)
    m["x3"] = np.ascontiguousarray(x3)
    m["c_b"] = col(inputs["c"][b])
    m["tmask"] = tm
    return m


def shared_inputs(inputs):
    vl = [layer_small(inputs, l) for l in range(2)]
    lbl = np.asarray(inputs["hgrn_lb_logits"], np.float32).reshape(2, 8, 128).transpose(2, 0, 1)
    sh = {
        "consts": make_consts(), "vecs": np.stack([v[0] for v in vl]), "rows": np.stack([v[1] for v in vl]),
        "lbl": np.ascontiguousarray(lbl), "rel_bias": np.asarray(inputs["rel_bias"], np.float32), "onehot": make_onehot(),
    }
    for k in ("w_ada", "w_in", "w_gate", "w_a_out", "w_b_out", "w_c_out", "w_o", "w_up", "w_down"):
        sh[k] = np.asarray(inputs[k], np.float32)
    return sh


_NC_CACHE = {}


def kernel(**inputs):
    inputs = {k: np.asarray(v) for k, v in inputs.items()}
    if "nc" not in _NC_CACHE:
        _NC_CACHE["nc"] = build_fused()[0]
    nc = _NC_CACHE["nc"]
    sh = shared_inputs(inputs)
    maps = [fused_core_inputs(inputs, b, half, sh) for b in range(4) for half in range(2)]
    res = run_bass_kernel_spmd(nc, maps, core_ids=list(range(8)))
    out = np.empty((4, 2048, D), np.float32)
    for b in range(4):
        for half in range(2):
            out[b, half * 1024:(half + 1) * 1024] = res.results[b * 2 + half]["xout"]
    return out
```
